# Optimizing a Trainium2 kernel written in Bass

```python
import math
import jax, jax.numpy as jnp
from jax import lax
import numpy as np

D_MODEL = 1024
BATCH = 16
SEQ = 4096
DEPTH = 4

N_MIXERS = 4
N_A = (DEPTH + 3) // 4
N_B = (DEPTH + 2) // 4
N_C = (DEPTH + 1) // 4
N_D = DEPTH // 4
A_HEADS = 4
A_DV = D_MODEL // A_HEADS
A_DK = A_DV // 2
A_CHUNK = 64
A_IN = 2 * A_HEADS * A_DK + 2 * A_HEADS * A_DV + 2 * A_HEADS
B_HEAD_DIM = 64
B_Q_HEADS = D_MODEL // B_HEAD_DIM
B_KV_HEADS = B_Q_HEADS // 8
B_WINDOW = 128
B_BLOCK = 128
B_IN = (B_Q_HEADS + 2 * B_KV_HEADS) * B_HEAD_DIM
C_EXPAND = 128
C_HEADS = D_MODEL // C_EXPAND
C_DK = C_EXPAND
C_DV = D_MODEL // C_HEADS
C_CHUNK = 16
C_IN = 2 * C_HEADS * C_DK + 2 * C_HEADS * C_DV
D_HEAD_DIM = 64
D_HEADS = D_MODEL // (2 * D_HEAD_DIM)
D_QBLOCK = 128
D_IN = 3 * D_HEADS * 2 * D_HEAD_DIM
ROPE_THETA = 500000.0
ROT_FRAC = 4
FFN_HIDDEN = -(-8 * D_MODEL // (3 * 256)) * 256
DEEPNORM_ALPHA = (2 * DEPTH) ** 0.25
DEEPNORM_BETA = (8 * DEPTH) ** -0.25
F32 = jnp.float32

kernel_name = "hybrid_interleaved_mlstm_swa_hgrn2_diffattn"


def layer_norm(x, g, b, eps=1e-5):
    xf = x.astype(F32)
    mu = xf.mean(-1, keepdims=True)
    var = jnp.square(xf - mu).mean(-1, keepdims=True)
    return ((xf - mu) * lax.rsqrt(var + eps) * g.astype(F32) + b.astype(F32)).astype(x.dtype)


def head_rms_norm(h, w, eps=1e-6):
    hf = h.astype(F32)
    return hf * lax.rsqrt(jnp.mean(hf * hf, -1, keepdims=True) + eps) * w.astype(F32)


def partial_rope(x, positions):
    hd = x.shape[-1]
    rot = hd // ROT_FRAC
    half = rot // 2
    inv = jnp.power(ROPE_THETA, -jnp.arange(half, dtype=F32) * 2.0 / rot)
    ang = positions.astype(F32)[..., None] * inv
    cos = jnp.cos(ang)[:, :, None, :]
    sin = jnp.sin(ang)[:, :, None, :]
    xf = x.astype(F32)
    x1, x2 = xf[..., :half], xf[..., half:rot]
    out = jnp.concatenate([x1 * cos - x2 * sin, x2 * cos + x1 * sin, xf[..., rot:]], axis=-1)
    return out.astype(x.dtype)


def to_chunks(a, L):
    r = a.reshape(a.shape[0], a.shape[1] // L, L, *a.shape[2:])
    return jnp.swapaxes(jnp.moveaxis(r, 1, 0), 2, 3)


def from_chunks(a):
    r = jnp.moveaxis(jnp.swapaxes(a, 2, 3), 0, 1)
    return r.reshape(r.shape[0], r.shape[1] * r.shape[2], *r.shape[3:])


def mlstm_mixer(h, w_in, b_gate, norm_w, w_out):
    bsz, seq, _ = h.shape
    qk, vw = A_HEADS * A_DK, A_HEADS * A_DV
    proj = (h @ w_in).astype(F32)
    q = proj[..., :qk].reshape(bsz, seq, A_HEADS, A_DK)
    k = proj[..., qk:2 * qk].reshape(bsz, seq, A_HEADS, A_DK) * (A_DK ** -0.5)
    v = proj[..., 2 * qk:2 * qk + vw].reshape(bsz, seq, A_HEADS, A_DV)
    o = jax.nn.sigmoid(proj[..., 2 * qk + vw:2 * qk + 2 * vw]).reshape(bsz, seq, A_HEADS, A_DV)
    gates = proj[..., 2 * qk + 2 * vw:] + b_gate.astype(F32)
    i_pre = gates[..., :A_HEADS]
    log_f = jax.nn.log_sigmoid(gates[..., A_HEADS:])
    causal = jnp.tril(jnp.ones((A_CHUNK, A_CHUNK), bool))

    def step(carry, xs):
        C, n, m = carry
        qc, kc, vc, ic, fc = xs
        b = jnp.cumsum(fc, axis=-1)
        dmat = jnp.where(causal, b[..., :, None] - b[..., None, :] + ic[..., None, :], -jnp.inf)
        m_inter = b + m[..., None]
        m_t = jnp.maximum(m_inter, dmat.max(-1))
        a = jnp.einsum('bhtd,bhsd->bhts', qc, kc) * jnp.exp(dmat - m_t[..., None])
        w_inter = jnp.exp(m_inter - m_t)
        num = jnp.einsum('bhts,bhse->bhte', a, vc) + w_inter[..., None] * jnp.einsum('bhtd,bhde->bhte', qc, C)
        den = a.sum(-1) + w_inter * jnp.einsum('bhtd,bhd->bht', qc, n)
        h_out = num / jnp.maximum(jnp.abs(den), jnp.exp(-m_t))[..., None]
        b_last = b[..., -1]
        g = b_last[..., None] - b + ic
        m_new = jnp.maximum(b_last + m, g.max(-1))
        decay = jnp.exp(b_last + m - m_new)
        ws = jnp.exp(g - m_new[..., None])
        C_new = decay[..., None, None] * C + jnp.einsum('bhs,bhsd,bhse->bhde', ws, kc, vc)
        n_new = decay[..., None] * n + jnp.einsum('bhs,bhsd->bhd', ws, kc)
        return (C_new, n_new, m_new), h_out

    init = (jnp.zeros((bsz, A_HEADS, A_DK, A_DV), F32),
            jnp.zeros((bsz, A_HEADS, A_DK), F32),
            jnp.zeros((bsz, A_HEADS), F32))
    xs = (to_chunks(q, A_CHUNK), to_chunks(k, A_CHUNK), to_chunks(v, A_CHUNK),
          to_chunks(i_pre, A_CHUNK), to_chunks(log_f, A_CHUNK))
    _, ys = lax.scan(step, init, xs)
    hs = from_chunks(ys)
    y = head_rms_norm(hs, norm_w.reshape(A_HEADS, A_DV)) * o
    return y.reshape(bsz, seq, vw).astype(w_out.dtype) @ w_out


def swa_mixer(h, positions, w_in, sinks, w_out):
    bsz, seq, _ = h.shape
    nb = seq // B_BLOCK
    grp = B_Q_HEADS // B_KV_HEADS
    qw, kw = B_Q_HEADS * B_HEAD_DIM, B_KV_HEADS * B_HEAD_DIM
    proj = h @ w_in
    q = partial_rope(proj[..., :qw].reshape(bsz, seq, B_Q_HEADS, B_HEAD_DIM), positions)
    k = partial_rope(proj[..., qw:qw + kw].reshape(bsz, seq, B_KV_HEADS, B_HEAD_DIM), positions)
    v = proj[..., qw + kw:].reshape(bsz, seq, B_KV_HEADS, B_HEAD_DIM)
    qb = jnp.moveaxis(q.reshape(bsz, nb, B_BLOCK, B_KV_HEADS, grp, B_HEAD_DIM), 1, 0)

    def band(a):
        ap = jnp.pad(a, ((0, 0), (B_BLOCK, 0), (0, 0), (0, 0)))
        ap = ap.reshape(bsz, nb + 1, B_BLOCK, B_KV_HEADS, B_HEAD_DIM)
        return jnp.moveaxis(jnp.concatenate([ap[:, :-1], ap[:, 1:]], axis=2), 1, 0)

    kb, vb = band(k), band(v)
    sink = sinks.astype(F32).reshape(1, B_KV_HEADS, grp, 1, 1)
    scale = B_HEAD_DIM ** -0.5

    def block(args):
        qi, ki, vi, n = args
        s = jnp.einsum('bqhgd,bkhd->bhgqk', qi, ki, preferred_element_type=F32) * scale
        qpos = n * B_BLOCK + jnp.arange(B_BLOCK)
        kpos = (n - 1) * B_BLOCK + jnp.arange(2 * B_BLOCK)
        rel = qpos[:, None] - kpos[None, :]
        valid = (rel >= 0) & (rel < B_WINDOW) & (kpos >= 0)[None, :]
        s = jnp.where(valid, s, -jnp.inf)
        mx = jnp.maximum(s.max(-1, keepdims=True), sink)
        p = jnp.exp(s - mx)
        p = p / (p.sum(-1, keepdims=True) + jnp.exp(sink - mx))
        return jnp.einsum('bhgqk,bkhd->bqhgd', p, vi.astype(F32))

    out = lax.map(block, (qb, kb, vb, jnp.arange(nb)))
    out = jnp.moveaxis(out, 0, 1).reshape(bsz, seq, qw)
    return out.astype(w_out.dtype) @ w_out


def hgrn2_mixer(h, lower_bound, w_in, norm_w, w_out):
    bsz, seq, _ = h.shape
    kw, vw = C_HEADS * C_DK, C_HEADS * C_DV
    proj = (h @ w_in).astype(F32)
    q = proj[..., :kw]
    f_pre = proj[..., kw:2 * kw]
    i_in = proj[..., 2 * kw:2 * kw + vw]
    g_out = proj[..., 2 * kw + vw:]
    lb = lower_bound.astype(F32)
    log_f = jnp.logaddexp(jnp.log(lb), jnp.log1p(-lb) + jax.nn.log_sigmoid(f_pre))
    k = (1.0 - lb) * jax.nn.sigmoid(-f_pre)
    heads = lambda a, d: a.reshape(bsz, seq, C_HEADS, d)
    causal = jnp.tril(jnp.ones((C_CHUNK, C_CHUNK), bool))[:, :, None]

    def step(S, xs):
        qc, kc, vc, fc = xs
        cf = jnp.cumsum(fc, axis=2)
        diff = cf[:, :, :, None, :] - cf[:, :, None, :, :]
        decay = jnp.exp(jnp.where(causal, diff, -jnp.inf))
        a = jnp.einsum('bhtd,bhtsd,bhsd->bhts', qc, decay, kc)
        o = jnp.einsum('bhts,bhse->bhte', a, vc) + jnp.einsum('bhtd,bhde->bhte', qc * jnp.exp(cf), S)
        last = cf[:, :, -1:, :]
        S_new = jnp.exp(last[:, :, 0, :])[..., None] * S + jnp.einsum('bhsd,bhse->bhde', kc * jnp.exp(last - cf), vc)
        return S_new, o

    xs = (to_chunks(heads(q, C_DK), C_CHUNK), to_chunks(heads(k, C_DK), C_CHUNK),
          to_chunks(heads(i_in, C_DV), C_CHUNK), to_chunks(heads(log_f, C_DK), C_CHUNK))
    _, ys = lax.scan(step, jnp.zeros((bsz, C_HEADS, C_DK, C_DV), F32), xs)
    o = from_chunks(ys)
    y = head_rms_norm(o, norm_w.reshape(C_HEADS, C_DV)) * jax.nn.silu(heads(g_out, C_DV))
    return y.reshape(bsz, seq, vw).astype(w_out.dtype) @ w_out


def diff_mixer(h, positions, lam_init, w_in, lam_vec, norm_w, w_out):
    bsz, seq, _ = h.shape
    nb = seq // D_QBLOCK
    w = D_HEADS * 2 * D_HEAD_DIM
    proj = h @ w_in
    q = partial_rope(proj[..., :w].reshape(bsz, seq, 2 * D_HEADS, D_HEAD_DIM), positions)
    k = partial_rope(proj[..., w:2 * w].reshape(bsz, seq, 2 * D_HEADS, D_HEAD_DIM), positions)
    q = q.reshape(bsz, seq, D_HEADS, 2, D_HEAD_DIM)
    k = k.reshape(bsz, seq, D_HEADS, 2, D_HEAD_DIM)
    v = proj[..., 2 * w:].reshape(bsz, seq, D_HEADS, 2 * D_HEAD_DIM).astype(F32)
    lv = lam_vec.astype(F32)
    lam = jnp.exp(jnp.sum(lv[0] * lv[1])) - jnp.exp(jnp.sum(lv[2] * lv[3])) + lam_init
    qb = jnp.moveaxis(q.reshape(bsz, nb, D_QBLOCK, D_HEADS, 2, D_HEAD_DIM), 1, 0)
    kpos = jnp.arange(seq)
    scale = D_HEAD_DIM ** -0.5

    def block(args):
        qi, n = args
        s = jnp.einsum('bqhcd,bkhcd->bhcqk', qi, k, preferred_element_type=F32) * scale
        qpos = n * D_QBLOCK + jnp.arange(D_QBLOCK)
        s = jnp.where(kpos[None, :] <= qpos[:, None], s, -jnp.inf)
        p = jax.nn.softmax(s, axis=-1)
        a = p[:, :, 0] - lam * p[:, :, 1]
        return jnp.einsum('bhqk,bkhe->bqhe', a, v)

    out = lax.map(block, (qb, jnp.arange(nb)))
    out = jnp.moveaxis(out, 0, 1).reshape(bsz, seq, D_HEADS, 2 * D_HEAD_DIM)
    out = head_rms_norm(out, norm_w) * (1.0 - lam_init)
    return out.reshape(bsz, seq, w).astype(w_out.dtype) @ w_out


def swiglu(h, w_in, w_out):
    gu = h @ w_in
    return (jax.nn.silu(gu[..., :FFN_HIDDEN]) * gu[..., FFN_HIDDEN:]) @ w_out


def setup_inputs(seed: int = 0) -> dict:
    key = jax.random.key(seed)
    ks = jax.random.split(key, 26)
    nrm = lambda k, shape, s: jax.random.normal(k, shape, F32) * s
    D = D_MODEL
    beta = DEEPNORM_BETA
    x = nrm(ks[0], (BATCH, SEQ, D), 1.0)
    c = nrm(ks[1], (BATCH, D), 1.0)
    positions = jnp.broadcast_to(jnp.arange(SEQ, dtype=jnp.int32), (BATCH, SEQ))
    ada_w = nrm(ks[2], (DEPTH, 2, D, 3 * D), 0.1 * D ** -0.5)
    ada_b = nrm(ks[3], (DEPTH, 2, 3 * D), 0.01)
    ln_g = 1.0 + nrm(ks[4], (DEPTH, 2, D), 0.02)
    ln_b = nrm(ks[5], (DEPTH, 2, D), 0.01)
    mlstm_w_in = nrm(ks[6], (N_A, D, A_IN), D ** -0.5)
    mlstm_b_gate = jnp.concatenate(
        [nrm(ks[7], (N_A, A_HEADS), 0.1),
         jnp.linspace(3.0, 6.0, A_HEADS, dtype=F32) + nrm(ks[8], (N_A, A_HEADS), 0.1)], axis=-1)
    mlstm_norm = 1.0 + nrm(ks[9], (N_A, A_HEADS * A_DV), 0.02)
    mlstm_w_out = nrm(ks[10], (N_A, A_HEADS * A_DV, D), beta * (A_HEADS * A_DV) ** -0.5)
    swa_w_in = nrm(ks[11], (N_B, D, B_IN), D ** -0.5)
    swa_sinks = nrm(ks[12], (N_B, B_Q_HEADS), 0.5)
    swa_w_out = nrm(ks[13], (N_B, B_Q_HEADS * B_HEAD_DIM, D), beta * (B_Q_HEADS * B_HEAD_DIM) ** -0.5)
    hgrn_w_in = nrm(ks[14], (N_C, D, C_IN), D ** -0.5)
    hgrn_lower_bounds = nrm(ks[15], (DEPTH, C_HEADS * C_DK), 0.1)
    hgrn_norm = 1.0 + nrm(ks[16], (N_C, C_HEADS * C_DV), 0.02)
    hgrn_w_out = nrm(ks[17], (N_C, C_HEADS * C_DV, D), beta * (C_HEADS * C_DV) ** -0.5)
    diff_w_in = nrm(ks[18], (N_D, D, D_IN), D ** -0.5)
    diff_lambda = nrm(ks[19], (N_D, 4, D_HEAD_DIM), 0.1)
    diff_norm = 1.0 + nrm(ks[20], (N_D, 2 * D_HEAD_DIM), 0.02)
    diff_w_out = nrm(ks[21], (N_D, D_HEADS * 2 * D_HEAD_DIM, D), beta * (D_HEADS * 2 * D_HEAD_DIM) ** -0.5)
    ffn_w_in = nrm(ks[22], (DEPTH, D, 2 * FFN_HIDDEN), D ** -0.5)
    ffn_w_out = nrm(ks[23], (DEPTH, FFN_HIDDEN, D), beta * FFN_HIDDEN ** -0.5)
    return {"x": x, "c": c, "positions": positions, "ada_w": ada_w, "ada_b": ada_b,
            "ln_g": ln_g, "ln_b": ln_b,
            "mlstm_w_in": mlstm_w_in, "mlstm_b_gate": mlstm_b_gate, "mlstm_norm": mlstm_norm,
            "mlstm_w_out": mlstm_w_out,
            "swa_w_in": swa_w_in, "swa_sinks": swa_sinks, "swa_w_out": swa_w_out,
            "hgrn_w_in": hgrn_w_in, "hgrn_lower_bounds": hgrn_lower_bounds, "hgrn_norm": hgrn_norm,
            "hgrn_w_out": hgrn_w_out,
            "diff_w_in": diff_w_in, "diff_lambda": diff_lambda, "diff_norm": diff_norm,
            "diff_w_out": diff_w_out,
            "ffn_w_in": ffn_w_in, "ffn_w_out": ffn_w_out}


def reference(x, c, positions, ada_w, ada_b, ln_g, ln_b,
              mlstm_w_in, mlstm_b_gate, mlstm_norm, mlstm_w_out,
              swa_w_in, swa_sinks, swa_w_out,
              hgrn_w_in, hgrn_lower_bounds, hgrn_norm, hgrn_w_out,
              diff_w_in, diff_lambda, diff_norm, diff_w_out,
              ffn_w_in, ffn_w_out):
    cond = jax.nn.silu(c.astype(F32))
    lbs = jnp.cumsum(jax.nn.softmax(hgrn_lower_bounds.astype(F32), axis=0), axis=0)
    lbs = lbs - lbs[0]
    for i in range(DEPTH):
        mixer, j = i % N_MIXERS, i // N_MIXERS
        mod = jnp.einsum('bd,sde->sbe', cond, ada_w[i].astype(F32)) + ada_b[i][:, None, :].astype(F32)
        shift, scale, gate = jnp.split(mod.astype(x.dtype), 3, axis=-1)
        h = x * (1 + scale[0][:, None]) + shift[0][:, None]
        if mixer == 0:
            y = mlstm_mixer(h, mlstm_w_in[j], mlstm_b_gate[j], mlstm_norm[j], mlstm_w_out[j])
        elif mixer == 1:
            y = swa_mixer(h, positions, swa_w_in[j], swa_sinks[j], swa_w_out[j])
        elif mixer == 2:
            y = hgrn2_mixer(h, lbs[i], hgrn_w_in[j], hgrn_norm[j], hgrn_w_out[j])
        else:
            lam_init = 0.8 - 0.6 * math.exp(-0.3 * i)
            y = diff_mixer(h, positions, lam_init, diff_w_in[j], diff_lambda[j], diff_norm[j], diff_w_out[j])
        x = layer_norm(DEEPNORM_ALPHA * x + (1 + gate[0][:, None]) * y.astype(x.dtype), ln_g[i, 0], ln_b[i, 0])
        h = x * (1 + scale[1][:, None]) + shift[1][:, None]
        y = swiglu(h, ffn_w_in[i], ffn_w_out[i])
        x = layer_norm(DEEPNORM_ALPHA * x + (1 + gate[1][:, None]) * y.astype(x.dtype), ln_g[i, 1], ln_b[i, 1])
    return x
```

```python
import contextlib
import math
import numpy as np
import concourse.bass as bass
import concourse.mybir as mybir
from concourse.bass_utils import run_bass_kernel_spmd

F32 = mybir.dt.float32
BF16 = mybir.dt.bfloat16
I32 = mybir.dt.int32
AF = mybir.ActivationFunctionType
ALU = mybir.AluOpType
AX = mybir.AxisListType

D = 1024
KC = 8
FFN_H = 2816
NJ = 22
DEPTH = 4
ALPHA = (2 * DEPTH) ** 0.25
SAME_ENG_WINDOW = 6


class Sem:
    def __init__(self, h, name):
        self.h = h
        self.name = name
        self.count = 0


class Buf:
    __slots__ = ("name", "lw", "rd")

    def __init__(self, name):
        self.name = name
        self.lw = None
        self.rd = {}


class Eng:
    def __init__(self, name, sem, is_pe=False):
        self.name = name
        self.sem = sem
        self.is_pe = is_pe
        self.seen = {}
        self.ops = []


class Prog:
    def __init__(self, nc, stack):
        self.nc = nc
        self.stack = stack
        self.eng = {}
        for name in ("pe", "act", "dve", "pool", "sp"):
            self.eng[name] = Eng(name, self.new_sem("e_" + name), is_pe=(name == "pe"))
        self.snap = {}
        self.n_ops = 0
        self.dma_sems = []

    def new_sem(self, name):
        return Sem(self.stack.enter_context(self.nc.semaphore(name)), name)

    def dma_sem(self, name):
        s = self.new_sem(name)
        self.dma_sems.append(s)
        return s

    def _waits(self, E, reads, writes, is_dma):
        deps = {}

        def add(d, raw):
            if d is None:
                return
            S, v = d
            if S is E.sem and not is_dma:
                if E.is_pe or not raw:
                    return
                if v <= E.sem.count - SAME_ENG_WINDOW:
                    return
            if deps.get(S, 0) < v:
                deps[S] = v

        for r in reads:
            add(r.lw, True)
        for w in writes:
            add(w.lw, False)
            for S, v in w.rd.items():
                add((S, v), False)
        out = []
        for S, v in deps.items():
            if E.seen.get(S, 0) >= v:
                continue
            out.append((S, v))
        return out

    def _apply_waits(self, E, waits):
        for S, v in waits:
            E.ops.append(("wait", S, v))
            sn = self.snap.get((S, v))
            if sn:
                for S2, v2 in sn.items():
                    if E.seen.get(S2, 0) < v2:
                        E.seen[S2] = v2
            if E.seen.get(S, 0) < v:
                E.seen[S] = v

    def op(self, eng, fn, reads=(), writes=()):
        E = self.eng[eng]
        self._apply_waits(E, self._waits(E, reads, writes, False))
        E.sem.count += 1
        done = (E.sem, E.sem.count)
        E.ops.append(("op", fn, E.sem, 1))
        self.snap[done] = dict(E.seen)
        for w in writes:
            w.lw = done
            w.rd = {}
        for r in reads:
            if r.rd.get(E.sem, 0) < E.sem.count:
                r.rd[E.sem] = E.sem.count
        self.n_ops += 1

    def dma(self, eng, sem, out, in_, reads=(), writes=(), **kw):
        E = self.eng[eng]
        waits = self._waits(E, reads, writes, True)
        if sem.count > 0 and E.seen.get(sem, 0) < sem.count and not any(S is sem for S, _ in waits):
            waits.append((sem, sem.count))
        waits = [(S, max(v, sem.count) if S is sem else v) for S, v in waits]
        self._apply_waits(E, waits)
        sem.count += 16
        done = (sem, sem.count)
        E.ops.append(("op", lambda e: e.dma_start(out=out, in_=in_, **kw), sem, 16))
        self.snap[done] = dict(E.seen)
        for w in writes:
            w.lw = done
            w.rd = {}
        for r in reads:
            if r.rd.get(sem, 0) < sem.count:
                r.rd[sem] = sem.count
        self.n_ops += 1

    def barrier(self):
        targets = [(E.sem, E.sem.count) for E in self.eng.values() if E.sem.count > 0]
        targets += [(s, s.count) for s in self.dma_sems if s.count > 0]
        for E in self.eng.values():
            ws = [(S, v) for S, v in targets if S is not E.sem and E.seen.get(S, 0) < v]
            self._apply_waits(E, ws)

    def final_wait(self, eng, sems):
        E = self.eng[eng]
        self._apply_waits(E, [(s, s.count) for s in sems if s.count > 0 and E.seen.get(s, 0) < s.count])

    def emit(self):
        nc = self.nc
        with nc.Block() as block:
            def run(E):
                def body(e):
                    for o in E.ops:
                        if o[0] == "wait":
                            e.wait_ge(o[1].h, o[2])
                        else:
                            o[1](e).then_inc(o[2].h, o[3])
                return body

            block.tensor(run(self.eng["pe"]))
            block.scalar(run(self.eng["act"]))
            block.vector(run(self.eng["dve"]))
            block.gpsimd(run(self.eng["pool"]))
            block.sync(run(self.eng["sp"]))


class Arena:
    def __init__(self, P, name, nbytes):
        self.t = P.stack.enter_context(P.nc.sbuf_tensor(name, [128, nbytes // 4], F32))
        self.nbytes = nbytes
        self.off = 0
        self.name = name

    def reset(self):
        self.off = 0

    def alloc(self, name, shape, dt, nbuf=None):
        esz = 2 if dt == BF16 else 4
        n = int(np.prod(shape))
        nb = (n * esz + 31) // 32 * 32
        res = []
        for i in range(nbuf or 1):
            assert self.off + nb <= self.nbytes, (self.name, name, self.off, nb, self.nbytes)
            v = self.t[:, self.off // 4:(self.off + nb) // 4]
            if dt != F32:
                v = v.bitcast(dt)
            v = v[:, 0:n]
            if len(shape) == 2:
                v = v.rearrange("p (a b) -> p a b", b=shape[1])
            elif len(shape) == 3:
                v = v.rearrange("p (a b c) -> p a b c", b=shape[1], c=shape[2])
            self.off += nb
            res.append((v, Buf(name + str(i))))
        return res if nbuf else res[0]


C_ID = 0
C_CAUS = 128
C_STRICT = 256
C_INV = 384
C_ONES = 392
C_N = 520
TWO_PI = 2.0 * math.pi
CW1 = 6.28125
CW2 = float(np.float32(TWO_PI - CW1))
CW3 = float(TWO_PI - CW1 - CW2)


def make_consts():
    c = np.zeros((128, C_N), np.float32)
    c[:, C_ID:C_ID + 128] = np.eye(128, dtype=np.float32)
    k = np.arange(128)[:, None]
    q = np.arange(128)[None, :]
    c[:, C_CAUS:C_CAUS + 128] = (k <= q)
    c[:, C_STRICT:C_STRICT + 128] = (k > q)
    inv = np.power(np.float32(500000.0), -np.arange(8, dtype=np.float32) * np.float32(2.0) / np.float32(16.0)).astype(np.float32)
    c[:, C_INV:C_INV + 8] = inv[None, :]
    c[:, C_ONES:C_ONES + 128] = 1.0
    return c


W_SPECS = [
    ("mlstm_w_in", 1024, 3080), ("mlstm_w_out", 1024, 1024),
    ("swa_w_in", 1024, 1280), ("swa_w_out", 1024, 1024),
    ("hgrn_w_in", 1024, 4096), ("hgrn_w_out", 1024, 1024),
    ("diff_w_in", 1024, 3072), ("diff_w_out", 1024, 1024),
]


DBG_SKIP = set()


class K:
    pass


def build(NSEQ=2, S=4096, passes=None, arena_kb=188, debug=False):
    NT = S // 128
    NST = S // 512
    if passes is None:
        passes = []
        for l in range(DEPTH):
            passes += [("mix", l), ("ffn", l)]
    nc = bass.Bass("TRN2", target_bir_lowering=False)
    stack = contextlib.ExitStack()
    P = Prog(nc, stack)
    k = K()
    k.P, k.nc, k.NSEQ, k.S, k.NT, k.NST = P, nc, NSEQ, S, NT, NST
    k.debug = debug

    def din(name, shape, dt=F32):
        return nc.dram_tensor(name, list(shape), dt, kind="ExternalInput").ap()

    k.x = din("x", [NSEQ, S, D])
    k.c = din("c", [NSEQ, D])
    k.pos = din("positions", [NSEQ, S], I32)
    k.ada_w = din("ada_w", [DEPTH, 2, D, 3 * D])
    k.ada_b = din("ada_b", [DEPTH, 2, 3 * D])
    k.ln_g = din("ln_g", [DEPTH, 2, D])
    k.ln_b = din("ln_b", [DEPTH, 2, D])
    k.w32 = {}
    for name, r, c_ in W_SPECS:
        k.w32[name] = din(name, [1, r, c_])
    k.mlstm_b_gate = din("mlstm_b_gate", [1, 8])
    k.mlstm_norm = din("mlstm_norm", [1, 1024])
    k.swa_sinks = din("swa_sinks", [1, 16])
    k.hgrn_lb = din("hgrn_lower_bounds", [4, 1024])
    k.hgrn_norm = din("hgrn_norm", [1, 1024])
    k.diff_lambda = din("diff_lambda", [1, 4, 64])
    k.diff_norm = din("diff_norm", [1, 128])
    k.ffn_w_in = din("ffn_w_in", [DEPTH, D, 2 * FFN_H])
    k.ffn_w_out = din("ffn_w_out", [DEPTH, FFN_H, D])
    k.consts = din("consts", [128, C_N])
    k.out = nc.dram_tensor("out", [NSEQ, S, D], F32, kind="ExternalOutput").ap()

    def dscr(name, shape, dt):
        return nc.dram_tensor(name, list(shape), dt, kind="Internal").ap()

    k.act = dscr("act_scr", [NSEQ, S, D], F32)
    k.modrows = dscr("modrows", [8, NSEQ, 3 * D], F32)
    k.qt_scr = dscr("qt_scr", [8, 128, S], BF16)
    k.kt_scr = dscr("kt_scr", [8, 128, S], BF16)
    k.v_scr = dscr("v_scr", [S, 8 * 130], BF16)
    k.qt_bufs = [Buf("qt%d" % i) for i in range(NST)]
    k.kt_bufs = [Buf("kt%d" % i) for i in range(NST)]
    k.v_bufs = [Buf("v%d" % i) for i in range(NT)]
    k.wb = {}
    k.wb_buf = {}
    for name, r, c_ in W_SPECS:
        k.wb[name] = dscr(name + "_bf", [r, c_], BF16)
        k.wb_buf[name] = Buf(name)
    layers_used = sorted(set(l for _, l in passes))
    for l in layers_used:
        k.wb["w1_%d" % l] = dscr("w1bf_%d" % l, [D, 2 * FFN_H], BF16)
        k.wb["w2_%d" % l] = dscr("w2bf_%d" % l, [FFN_H, D], BF16)
        k.wb_buf["w1_%d" % l] = Buf("w1_%d" % l)
        k.wb_buf["w2_%d" % l] = Buf("w2_%d" % l)
    k.act_buf = [[Buf("act%d_%d" % (b, t)) for t in range(NT)] for b in range(NSEQ)]
    k.modrows_buf = [Buf("modrows%d" % i) for i in range(8)]

    pers = Arena(P, "pers", 12 * 1024)
    k.cst, k.cst_b = pers.alloc("cst", [C_N], F32)
    k.idb, k.idb_b = pers.alloc("idb", [128], BF16)
    k.caus_b16, k.caus_b16_b = pers.alloc("causb", [128], BF16)
    k.strict_b16, k.strict_b16_b = pers.alloc("strictb", [128], BF16)
    k.modT, k.modT_b = pers.alloc("modT", [8 * NSEQ * 2 * KC], F32)
    k.rope, k.rope_b = pers.alloc("rope", [NSEQ * 2 * NT * 8], F32)
    k.small, k.small_b = pers.alloc("small", [64], F32)
    k.pers = pers
    k.arena = Arena(P, "arena", arena_kb * 1024)
    k.bank = []
    for i in range(8):
        t = stack.enter_context(nc.psum_tensor("bank%d" % i, [128, 512], F32))
        k.bank.append((t, Buf("bank%d" % i)))
    k.ld_sems = [P.dma_sem("ld%d" % i) for i in range(12)]
    k.st_sems = [P.dma_sem("st%d" % i) for i in range(6)]
    k.ld_i = 0
    k.st_i = 0

    def ld_sem():
        k.ld_i += 1
        return k.ld_sems[k.ld_i % len(k.ld_sems)]

    def st_sem():
        k.st_i += 1
        return k.st_sems[k.st_i % len(k.st_sems)]

    k.ld_sem, k.st_sem = ld_sem, st_sem
    k.dumped = {}
    k.cast_i = {}
    k.cast_n = 0

    def dump(name, ap, buf, dt=F32):
        if not k.debug or name in k.dumped:
            return
        shape = [int(x) for x in ap.shape]
        d = nc.dram_tensor("dbg_" + name, shape, dt, kind="ExternalOutput").ap()
        k.dumped[name] = d
        P.dma("sp", k.ld_sem(), d, ap, reads=[buf])

    k.dump = dump

    P.dma("sp", ld_sem(), k.cst, k.consts, writes=[k.cst_b])
    P.op("dve", lambda e: e.tensor_copy(out=k.idb, in_=k.cst[:, C_ID:C_ID + 128]), reads=[k.cst_b], writes=[k.idb_b])
    P.op("dve", lambda e: e.tensor_copy(out=k.caus_b16, in_=k.cst[:, C_CAUS:C_CAUS + 128]), reads=[k.cst_b], writes=[k.caus_b16_b])
    P.op("dve", lambda e: e.tensor_copy(out=k.strict_b16, in_=k.cst[:, C_STRICT:C_STRICT + 128]), reads=[k.cst_b], writes=[k.strict_b16_b])
    k.ident = k.cst[:, C_ID:C_ID + 128]

    phase_prepass(k, layers_used, passes)
    phase_ada(k, passes)
    if any(kind == "mix" and l in (1, 3) for kind, l in passes):
        phase_rope(k)
    src_is_x = True
    for pi, (kind, l) in enumerate(passes):
        last = pi == len(passes) - 1
        for b in range(NSEQ):
            P.barrier()
            k.arena.reset()
            src = k.x if src_is_x else k.act
            dst = k.out if last else k.act
            if (kind, b) in DBG_SKIP:
                continue
            if kind == "ffn":
                pass_ffn(k, l, b, src, dst)
            else:
                pass_mix(k, l, b, src, dst)
        src_is_x = False
    if debug:
        P.barrier()
        dact = nc.dram_tensor("dbg_act", [NSEQ, S, D], F32, kind="ExternalOutput").ap()
        for b in range(NSEQ):
            P.dma("pool", k.st_sem(), dact[b], k.act[b], reads=[x for x in k.act_buf[b]])
        dmt = nc.dram_tensor("dbg_modT", [128, 8 * NSEQ * 2 * KC], F32, kind="ExternalOutput").ap()
        P.dma("pool", k.st_sem(), dmt, k.modT, reads=[k.modT_b])
        drp = nc.dram_tensor("dbg_rope2", [128, NSEQ * 2 * NT * 8], F32, kind="ExternalOutput").ap()
        P.dma("pool", k.st_sem(), drp, k.rope, reads=[k.rope_b])
        dmod = nc.dram_tensor("dbg_mod", [8, NSEQ, 3 * D], F32, kind="ExternalOutput").ap()
        P.dma("pool", k.st_sem(), dmod, k.modrows, reads=k.modrows_buf)
    P.final_wait("sp", k.st_sems)
    P.emit()
    stack.close()
    return nc


BACKGROUND_CAST = True
CAST_W = 1408

MIXW = {0: ("mlstm_w_in", "mlstm_w_out"), 1: ("swa_w_in", "swa_w_out"),
        2: ("hgrn_w_in", "hgrn_w_out"), 3: ("diff_w_in", "diff_w_out")}


def phase_prepass(k, layers_used, passes):
    P, A = k.P, k.arena
    A.reset()
    s32 = A.alloc("s32", [5632], F32, nbuf=2)
    s16 = A.alloc("s16", [5632], BF16, nbuf=2)
    jobs = []
    k.deferred = {}
    for L in layers_used:
        lj = []
        idxs = [i for i, (kd, ll) in enumerate(passes) if ll == L]
        if ("mix", L) in passes:
            for name in MIXW[L]:
                lj.append((k.w32[name][0], k.wb[name], k.wb_buf[name]))
        if ("ffn", L) in passes:
            lj.append((k.ffn_w_in[L], k.wb["w1_%d" % L], k.wb_buf["w1_%d" % L]))
            lj.append((k.ffn_w_out[L], k.wb["w2_%d" % L], k.wb_buf["w2_%d" % L]))
        host = ("ffn", L - 1)
        if BACKGROUND_CAST and host in passes and passes.index(host) < min(idxs):
            ch = []
            for src, dst, buf in lj:
                R, C = src.shape
                for rb in range(R // 128):
                    for c0 in range(0, C, CAST_W):
                        ch.append((src[rb * 128:(rb + 1) * 128, c0:min(C, c0 + CAST_W)],
                                   dst[rb * 128:(rb + 1) * 128, c0:min(C, c0 + CAST_W)], buf, min(C, c0 + CAST_W) - c0))
            k.deferred[L - 1] = ch
        else:
            jobs += lj
    engs = ["act", "dve", "pool"]
    i = 0
    for src, dst, buf in jobs:
        R, C = src.shape
        for rb in range(R // 128):
            (a32, b32), (a16, b16) = s32[i % 2], s16[i % 2]
            P.dma("sp", k.ld_sem(), a32[:, 0:C], src[rb * 128:(rb + 1) * 128, :], writes=[b32])
            eng = engs[i % 3]
            if eng == "act":
                P.op("act", lambda e, o=a16[:, 0:C], i_=a32[:, 0:C]: e.activation(out=o, in_=i_, func=AF.Copy),
                     reads=[b32], writes=[b16])
            else:
                P.op(eng, lambda e, o=a16[:, 0:C], i_=a32[:, 0:C]: e.tensor_copy(out=o, in_=i_),
                     reads=[b32], writes=[b16])
            P.dma("pool", k.st_sem(), dst[rb * 128:(rb + 1) * 128, :], a16[:, 0:C], reads=[b16], writes=[buf])
            i += 1


def modT5(k):
    return k.modT.rearrange("p (s b w kc) -> p s b w kc", s=8, b=k.NSEQ, w=2, kc=KC)


def phase_ada(k, passes):
    P, A, NSEQ = k.P, k.arena, k.NSEQ
    P.barrier()
    A.reset()
    condT, condT_b = A.alloc("condT", [KC, NSEQ], F32)
    for b in range(NSEQ):
        P.dma("sp", k.ld_sem(), condT[:, :, b], k.c[b].rearrange("(kc p) -> p kc", p=128), writes=[condT_b],
              allow_slow_non_contiguous=True)
    P.op("act", lambda e: e.activation(out=condT, in_=condT, func=AF.Silu), reads=[condT_b], writes=[condT_b])
    slab = A.alloc("aslab", [KC, 512], F32, nbuf=2)
    biasrow = A.alloc("abias", [3072], F32, nbuf=2)
    modrow = A.alloc("modrow", [3072], F32, nbuf=2)
    sls = sorted(set(2 * l + (0 if kind == "mix" else 1) for kind, l in passes))
    m5 = modT5(k)
    cnt = 0
    for idx, sl in enumerate(sls):
        l, s = sl // 2, sl % 2
        br, br_b = biasrow[idx % 2]
        mr, mr_b = modrow[idx % 2]
        for b in range(NSEQ):
            P.dma("sp", k.ld_sem(), br[b:b + 1, :], k.ada_b[l, s:s + 1, :], writes=[br_b])
        wv = k.ada_w[l, s].rearrange("(kc p) e -> p kc e", p=128)
        for n in range(6):
            sa, sa_b = slab[cnt % 2]
            bk, bk_b = k.bank[cnt % 2]
            cnt += 1
            P.dma("sp", k.ld_sem(), sa, wv[:, :, n * 512:(n + 1) * 512], writes=[sa_b])
            for kc in range(KC):
                P.op("pe", lambda e, bk=bk, sa=sa, kc=kc: e.matmul(bk[0:NSEQ, :], condT[:, kc, :], sa[:, kc, :],
                                                                     start=(kc == 0), stop=(kc == KC - 1)),
                     reads=[condT_b, sa_b], writes=[bk_b])
            P.op("dve", lambda e, bk=bk, n=n, mr=mr, br=br: e.tensor_tensor(
                out=mr[0:NSEQ, n * 512:(n + 1) * 512], in0=bk[0:NSEQ, :], in1=br[0:NSEQ, n * 512:(n + 1) * 512],
                op=ALU.add), reads=[bk_b, br_b], writes=[mr_b])
        P.op("dve", lambda e, mr=mr: e.tensor_scalar(out=mr[0:NSEQ, 1024:3072], in0=mr[0:NSEQ, 1024:3072],
                                                    scalar1=1.0, scalar2=None, op0=ALU.add),
             reads=[mr_b], writes=[mr_b])
        P.dma("pool", k.st_sem(), k.modrows[sl], mr[0:NSEQ, :], reads=[mr_b], writes=[k.modrows_buf[sl]])
        for b in range(NSEQ):
            P.dma("sp", k.ld_sem(), m5[:, sl, b, :, :],
                  k.modrows[sl, b, 0:2048].rearrange("(w kc p) -> p w kc", p=128, kc=KC),
                  reads=[k.modrows_buf[sl]], writes=[k.modT_b], allow_slow_non_contiguous=True)


def load_x(k, b, st, src, xs):
    P = k.P
    for t in range(4):
        tt = st * 4 + t
        rd = [k.act_buf[b][tt]] if src is k.act else []
        P.dma("sp", k.ld_sem(), xs[t][0], src[b, tt * 128:(tt + 1) * 128, :], reads=rd, writes=[xs[t][1]])


def prologue(k, b, sl, xs, hT, hT_bufs, tpb=(0, 1)):
    P = k.P
    m5 = modT5(k)
    for kc in range(KC):
        bk, bk_b = k.bank[tpb[kc % 2]]
        for t in range(4):
            P.op("pe", lambda e, bk=bk, t=t, kc=kc: e.transpose(out=bk[:, t * 128:(t + 1) * 128],
                                                                in_=xs[t][0][:, kc * 128:(kc + 1) * 128],
                                                                identity=k.ident),
                 reads=[xs[t][1], k.cst_b], writes=[bk_b])
        P.op("act", lambda e, bk=bk, kc=kc: e.activation(out=hT[:, kc, :], in_=bk[:, :], func=AF.Identity,
                                                         scale=m5[:, sl, b, 1, kc:kc + 1],
                                                         bias=m5[:, sl, b, 0, kc:kc + 1]),
             reads=[bk_b, k.modT_b], writes=[hT_bufs[kc]])


def alloc_epi(k, sl, b, nbuf=2):
    P, A = k.P, k.arena
    ep = K()
    l, s = sl // 2, sl % 2
    ep.g1p = A.alloc("g1p", [1024], F32)
    ep.lng = A.alloc("lng", [1024], F32)
    ep.lnb = A.alloc("lnb", [1024], F32)
    P.dma("sp", k.ld_sem(), ep.g1p[0], k.modrows[sl, b, 2048:3072].partition_broadcast(128),
          reads=[k.modrows_buf[sl]], writes=[ep.g1p[1]])
    P.dma("sp", k.ld_sem(), ep.lng[0], k.ln_g[l, s, :].partition_broadcast(128), writes=[ep.lng[1]])
    P.dma("sp", k.ld_sem(), ep.lnb[0], k.ln_b[l, s, :].partition_broadcast(128), writes=[ep.lnb[1]])
    ep.z = A.alloc("z", [1024], F32, nbuf=nbuf)
    ep.xo = A.alloc("xo", [1024], F32, nbuf=nbuf)
    ep.st = A.alloc("epst", [32], F32, nbuf=4)
    ep.nh = A.alloc("neghalf", [1], F32)
    P.op("pool", lambda e: e.memset(ep.nh[0], -0.5), writes=[ep.nh[1]])
    ep.i = 0
    return ep


def epilogue(k, ep, b, tt, ybanks, x_ap, x_buf, dst):
    P = k.P
    i = ep.i
    ep.i += 1
    z, z_b = ep.z[i % len(ep.z)]
    xn, xn_b = z, z_b
    xo, xo_b = ep.xo[i % len(ep.xo)]
    st, st_b = ep.st[i % 4]
    g1p, g1p_b = ep.g1p
    for h in range(2):
        yb, yb_b = k.bank[ybanks[h]]
        P.op("dve", lambda e, h=h, yb=yb: e.tensor_tensor(out=z[:, h * 512:(h + 1) * 512], in0=yb[:, :],
                                                         in1=g1p[:, h * 512:(h + 1) * 512], op=ALU.mult),
             reads=[yb_b, g1p_b], writes=[z_b])
    P.op("dve", lambda e: e.scalar_tensor_tensor(out=z, in0=x_ap, scalar=ALPHA, in1=z, op0=ALU.mult, op1=ALU.add),
         reads=[x_buf, z_b], writes=[z_b])
    for h in range(2):
        P.op("dve", lambda e, h=h: e.bn_stats(out=st[:, h * 6:(h + 1) * 6], in_=z[:, h * 512:(h + 1) * 512]),
             reads=[z_b], writes=[st_b])
    mv = st[:, 12:14]
    P.op("dve", lambda e: e.bn_aggr(out=mv, in_=st[:, 0:12]), reads=[st_b], writes=[st_b])
    P.op("dve", lambda e: e.tensor_scalar(out=st[:, 14:15], in0=st[:, 13:14], scalar1=1e-5, scalar2=None, op0=ALU.add),
         reads=[st_b], writes=[st_b])
    P.op("pool", lambda e: e.tensor_tensor(out=st[:, 15:16], in0=st[:, 14:15], in1=ep.nh[0], op=ALU.pow),
         reads=[st_b, ep.nh[1]], writes=[st_b])
    P.op("dve", lambda e: e.scalar_tensor_tensor(out=st[:, 16:17], in0=st[:, 12:13], scalar=-1.0, in1=st[:, 15:16],
                                                 op0=ALU.mult, op1=ALU.mult), reads=[st_b], writes=[st_b])
    P.op("act", lambda e: e.activation(out=xn, in_=z, func=AF.Identity, scale=st[:, 15:16], bias=st[:, 16:17]),
         reads=[z_b, st_b], writes=[xn_b])
    P.op("pool", lambda e: e.tensor_tensor(out=xn, in0=xn, in1=ep.lng[0], op=ALU.mult),
         reads=[xn_b, ep.lng[1]], writes=[xn_b])
    P.op("pool", lambda e: e.tensor_tensor(out=xo, in0=xn, in1=ep.lnb[0], op=ALU.add),
         reads=[xn_b, ep.lnb[1]], writes=[xo_b])
    wr = [k.act_buf[b][tt]] if dst is k.act else []
    P.dma("pool", k.st_sem(), dst[b, tt * 128:(tt + 1) * 128, :], xo, reads=[xo_b], writes=wr)


def pass_ffn(k, l, b, src, dst):
    P, A = k.P, k.arena
    sl = 2 * l + 1
    xs = [[A.alloc("x", [1024], F32) for t in range(4)] for i in range(2)]
    hT, _ = A.alloc("hT", [KC, 512], BF16)
    hT_bufs = [Buf("hT%d" % i) for i in range(KC)]
    hid, _ = A.alloc("hid", [NJ, 512], BF16)
    hid_bufs = [Buf("hid%d" % i) for i in range(NJ)]
    w1s = A.alloc("w1s", [KC, 2, 256], BF16, nbuf=2)
    w1s_bufs = [[Buf("w1g"), Buf("w1u")] for i in range(2)]
    w2, w2_b = A.alloc("w2", [NJ, 1024], BF16)
    sg = A.alloc("sg", [512], F32, nbuf=2)
    ep = alloc_epi(k, sl, b)
    w1v = k.wb["w1_%d" % l].rearrange("(kc p) n -> p kc n", p=128)
    w1_buf = k.wb_buf["w1_%d" % l]
    P.dma("sp", k.ld_sem(), w2, k.wb["w2_%d" % l].rearrange("(j p) n -> p j n", p=128),
          reads=[k.wb_buf["w2_%d" % l]], writes=[w2_b])
    dj = k.deferred.get(l) or []
    if dj:
        c32 = A.alloc("c32", [CAST_W], F32, nbuf=2)
        c16 = A.alloc("c16", [CAST_W], BF16, nbuf=2)
        n_slots = k.NSEQ * k.NST
        per = -(-len(dj) // n_slots) if k.cast_i.get(l, 0) == 0 or True else 0

    def cast_tick(n):
        i0 = k.cast_i.get(l, 0)
        for (src_c, dst_c, buf, w_) in dj[i0:i0 + n]:
            i = k.cast_n
            k.cast_n += 1
            (a32, b32), (a16, b16) = c32[i % 2], c16[i % 2]
            P.dma("sp", k.ld_sem(), a32[:, 0:w_], src_c, writes=[b32])
            P.op("pool", lambda e, o=a16[:, 0:w_], i_=a32[:, 0:w_]: e.tensor_copy(out=o, in_=i_), reads=[b32], writes=[b16])
            P.dma("pool", k.st_sem(), dst_c, a16[:, 0:w_], reads=[b16], writes=[buf])
        k.cast_i[l] = min(len(dj), i0 + n)

    load_x(k, b, 0, src, xs[0])
    nslab = 0
    for st in range(k.NST):
        if st + 1 < k.NST:
            load_x(k, b, st + 1, src, xs[(st + 1) % 2])
        xcur = xs[st % 2]
        prologue(k, b, sl, xcur, hT, hT_bufs)
        done_here = 0
        for jj in range(NJ // 2):
            if dj and done_here < per and jj * per // (NJ // 2) >= done_here:
                cast_tick(1)
                done_here += 1
            ws, _ = w1s[nslab % 2]
            wsb = w1s_bufs[nslab % 2]
            nslab += 1
            P.dma("sp", k.ld_sem(), ws[:, :, 0, :], w1v[:, :, jj * 256:(jj + 1) * 256], reads=[w1_buf], writes=[wsb[0]])
            P.dma("sp", k.ld_sem(), ws[:, :, 1, :], w1v[:, :, FFN_H + jj * 256:FFN_H + (jj + 1) * 256],
                  reads=[w1_buf], writes=[wsb[1]])
            for jl in range(2):
                j = 2 * jj + jl
                gb, gb_b = k.bank[2 + j % 2]
                ub, ub_b = k.bank[4 + j % 2]
                for which, (bk, bk_b) in enumerate(((gb, gb_b), (ub, ub_b))):
                    for kc in range(KC):
                        P.op("pe", lambda e, bk=bk, ws=ws, kc=kc, which=which, jl=jl: e.matmul(
                            bk[:, :], ws[:, kc, which, jl * 128:(jl + 1) * 128], hT[:, kc, :],
                            start=(kc == 0), stop=(kc == KC - 1)),
                            reads=[wsb[which], hT_bufs[kc]], writes=[bk_b])
                sga, sga_b = sg[j % 2]
                P.op("act", lambda e, gb=gb, sga=sga: e.activation(out=sga, in_=gb[:, :], func=AF.Silu),
                     reads=[gb_b], writes=[sga_b])
                P.op("dve", lambda e, ub=ub, sga=sga, j=j: e.tensor_tensor(out=hid[:, j, :], in0=ub[:, :], in1=sga,
                                                                        op=ALU.mult),
                     reads=[ub_b, sga_b], writes=[hid_bufs[j]])
        for t in range(4):
            ybk = (6, 7) if t % 2 == 0 else (2, 3)
            for h in range(2):
                yb, yb_b = k.bank[ybk[h]]
                for j in range(NJ):
                    P.op("pe", lambda e, yb=yb, j=j, t=t, h=h: e.matmul(
                        yb[:, :], hid[:, j, t * 128:(t + 1) * 128], w2[:, j, h * 512:(h + 1) * 512],
                        start=(j == 0), stop=(j == NJ - 1)),
                        reads=[hid_bufs[j], w2_b], writes=[yb_b])
            epilogue(k, ep, b, st * 4 + t, ybk, xcur[t][0], xcur[t][1], dst)
        if dj and b == k.NSEQ - 1 and st == k.NST - 1:
            cast_tick(len(dj))


PI_LO = float(np.nextafter(np.float32(np.pi), np.float32(0)))


def phase_rope(k):
    P, A, NSEQ, NT = k.P, k.arena, k.NSEQ, k.NT
    P.barrier()
    A.reset()
    rv = k.rope.rearrange("p (b w n j) -> p b w n j", b=NSEQ, w=2, n=NT, j=8)

    def one(b):
        pi, pi_b = A.alloc("posi", [NT], I32)
        pf, pf_b = A.alloc("posf", [NT], F32)
        ang, ang_b = A.alloc("ang", [NT, 8], F32)
        kf, kf_b = A.alloc("kf", [NT, 8], F32)
        ki, ki_b = A.alloc("ki", [NT, 8], I32)
        r, r_b = A.alloc("r", [NT, 8], F32)
        m, m_b = A.alloc("m", [NT, 8], F32)
        rc, rc_b = A.alloc("rc", [NT, 8], F32)
        P.dma("sp", k.ld_sem(), pi, k.pos[b].rearrange("(n p) -> p n", p=128), writes=[pi_b],
              allow_slow_non_contiguous=True)
        P.op("dve", lambda e: e.tensor_copy(out=pf, in_=pi), reads=[pi_b], writes=[pf_b])
        P.op("dve", lambda e: e.tensor_tensor(out=ang, in0=pf.unsqueeze(2).to_broadcast([128, NT, 8]),
                                              in1=k.cst[:, C_INV:C_INV + 8].unsqueeze(1).to_broadcast([128, NT, 8]),
                                              op=ALU.mult), reads=[pf_b, k.cst_b], writes=[ang_b])
        P.op("dve", lambda e: e.tensor_scalar(out=kf, in0=ang, scalar1=1.0 / TWO_PI, scalar2=None, op0=ALU.mult),
             reads=[ang_b], writes=[kf_b])
        P.op("dve", lambda e: e.tensor_copy(out=ki, in_=kf), reads=[kf_b], writes=[ki_b])
        P.op("dve", lambda e: e.tensor_copy(out=kf, in_=ki), reads=[ki_b], writes=[kf_b])
        P.op("dve", lambda e: e.scalar_tensor_tensor(out=r, in0=kf, scalar=-CW1, in1=ang, op0=ALU.mult, op1=ALU.add),
             reads=[kf_b, ang_b], writes=[r_b])
        for cw in (CW2, CW3):
            P.op("dve", lambda e, cw=cw: e.scalar_tensor_tensor(out=r, in0=kf, scalar=-cw, in1=r, op0=ALU.mult, op1=ALU.add),
                 reads=[kf_b, r_b], writes=[r_b])

        def wrap(t, t_b):
            P.op("dve", lambda e: e.tensor_scalar(out=m, in0=t, scalar1=math.pi, scalar2=None, op0=ALU.is_gt),
                 reads=[t_b], writes=[m_b])
            P.op("dve", lambda e: e.scalar_tensor_tensor(out=t, in0=m, scalar=-TWO_PI, in1=t, op0=ALU.mult, op1=ALU.add),
                 reads=[m_b, t_b], writes=[t_b])
            P.op("dve", lambda e: e.tensor_scalar(out=m, in0=t, scalar1=-math.pi, scalar2=None, op0=ALU.is_lt),
                 reads=[t_b], writes=[m_b])
            P.op("dve", lambda e: e.scalar_tensor_tensor(out=t, in0=m, scalar=TWO_PI, in1=t, op0=ALU.mult, op1=ALU.add),
                 reads=[m_b, t_b], writes=[t_b])
            P.op("dve", lambda e: e.tensor_scalar(out=t, in0=t, scalar1=-PI_LO, scalar2=PI_LO, op0=ALU.max, op1=ALU.min),
                 reads=[t_b], writes=[t_b])

        wrap(r, r_b)
        P.op("dve", lambda e: e.tensor_scalar(out=rc, in0=r, scalar1=math.pi / 2, scalar2=None, op0=ALU.add),
             reads=[r_b], writes=[rc_b])
        wrap(rc, rc_b)
        P.op("act", lambda e, b=b: e.activation(out=rv[:, b, 1, :, :], in_=r, func=AF.Sin), reads=[r_b], writes=[k.rope_b])
        P.op("act", lambda e, b=b: e.activation(out=rv[:, b, 0, :, :], in_=rc, func=AF.Sin), reads=[rc_b], writes=[k.rope_b])

    for b in range(NSEQ):
        one(b)
    k.dump("rope", k.rope, k.rope_b)


def pass_mix(k, l, b, src, dst):
    if l == 0:
        return pass_mlstm(k, b, src, dst)
    if l in (1, 3):
        pass_attn_a(k, l, b, src)
        k.P.barrier()
        k.arena.reset()
        return pass_attn_b(k, l, b, src, dst)
    if l == 2:
        return pass_hgrn(k, b, src, dst)
    raise NotImplementedError


def bcast_row(k, name, row_ap, n, reads=()):
    t = k.arena.alloc(name, [n], F32)
    k.P.dma("sp", k.ld_sem(), t[0], row_ap.partition_broadcast(128), reads=list(reads), writes=[t[1]])
    return t


def out_proj_epi(k, ep, b, tt, ypre, ypre_b, ypT, ypT_b, wout, wout_b, x_ap, x_buf, dst, tpbank, ybanks):
    P = k.P
    bk, bk_b = k.bank[tpbank]
    bk16 = bk.bitcast(BF16)
    for kc in range(KC):
        P.op("pe", lambda e, kc=kc: e.transpose(out=bk16[:, kc * 128:(kc + 1) * 128], in_=ypre[:, kc * 128:(kc + 1) * 128],
                                                identity=k.idb), reads=[ypre_b, k.idb_b], writes=[bk_b])
    P.op("act", lambda e: e.activation(out=ypT, in_=bk16[:, 0:1024], func=AF.Copy), reads=[bk_b], writes=[ypT_b])
    for h in range(2):
        yb, yb_b = k.bank[ybanks[h]]
        for kc in range(KC):
            P.op("pe", lambda e, yb=yb, kc=kc, h=h: e.matmul(yb[:, :], ypT[:, kc * 128:(kc + 1) * 128],
                                                            wout[:, kc, h * 512:(h + 1) * 512],
                                                            start=(kc == 0), stop=(kc == KC - 1)),
                 reads=[ypT_b, wout_b], writes=[yb_b])
    epilogue(k, ep, b, tt, ybanks, x_ap, x_buf, dst)


def pass_mlstm(k, b, src, dst):
    P, A = k.P, k.arena
    sl = 0
    NIN = 3080
    xs = [[A.alloc("x", [1024], F32) for t in range(4)] for i in range(2)]
    hT, _ = A.alloc("hT", [KC, 512], BF16)
    hT_bufs = [Buf("hT%d" % i) for i in range(KC)]
    win, win_b = A.alloc("win", [KC, NIN], BF16)
    wout, wout_b = A.alloc("wout", [KC, 1024], BF16)
    ep = alloc_epi(k, sl, b)
    bg = bcast_row(k, "bgate", k.mlstm_b_gate[0, :], 8)
    nw = bcast_row(k, "normw", k.mlstm_norm[0, :], 1024)
    P.dma("sp", k.ld_sem(), win, k.wb["mlstm_w_in"].rearrange("(kc p) n -> p kc n", p=128),
          reads=[k.wb_buf["mlstm_w_in"]], writes=[win_b])
    P.dma("sp", k.ld_sem(), wout, k.wb["mlstm_w_out"].rearrange("(kc p) n -> p kc n", p=128),
          reads=[k.wb_buf["mlstm_w_out"]], writes=[wout_b])
    gs = A.alloc("gs", [48], F32, nbuf=2)
    qp = A.alloc("qp", [4, 128], BF16, nbuf=2)
    kp = A.alloc("kp", [4, 128], BF16, nbuf=2)
    kpp = A.alloc("kpp", [4, 128], BF16, nbuf=2)
    va = A.alloc("va", [4, 258], BF16, nbuf=2)
    sgo = A.alloc("sgo", [1024], F32, nbuf=2)
    qkT = A.alloc("qkT", [8, 128], BF16, nbuf=2)
    aT = A.alloc("aT", [4, 128], BF16, nbuf=2)
    C32, C32_b = A.alloc("C32", [4, 258], F32)
    Cb = A.alloc("Cb", [4, 258], BF16, nbuf=2)
    ypre = A.alloc("ypre", [1024], BF16, nbuf=1)
    ypT = A.alloc("ypT", [1024], BF16, nbuf=1)
    junk = A.alloc("junk", [256], F32, nbuf=2)
    ytmp = A.alloc("ytmp", [256], F32, nbuf=2)
    hs = A.alloc("hs", [16], F32, nbuf=4)
    P.op("pool", lambda e: e.memset(C32, 0.0), writes=[C32_b])
    for i in range(2):
        P.op("pool", lambda e, i=i: e.memset(Cb[i][0], 0.0), writes=[Cb[i][1]])
        P.op("pool", lambda e, i=i: e.memset(va[i][0][:, :, 256:258], 1.0), writes=[va[i][1]])
    caus32 = k.cst[:, C_CAUS:C_CAUS + 128]
    ones32 = k.cst[:, C_ONES:C_ONES + 128]
    SC = 128.0 ** -0.5
    cnt_h = [0]

    def stage_P(st, t, ci, xcur):
        tok = slice(t * 128, (t + 1) * 128)
        g, g_b = gs[ci % 2]
        qpa, qp_b = qp[ci % 2]
        kpa, kp_b = kp[ci % 2]
        kppa, kpp_b = kpp[ci % 2]
        vaa, va_b = va[ci % 2]
        sgoa, sgo_b = sgo[ci % 2]
        qkTa, qkT_b = qkT[ci % 2]
        aTa, aT_b = aT[ci % 2]
        Cold, Cold_b = Cb[ci % 2]
        Cnew, Cnew_b = Cb[(ci + 1) % 2]
        ypa, yp_b = ypre[0]
        ypTa, ypT_b = ypT[0]
        b4, b4_b = k.bank[4]

        def proj(bank, c0, n):
            bk, bk_b = k.bank[bank]
            for kc in range(KC):
                P.op("pe", lambda e, bk=bk, kc=kc, c0=c0, n=n, tok=tok: e.matmul(bk[:, 0:n], hT[:, kc, tok], win[:, kc, c0:c0 + n],
                                                                      start=(kc == 0), stop=(kc == KC - 1)),
                     reads=[hT_bufs[kc], win_b], writes=[bk_b])
            return bk, bk_b

        proj(4, 3072, 8)
        P.op("dve", lambda e, g=g: e.tensor_tensor(out=g[:, 0:8], in0=b4[:, 0:8], in1=bg[0], op=ALU.add),
             reads=[b4_b, bg[1]], writes=[g_b])
        P.op("act", lambda e, g=g: e.activation(out=g[:, 8:12], in_=g[:, 4:8], func=AF.Exp, scale=-1.0),
             reads=[g_b], writes=[g_b])
        P.op("act", lambda e, g=g: e.activation(out=g[:, 8:12], in_=g[:, 8:12], func=AF.Ln, bias=1.0),
             reads=[g_b], writes=[g_b])
        for c in range(2):
            bk, bk_b = proj(2 + c, 1024 + c * 512, 512)
            P.op("act", lambda e, bk=bk, c=c, vaa=vaa: e.activation(
                out=vaa[:, 2 * c:2 * c + 2, 0:256], in_=bk[:, :].rearrange("p (h d) -> p h d", h=2), func=AF.Copy),
                reads=[bk_b], writes=[va_b])
        for c in range(2):
            bk, bk_b = proj(2 + c, 2048 + c * 512, 512)
            P.op("act", lambda e, bk=bk, c=c, sgoa=sgoa: e.activation(out=sgoa[:, c * 512:(c + 1) * 512], in_=bk[:, :],
                                                                     func=AF.Sigmoid),
                 reads=[bk_b], writes=[sgo_b])
        P.op("pe", lambda e, g=g: e.matmul(b4[:, 8:12], caus32, g[:, 8:12], start=True, stop=True),
             reads=[g_b, k.cst_b], writes=[b4_b])
        P.op("pe", lambda e, g=g: e.matmul(b4[:, 12:16], ones32, g[:, 8:12], start=True, stop=True),
             reads=[g_b, k.cst_b], writes=[b4_b])
        P.op("dve", lambda e, g=g: e.tensor_tensor(out=g[:, 16:20], in0=b4[:, 8:12], in1=g[:, 0:4], op=ALU.add),
             reads=[b4_b, g_b], writes=[g_b])
        P.op("dve", lambda e, g=g: e.tensor_tensor(out=g[:, 20:24], in0=g[:, 16:20], in1=b4[:, 12:16], op=ALU.subtract),
             reads=[b4_b, g_b], writes=[g_b])
        P.op("act", lambda e, g=g: e.activation(out=g[:, 12:16], in_=b4[:, 8:12], func=AF.Exp, scale=-1.0),
             reads=[b4_b, g_b], writes=[g_b])
        P.op("act", lambda e, g=g: e.activation(out=g[:, 24:28], in_=b4[:, 12:16], func=AF.Exp, scale=-1.0),
             reads=[b4_b, g_b], writes=[g_b])
        P.op("act", lambda e, g=g: e.activation(out=g[:, 16:24], in_=g[:, 16:24], func=AF.Exp),
             reads=[g_b], writes=[g_b])
        bk, bk_b = proj(2, 0, 512)
        P.op("dve", lambda e, bk=bk, g=g, qpa=qpa: e.tensor_tensor(
            out=qpa, in0=bk[:, :].rearrange("p (h d) -> p h d", h=4),
            in1=g[:, 12:16].unsqueeze(2).to_broadcast([128, 4, 128]), op=ALU.mult),
            reads=[bk_b, g_b], writes=[qp_b])
        bk, bk_b = proj(3, 512, 512)
        P.op("dve", lambda e, bk=bk, g=g, kpa=kpa: e.scalar_tensor_tensor(
            out=kpa, in0=bk[:, :].rearrange("p (h d) -> p h d", h=4), scalar=SC,
            in1=g[:, 16:20].unsqueeze(2).to_broadcast([128, 4, 128]), op0=ALU.mult, op1=ALU.mult),
            reads=[bk_b, g_b], writes=[kp_b])
        P.op("dve", lambda e, bk=bk, g=g, kppa=kppa: e.scalar_tensor_tensor(
            out=kppa, in0=bk[:, :].rearrange("p (h d) -> p h d", h=4), scalar=SC,
            in1=g[:, 20:24].unsqueeze(2).to_broadcast([128, 4, 128]), op0=ALU.mult, op1=ALU.mult),
            reads=[bk_b, g_b], writes=[kpp_b])
        tb, tb_b = k.bank[0]
        tb16 = tb.bitcast(BF16)
        for h in range(4):
            P.op("pe", lambda e, h=h, qpa=qpa: e.transpose(out=tb16[:, h * 128:(h + 1) * 128], in_=qpa[:, h, :],
                                                          identity=k.idb), reads=[qp_b, k.idb_b], writes=[tb_b])
        for h in range(4):
            P.op("pe", lambda e, h=h, kpa=kpa: e.transpose(out=tb16[:, 512 + h * 128:512 + (h + 1) * 128],
                                                          in_=kpa[:, h, :], identity=k.idb),
                 reads=[kp_b, k.idb_b], writes=[tb_b])
        P.op("act", lambda e, qkTa=qkTa: e.activation(out=qkTa, in_=tb16[:, 0:1024].rearrange("p (h d) -> p h d", h=8),
                                                    func=AF.Copy), reads=[tb_b], writes=[qkT_b])
        sb_, sb_b = k.bank[5]
        for h in range(4):
            P.op("pe", lambda e, h=h, qkTa=qkTa: e.matmul(sb_[:, h * 128:(h + 1) * 128], qkTa[:, 4 + h, :], qkTa[:, h, :],
                                                         start=True, stop=True), reads=[qkT_b], writes=[sb_b])
        P.op("dve", lambda e, aTa=aTa: e.tensor_tensor(
            out=aTa, in0=sb_[:, :].rearrange("p (h d) -> p h d", h=4),
            in1=caus32.unsqueeze(1).to_broadcast([128, 4, 128]), op=ALU.mult),
            reads=[sb_b, k.cst_b], writes=[aT_b])
        return locals()

    def stage_R(L):
        (st, t, xcur, g, g_b, kppa, kpp_b, vaa, va_b, sgoa, sgo_b, qkTa, qkT_b, aTa, aT_b, Cold, Cold_b, Cnew, Cnew_b,
         ypa, yp_b, ypTa, ypT_b) = (L[n] for n in (
            "st", "t", "xcur", "g", "g_b", "kppa", "kpp_b", "vaa", "va_b", "sgoa", "sgo_b", "qkTa", "qkT_b", "aTa", "aT_b",
            "Cold", "Cold_b", "Cnew", "Cnew_b", "ypa", "yp_b", "ypTa", "ypT_b"))
        for h in range(4):
            ab, ab_b = k.bank[6 + h % 2]
            P.op("pe", lambda e, h=h, ab=ab, aTa=aTa, vaa=vaa: e.matmul(ab[:, 0:257], aTa[:, h, :], vaa[:, h, 0:257],
                                                                       start=True, stop=False),
                 reads=[aT_b, va_b], writes=[ab_b])
            P.op("pe", lambda e, h=h, ab=ab, qkTa=qkTa, Cold=Cold: e.matmul(ab[:, 0:257], qkTa[:, h, :], Cold[:, h, 0:257],
                                                                         start=False, stop=True),
                 reads=[qkT_b, Cold_b], writes=[ab_b])
            s_, s_b = hs[cnt_h[0] % 4]
            jk, jk_b = junk[cnt_h[0] % 2]
            yt, yt_b = ytmp[cnt_h[0] % 2]
            cnt_h[0] += 1
            P.op("act", lambda e, ab=ab, s_=s_: e.activation(out=s_[:, 0:1], in_=ab[:, 256:257], func=AF.Abs),
                 reads=[ab_b], writes=[s_b])
            P.op("dve", lambda e, s_=s_: e.tensor_scalar(out=s_[:, 0:1], in0=s_[:, 0:1], scalar1=1.0, scalar2=None,
                                                        op0=ALU.max), reads=[s_b], writes=[s_b])
            P.op("dve", lambda e, s_=s_: e.reciprocal(out=s_[:, 1:2], in_=s_[:, 0:1]), reads=[s_b], writes=[s_b])
            P.op("act", lambda e, ab=ab, s_=s_, jk=jk: e.activation(out=jk, in_=ab[:, 0:256], func=AF.Square,
                                                                   accum_out=s_[:, 2:3]),
                 reads=[ab_b], writes=[jk_b, s_b])
            P.op("dve", lambda e, s_=s_: e.scalar_tensor_tensor(out=s_[:, 3:4], in0=s_[:, 2:3], scalar=s_[:, 1:2],
                                                               in1=s_[:, 1:2], op0=ALU.mult, op1=ALU.mult),
                 reads=[s_b], writes=[s_b])
            P.op("dve", lambda e, s_=s_: e.tensor_scalar(out=s_[:, 4:5], in0=s_[:, 3:4], scalar1=1.0 / 256.0, scalar2=1e-6,
                                                        op0=ALU.mult, op1=ALU.add), reads=[s_b], writes=[s_b])
            P.op("pool", lambda e, s_=s_: e.tensor_tensor(out=s_[:, 5:6], in0=s_[:, 4:5], in1=ep.nh[0], op=ALU.pow),
                 reads=[s_b, ep.nh[1]], writes=[s_b])
            P.op("dve", lambda e, s_=s_: e.tensor_tensor(out=s_[:, 6:7], in0=s_[:, 5:6], in1=s_[:, 1:2], op=ALU.mult),
                 reads=[s_b], writes=[s_b])
            P.op("dve", lambda e, ab=ab, s_=s_, yt=yt, h=h: e.scalar_tensor_tensor(
                out=yt, in0=ab[:, 0:256], scalar=s_[:, 6:7], in1=nw[0][:, h * 256:(h + 1) * 256],
                op0=ALU.mult, op1=ALU.mult), reads=[ab_b, s_b, nw[1]], writes=[yt_b])
            P.op("pool", lambda e, yt=yt, h=h, ypa=ypa, sgoa=sgoa: e.tensor_tensor(
                out=ypa[:, h * 256:(h + 1) * 256], in0=yt, in1=sgoa[:, h * 256:(h + 1) * 256], op=ALU.mult),
                reads=[yt_b, sgo_b], writes=[yp_b])
        for h in range(4):
            cb_, cb_b = k.bank[2 + h % 2]
            P.op("pe", lambda e, h=h, cb_=cb_, kppa=kppa, vaa=vaa: e.matmul(cb_[:, 0:257], kppa[:, h, :], vaa[:, h, 0:257],
                                                                           start=True, stop=True),
                 reads=[kpp_b, va_b], writes=[cb_b])
            P.op("dve", lambda e, h=h, cb_=cb_, g=g: e.scalar_tensor_tensor(
                out=C32[:, h, 0:257], in0=C32[:, h, 0:257], scalar=g[:, 24 + h:25 + h], in1=cb_[:, 0:257],
                op0=ALU.mult, op1=ALU.add), reads=[cb_b, g_b, C32_b], writes=[C32_b])
        P.op("act", lambda e, Cnew=Cnew: e.activation(out=Cnew, in_=C32, func=AF.Copy), reads=[C32_b], writes=[Cnew_b])
        out_proj_epi(k, ep, b, st * 4 + t, ypa, yp_b, ypTa, ypT_b, wout, wout_b, xcur[t][0], xcur[t][1], dst, 1, (2, 3))

    seq = [(st, t) for st in range(k.NST) for t in range(4)]
    load_x(k, b, 0, src, xs[0])
    if k.NST > 1:
        load_x(k, b, 1, src, xs[1])
    prologue(k, b, sl, xs[0], hT, hT_bufs)
    ctx = stage_P(0, 0, 0, xs[0])
    for i, (st, t) in enumerate(seq):
        nxt = None
        if i + 1 < len(seq):
            st2, t2 = seq[i + 1]
            if t2 == 0:
                prologue(k, b, sl, xs[st2 % 2], hT, hT_bufs)
            nxt = stage_P(st2, t2, i + 1, xs[st2 % 2])
        stage_R(ctx)
        if t == 3 and st + 2 < k.NST:
            load_x(k, b, st + 2, src, xs[st % 2])
        ctx = nxt


def make_in_map(inp, xb, cb, posb):
    m = {"x": np.ascontiguousarray(xb, dtype=np.float32), "c": np.ascontiguousarray(cb, dtype=np.float32),
         "positions": np.ascontiguousarray(posb, dtype=np.int32), "consts": make_consts()}
    for name in ("ada_w", "ada_b", "ln_g", "ln_b", "mlstm_b_gate", "mlstm_norm", "swa_sinks", "hgrn_lower_bounds",
                 "hgrn_norm", "diff_lambda", "diff_norm", "ffn_w_in", "ffn_w_out"):
        m[name] = np.ascontiguousarray(inp[name], dtype=np.float32)
    for name, r, c_ in W_SPECS:
        m[name] = np.ascontiguousarray(inp[name], dtype=np.float32)
    return m


DBG_STOP = 0


def attn_cfg(l):
    c = K()
    if l == 1:
        c.NIN, c.wi, c.wo = 1280, "swa_w_in", "swa_w_out"
        c.NKP, c.NVH, c.DV = 1, 2, 64
        c.qscale = 0.125
    else:
        c.NIN, c.wi, c.wo = 3072, "diff_w_in", "diff_w_out"
        c.NKP, c.NVH, c.DV = 8, 8, 128
        c.qscale = 0.125
    c.VW = c.NVH * (c.DV + 2)
    return c


def pass_attn_a(k, l, b, src):
    P, A, NSEQ, NT = k.P, k.arena, k.NSEQ, k.NT
    cf = attn_cfg(l)
    sl = 2 * l
    xs = [[A.alloc("x", [1024], F32) for t in range(4)] for i in range(2)]
    hT, _ = A.alloc("hT", [KC, 512], BF16)
    hT_bufs = [Buf("hT%d" % i) for i in range(KC)]
    win, win_b = A.alloc("win", [KC, cf.NIN], BF16)
    P.dma("sp", k.ld_sem(), win, k.wb[cf.wi].rearrange("(kc p) n -> p kc n", p=128), reads=[k.wb_buf[cf.wi]], writes=[win_b])
    qr = A.alloc("qr", [1024], BF16, nbuf=2)
    kr = A.alloc("kr", [128 * cf.NKP], BF16, nbuf=2)
    QTst, QTst_b = A.alloc("QTst", [8, 512], BF16)
    KTst, KTst_b = A.alloc("KTst", [cf.NKP, 512], BF16)
    Vst = A.alloc("Vst", [cf.NVH, cf.DV + 2], BF16, nbuf=2)
    rt = A.alloc("rt", [4, 64], F32, nbuf=2)
    for i in range(2):
        P.op("pool", lambda e, i=i: e.memset(Vst[i][0][:, :, cf.DV:cf.DV + 2], 1.0), writes=[Vst[i][1]])
    rv = k.rope.rearrange("p (b w n j) -> p b w n j", b=NSEQ, w=2, n=NT, j=8)
    qt_v = k.qt_scr.rearrange("n p s -> p n s")
    kt_v = k.kt_scr.rearrange("n p s -> p n s")
    load_x(k, b, 0, src, xs[0])
    ci = 0
    nb = 0
    nr = 0
    for st in range(k.NST):
        if st + 1 < k.NST:
            load_x(k, b, st + 1, src, xs[(st + 1) % 2])
        xcur = xs[st % 2]
        prologue(k, b, sl, xcur, hT, hT_bufs)
        for t in range(4):
            tt = st * 4 + t
            tok = slice(t * 128, (t + 1) * 128)
            qra, qr_b = qr[ci % 2]
            kra, kr_b = kr[ci % 2]
            Va, V_b = Vst[ci % 2]
            ci += 1
            cosb = rv[:, b, 0, tt, :].unsqueeze(1).to_broadcast([128, 8, 8])
            sinb = rv[:, b, 1, tt, :].unsqueeze(1).to_broadcast([128, 8, 8])

            def proj(c0, n):
                nonlocal nb
                bk, bk_b = k.bank[2 + nb % 4]
                nb += 1
                for kc in range(KC):
                    P.op("pe", lambda e, bk=bk, kc=kc, c0=c0, n=n, tok=tok: e.matmul(
                        bk[:, 0:n], hT[:, kc, tok], win[:, kc, c0:c0 + n], start=(kc == 0), stop=(kc == KC - 1)),
                        reads=[hT_bufs[kc], win_b], writes=[bk_b])
                return bk, bk_b

            def rope_evac(src3, src_b, dst3, dst_b, nh, scale):
                nonlocal nr
                r4, r4_b = rt[nr % 2]
                nr += 1
                cb_ = cosb[:, 0:nh, :]
                sb_ = sinb[:, 0:nh, :]
                tv = [r4[:, i, 0:nh * 8].rearrange("p (h j) -> p h j", j=8) for i in range(4)]
                x1, x2 = src3[:, :, 0:8], src3[:, :, 8:16]
                for i, (xa, tb_) in enumerate(((x1, cb_), (x2, sb_), (x2, cb_), (x1, sb_))):
                    P.op("dve", lambda e, i=i, xa=xa, tb_=tb_: e.scalar_tensor_tensor(
                        out=tv[i], in0=xa, scalar=scale, in1=tb_, op0=ALU.mult, op1=ALU.mult),
                        reads=[src_b, k.rope_b], writes=[r4_b])
                P.op("dve", lambda e: e.tensor_tensor(out=dst3[:, :, 0:8], in0=tv[0], in1=tv[1], op=ALU.subtract),
                     reads=[r4_b], writes=[dst_b])
                P.op("dve", lambda e: e.tensor_tensor(out=dst3[:, :, 8:16], in0=tv[2], in1=tv[3], op=ALU.add),
                     reads=[r4_b], writes=[dst_b])
                P.op("act", lambda e: e.activation(out=dst3[:, :, 16:64], in_=src3[:, :, 16:64], func=AF.Copy, scale=scale),
                     reads=[src_b], writes=[dst_b])

            for c in range(2):
                bk, bk_b = proj(c * 512, 512)
                s3 = bk[:, :].rearrange("p (h d) -> p h d", d=64)
                if l == 1:
                    d3 = qra.rearrange("p (m c d) -> p m c d", c=2, d=64)[:, :, c, :]
                else:
                    d3 = qra[:, c * 512:(c + 1) * 512].rearrange("p (h d) -> p h d", d=64)
                rope_evac(s3, bk_b, d3, qr_b, 8, cf.qscale)
            if l == 1:
                bk, bk_b = proj(1024, 256)
                rope_evac(bk[:, 0:128].rearrange("p (h d) -> p h d", d=64), bk_b,
                          kra.rearrange("p (h d) -> p h d", d=64), kr_b, 2, 1.0)
                P.op("act", lambda e, bk=bk, Va=Va: e.activation(out=Va[:, :, 0:64],
                                                                in_=bk[:, 128:256].rearrange("p (h d) -> p h d", d=64),
                                                                func=AF.Copy), reads=[bk_b], writes=[V_b])
            else:
                for c in range(2):
                    bk, bk_b = proj(1024 + c * 512, 512)
                    rope_evac(bk[:, :].rearrange("p (h d) -> p h d", d=64), bk_b,
                              kra[:, c * 512:(c + 1) * 512].rearrange("p (h d) -> p h d", d=64), kr_b, 8, 1.0)
                for c in range(2):
                    bk, bk_b = proj(2048 + c * 512, 512)
                    P.op("act", lambda e, bk=bk, Va=Va, c=c: e.activation(
                        out=Va[:, 4 * c:4 * c + 4, 0:128], in_=bk[:, :].rearrange("p (h d) -> p h d", d=128), func=AF.Copy),
                        reads=[bk_b], writes=[V_b])
            for (sa, sa_b, npair, bank, dstT, dstT_b) in ((qra, qr_b, 8, 0, QTst, QTst_b), (kra, kr_b, cf.NKP, 1, KTst, KTst_b)):
                tb, tb_b = k.bank[bank]
                tb16 = tb.bitcast(BF16)
                for p_ in range(npair):
                    P.op("pe", lambda e, tb16=tb16, p_=p_, sa=sa: e.transpose(out=tb16[:, p_ * 128:(p_ + 1) * 128],
                                                                             in_=sa[:, p_ * 128:(p_ + 1) * 128], identity=k.idb),
                         reads=[sa_b, k.idb_b], writes=[tb_b])
                P.op("act", lambda e, tb16=tb16, npair=npair, dstT=dstT, tok=tok: e.activation(
                    out=dstT[:, :, tok], in_=tb16[:, 0:npair * 128].rearrange("p (n t) -> p n t", t=128), func=AF.Copy),
                    reads=[tb_b], writes=[dstT_b])
            P.dma("pool", k.st_sem(), k.v_scr[tt * 128:(tt + 1) * 128, 0:cf.VW], Va.rearrange("p h d -> p (h d)"),
                  reads=[V_b], writes=[k.v_bufs[tt]])
        P.dma("pool", k.st_sem(), qt_v[:, :, st * 512:(st + 1) * 512], QTst, reads=[QTst_b], writes=[k.qt_bufs[st]])
        P.dma("pool", k.st_sem(), kt_v[:, 0:cf.NKP, st * 512:(st + 1) * 512], KTst, reads=[KTst_b], writes=[k.kt_bufs[st]])


def pass_attn_b(k, l, b, src, dst):
    P, A, NSEQ, NT = k.P, k.arena, k.NSEQ, k.NT
    cf = attn_cfg(l)
    sl = 2 * l
    DV = cf.DV
    wout, wout_b = A.alloc("wout", [KC, 1024], BF16)
    P.dma("sp", k.ld_sem(), wout, k.wb[cf.wo].rearrange("(kc p) n -> p kc n", p=128), reads=[k.wb_buf[cf.wo]], writes=[wout_b])
    nb_ = 1 if l == 3 else 2
    ep = alloc_epi(k, sl, b, nbuf=nb_)
    KTc, _ = A.alloc("KTc", [cf.NKP, k.S], BF16)
    Vc, _ = A.alloc("Vc", [NT, cf.VW], BF16)
    KT_bufs = [Buf("KTc%d" % i) for i in range(NT)]
    V_bufs = [Buf("Vc%d" % i) for i in range(NT)]
    xr = A.alloc("xr", [1024], F32, nbuf=nb_)
    QTb = A.alloc("QTz", [8, 2, 128], BF16, nbuf=nb_)
    for i in range(nb_):
        P.op("pool", lambda e, i=i: e.memset(QTb[i][0], 0.0), writes=[QTb[i][1]])
    PT = A.alloc("PT", [512], BF16, nbuf=3)
    ypre, yp_b = A.alloc("ypre", [1024], BF16)
    ypT, ypT_b = A.alloc("ypT", [1024], BF16)
    sm = A.alloc("sm", [16], F32, nbuf=4)
    ot = A.alloc("ot", [128], F32, nbuf=2)
    jk = A.alloc("jk", [128], F32, nbuf=2)
    cst_t, cst_tb = A.alloc("acst", [160], F32)
    if l == 1:
        P.dma("sp", k.ld_sem(), cst_t[:, 0:16], k.swa_sinks[0, :].partition_broadcast(128), writes=[cst_tb])
        P.op("act", lambda e: e.activation(out=cst_t[:, 0:16], in_=cst_t[:, 0:16], func=AF.Exp), reads=[cst_tb], writes=[cst_tb])
    else:
        lam_init = 0.8 - 0.6 * math.exp(-0.3 * l)
        lv, lv_b = A.alloc("lv", [256], F32)
        P.dma("sp", k.ld_sem(), lv, k.diff_lambda[0].rearrange("a d -> (a d)").partition_broadcast(128), writes=[lv_b])
        P.dma("sp", k.ld_sem(), cst_t[:, 0:128], k.diff_norm[0, :].partition_broadcast(128), writes=[cst_tb])
        P.op("dve", lambda e: e.tensor_scalar(out=cst_t[:, 0:128], in0=cst_t[:, 0:128], scalar1=1.0 - lam_init, scalar2=None,
                                              op0=ALU.mult), reads=[cst_tb], writes=[cst_tb])
        for i in range(2):
            P.op("dve", lambda e, i=i: e.tensor_tensor(out=lv[:, i * 128:i * 128 + 64], in0=lv[:, i * 128:i * 128 + 64],
                                                      in1=lv[:, i * 128 + 64:i * 128 + 128], op=ALU.mult),
                 reads=[lv_b], writes=[lv_b])
            P.op("dve", lambda e, i=i: e.tensor_reduce(out=cst_t[:, 128 + i:129 + i], in_=lv[:, i * 128:i * 128 + 64],
                                                      axis=AX.X, op=ALU.add), reads=[lv_b, cst_tb], writes=[cst_tb])
        P.op("act", lambda e: e.activation(out=cst_t[:, 128:130], in_=cst_t[:, 128:130], func=AF.Exp), reads=[cst_tb], writes=[cst_tb])
        P.op("dve", lambda e: e.tensor_tensor(out=cst_t[:, 130:131], in0=cst_t[:, 129:130], in1=cst_t[:, 128:129], op=ALU.subtract),
             reads=[cst_tb], writes=[cst_tb])
        P.op("dve", lambda e: e.tensor_scalar(out=cst_t[:, 130:131], in0=cst_t[:, 130:131], scalar1=-lam_init, scalar2=None,
                                              op0=ALU.add), reads=[cst_tb], writes=[cst_tb])
    qt_v = k.qt_scr.rearrange("n p s -> p n s")
    kt_v = k.kt_scr.rearrange("n p s -> p n s")
    Vc4 = Vc.rearrange("p n (h d) -> p n h d", d=DV + 2)
    if DBG_STOP == 1:
        return
    steps = []
    for qb in range(NT):
        if l == 1:
            kbs = ([(qb - 1, k.strict_b16, k.strict_b16_b)] if qb > 0 else []) + [(qb, k.caus_b16, k.caus_b16_b)]
        else:
            kbs = [(i, None, None) for i in range(qb)] + [(qb, k.caus_b16, k.caus_b16_b)]
        for grp in range(4):
            for ki, (kb, msk, msk_b) in enumerate(kbs):
                steps.append(dict(qb=qb, grp=grp, ki=ki, kb=kb, msk=msk, msk_b=msk_b, nk=len(kbs)))
    cnt = {"nS": 0, "nsm": 0}

    def maps_of(grp):
        return [(2 * grp + pp, c) for pp in range(2) for c in range(2)]

    def do_S(s_):
        qb, grp, ki, kb = s_["qb"], s_["grp"], s_["ki"], s_["kb"]
        tok = slice(qb * 128, (qb + 1) * 128)
        QTa, QT_b = QTb[qb % nb_]
        if grp == 0 and ki == 0:
            for c in range(2):
                P.dma("sp", k.ld_sem(), QTa[c * 64:(c + 1) * 64, :, c, :], qt_v[c * 64:(c + 1) * 64, :, tok],
                      reads=[k.qt_bufs[qb // 4]], writes=[QT_b])
            P.dma("sp", k.ld_sem(), KTc[:, :, tok], kt_v[:, 0:cf.NKP, tok], reads=[k.kt_bufs[qb // 4]], writes=[KT_bufs[qb]])
            P.dma("sp", k.ld_sem(), Vc[:, qb, :], k.v_scr[tok, 0:cf.VW], reads=[k.v_bufs[qb]], writes=[V_bufs[qb]])
        ktok = slice(kb * 128, (kb + 1) * 128)
        sbk, sbk_b = k.bank[(0, 1, 7)[cnt["nS"] % 3]]
        pt, pt_b = PT[cnt["nS"] % 3]
        cnt["nS"] += 1
        s_["pt"], s_["pt_b"] = pt, pt_b
        if l == 1:
            P.op("pe", lambda e, sbk=sbk, grp=grp, ktok=ktok, QTa=QTa: e.matmul(
                sbk[:, 0:512], KTc[:, 0, ktok], QTa[:, 2 * grp:2 * grp + 2, :, :].rearrange("p a c q -> p (a c q)"),
                start=True, stop=True), reads=[KT_bufs[kb], QT_b], writes=[sbk_b])
        else:
            for pp in range(2):
                p_ = 2 * grp + pp
                P.op("pe", lambda e, sbk=sbk, pp=pp, p_=p_, ktok=ktok, QTa=QTa: e.matmul(
                    sbk[:, pp * 256:(pp + 1) * 256], KTc[:, p_, ktok], QTa[:, p_, :, :].rearrange("p c q -> p (c q)"),
                    start=True, stop=True), reads=[KT_bufs[kb], QT_b], writes=[sbk_b])
        P.op("act", lambda e, sbk=sbk, pt=pt: e.activation(out=pt, in_=sbk[:, :], func=AF.Exp), reads=[sbk_b], writes=[pt_b])
        msk, msk_b = s_["msk"], s_["msk_b"]
        if msk is not None:
            P.op("pool" if l == 1 else "dve", lambda e, pt=pt, msk=msk: e.tensor_tensor(
                out=pt.rearrange("p (m q) -> p m q", q=128), in0=pt.rearrange("p (m q) -> p m q", q=128),
                in1=msk.unsqueeze(1).to_broadcast([128, 4, 128]), op=ALU.mult), reads=[pt_b, msk_b], writes=[pt_b])

    def do_PV(s_):
        qb, grp, ki, kb, nk = s_["qb"], s_["grp"], s_["ki"], s_["kb"], s_["nk"]
        tok = slice(qb * 128, (qb + 1) * 128)
        xra, xr_b = xr[qb % nb_]
        pt, pt_b = s_["pt"], s_["pt_b"]
        maps = maps_of(grp)
        if grp == 0 and ki == 0:
            rd = [k.act_buf[b][qb]] if src is k.act else []
            P.dma("sp", k.ld_sem(), xra, src[b, tok, :], reads=rd, writes=[xr_b])
        for mi, (p_, c) in enumerate(maps):
            vh = c if l == 1 else p_
            ab, ab_b = k.bank[2 + mi]
            P.op("pe", lambda e, ab=ab, mi=mi, pt=pt, kb=kb, vh=vh, ki=ki, nk=nk: e.matmul(
                ab[:, 0:DV + 1], pt[:, mi * 128:(mi + 1) * 128], Vc4[:, kb, vh, 0:DV + 1],
                start=(ki == 0), stop=(ki == nk - 1)), reads=[pt_b, V_bufs[kb]], writes=[ab_b])
        if ki != nk - 1:
            return
        if l == 1:
            for mi, (p_, c) in enumerate(maps):
                m = p_ + 8 * c
                ab, ab_b = k.bank[2 + mi]
                s1, s1_b = sm[cnt["nsm"] % 4]
                cnt["nsm"] += 1
                P.op("dve", lambda e, ab=ab, s1=s1, m=m: e.tensor_tensor(out=s1[:, 0:1], in0=ab[:, DV:DV + 1], in1=cst_t[:, m:m + 1],
                                                                      op=ALU.add), reads=[ab_b, cst_tb], writes=[s1_b])
                P.op("dve", lambda e, s1=s1: e.reciprocal(out=s1[:, 1:2], in_=s1[:, 0:1]), reads=[s1_b], writes=[s1_b])
                P.op("act", lambda e, ab=ab, s1=s1, m=m: e.activation(out=ypre[:, m * 64:(m + 1) * 64], in_=ab[:, 0:DV],
                                                                   func=AF.Copy, scale=s1[:, 1:2]),
                     reads=[ab_b, s1_b], writes=[yp_b])
        else:
            for pp in range(2):
                p_ = 2 * grp + pp
                a1, a1_b = k.bank[2 + 2 * pp]
                a2, a2_b = k.bank[3 + 2 * pp]
                s1, s1_b = sm[cnt["nsm"] % 4]
                o1, o1_b = ot[cnt["nsm"] % 2]
                j_, j_b = jk[cnt["nsm"] % 2]
                cnt["nsm"] += 1
                P.op("dve", lambda e, a1=a1, s1=s1: e.reciprocal(out=s1[:, 0:1], in_=a1[:, DV:DV + 1]), reads=[a1_b], writes=[s1_b])
                P.op("dve", lambda e, a2=a2, s1=s1: e.reciprocal(out=s1[:, 1:2], in_=a2[:, DV:DV + 1]), reads=[a2_b], writes=[s1_b])
                P.op("dve", lambda e, s1=s1: e.tensor_tensor(out=s1[:, 2:3], in0=s1[:, 1:2], in1=cst_t[:, 130:131], op=ALU.mult),
                     reads=[s1_b, cst_tb], writes=[s1_b])
                P.op("dve", lambda e, a1=a1, s1=s1, o1=o1: e.tensor_scalar(out=o1, in0=a1[:, 0:DV], scalar1=s1[:, 0:1], scalar2=None,
                                                                        op0=ALU.mult), reads=[a1_b, s1_b], writes=[o1_b])
                P.op("dve", lambda e, a2=a2, s1=s1, o1=o1: e.scalar_tensor_tensor(out=o1, in0=a2[:, 0:DV], scalar=s1[:, 2:3], in1=o1,
                                                                               op0=ALU.mult, op1=ALU.add),
                     reads=[a2_b, s1_b, o1_b], writes=[o1_b])
                P.op("act", lambda e, o1=o1, s1=s1, j_=j_: e.activation(out=j_, in_=o1, func=AF.Square, accum_out=s1[:, 3:4]),
                     reads=[o1_b], writes=[j_b, s1_b])
                P.op("dve", lambda e, s1=s1: e.tensor_scalar(out=s1[:, 4:5], in0=s1[:, 3:4], scalar1=1.0 / 128.0, scalar2=1e-6,
                                                            op0=ALU.mult, op1=ALU.add), reads=[s1_b], writes=[s1_b])
                P.op("pool", lambda e, s1=s1: e.tensor_tensor(out=s1[:, 5:6], in0=s1[:, 4:5], in1=ep.nh[0], op=ALU.pow),
                     reads=[s1_b, ep.nh[1]], writes=[s1_b])
                P.op("dve", lambda e, s1=s1, o1=o1, p_=p_: e.scalar_tensor_tensor(
                    out=ypre[:, p_ * 128:(p_ + 1) * 128], in0=o1, scalar=s1[:, 5:6], in1=cst_t[:, 0:128],
                    op0=ALU.mult, op1=ALU.mult), reads=[o1_b, s1_b, cst_tb], writes=[yp_b])
        if grp == 3:
            out_proj_epi(k, ep, b, qb, ypre, yp_b, ypT, ypT_b, wout, wout_b, xra, xr_b, dst, 6, (7, 6))

    SKEW = 2
    for i in range(min(SKEW, len(steps))):
        do_S(steps[i])
    for i in range(len(steps)):
        if i + SKEW < len(steps):
            do_S(steps[i + SKEW])
        do_PV(steps[i])


def pass_hgrn(k, b, src, dst):
    P, A, NSEQ, NT = k.P, k.arena, k.NSEQ, k.NT
    l, sl = 2, 4
    NIN = 4096
    xs = [A.alloc("x", [1024], F32) for t in range(4)]
    hT, _ = A.alloc("hT", [KC, 512], BF16)
    hT_bufs = [Buf("hT%d" % i) for i in range(KC)]
    win, win_b = A.alloc("win", [KC, 2048], BF16)
    wqf = A.alloc("wqf", [KC, 2, 128], BF16, nbuf=2)
    wout, wout_b = A.alloc("wout", [KC, 1024], BF16)
    winv = k.wb["hgrn_w_in"].rearrange("(kc p) n -> p kc n", p=128)
    P.dma("sp", k.ld_sem(), win, winv[:, :, 2048:4096], reads=[k.wb_buf["hgrn_w_in"]], writes=[win_b])
    P.dma("sp", k.ld_sem(), wout, k.wb["hgrn_w_out"].rearrange("(kc p) n -> p kc n", p=128), reads=[k.wb_buf["hgrn_w_out"]], writes=[wout_b])
    ep = alloc_epi(k, sl, b, nbuf=1)
    nw = bcast_row(k, "normw", k.hgrn_norm[0, :], 1024)
    lbp, lbp_b = A.alloc("lbp", [4, 8], F32)
    lbt, lbt_b = A.alloc("lbt", [5, 8], F32)
    P.dma("sp", k.ld_sem(), lbp, k.hgrn_lb.rearrange("l (h p) -> p l h", p=128), writes=[lbp_b], allow_slow_non_contiguous=True)
    P.op("act", lambda e: e.activation(out=lbp, in_=lbp, func=AF.Exp), reads=[lbp_b], writes=[lbp_b])
    P.op("dve", lambda e: e.tensor_tensor(out=lbt[:, 1, :], in0=lbp[:, 1, :], in1=lbp[:, 2, :], op=ALU.add), reads=[lbp_b], writes=[lbt_b])
    P.op("dve", lambda e: e.tensor_tensor(out=lbt[:, 4, :], in0=lbp[:, 0, :], in1=lbp[:, 3, :], op=ALU.add), reads=[lbp_b], writes=[lbt_b])
    P.op("dve", lambda e: e.tensor_tensor(out=lbt[:, 0, :], in0=lbt[:, 1, :], in1=lbt[:, 4, :], op=ALU.add), reads=[lbt_b], writes=[lbt_b])
    P.op("dve", lambda e: e.reciprocal(out=lbt[:, 0, :], in_=lbt[:, 0, :]), reads=[lbt_b], writes=[lbt_b])
    P.op("dve", lambda e: e.tensor_tensor(out=lbt[:, 2, :], in0=lbt[:, 1, :], in1=lbt[:, 0, :], op=ALU.mult), reads=[lbt_b], writes=[lbt_b])
    P.op("dve", lambda e: e.tensor_scalar(out=lbt[:, 3, :], in0=lbt[:, 2, :], scalar1=-1.0, scalar2=1.0, op0=ALU.mult, op1=ALU.add),
         reads=[lbt_b], writes=[lbt_b])
    rmask, rmask_b = A.alloc("rmask", [4, 128], F32)
    P.op("pool", lambda e: e.memset(rmask, 1.0), writes=[rmask_b])
    P.op("pool", lambda e: e.memset(rmask[:, :, 0:1], 0.0), reads=[rmask_b], writes=[rmask_b])
    T1 = A.alloc("T1", [512], F32, nbuf=2)
    T2 = A.alloc("T2", [512], F32, nbuf=2)
    T3s = A.alloc("T3", [512], F32, nbuf=2)
    E1s = A.alloc("E1", [512], F32, nbuf=2)
    E2s = A.alloc("E2", [512], F32, nbuf=2)
    qpT, _ = A.alloc("qpT", [8, 512], BF16)
    kpT, _ = A.alloc("kpT", [8, 512], BF16)
    qpT_bufs = [Buf("qpT%d" % i) for i in range(8)]
    kpT_bufs = [Buf("kpT%d" % i) for i in range(8)]
    es, _ = A.alloc("es", [8, 16], F32)
    es_bufs = [Buf("es%d" % i) for i in range(8)]
    vb = A.alloc("vb", [1024], BF16, nbuf=1)
    sg = A.alloc("sg", [1024], F32, nbuf=1)
    kp, kp_b = A.alloc("kp", [8, 128], BF16)
    aT, aT_b = A.alloc("aT", [8, 128], BF16)
    S32, S32_b = A.alloc("S32", [8, 128], F32)
    Sb, Sb_b = A.alloc("Sb", [8, 128], BF16)
    dS8, dS8_b = A.alloc("dS8", [8, 128], F32)
    osb, osb_b = A.alloc("osb", [1024], F32)
    ss, ss_b = A.alloc("hss", [24], F32)
    ypre, yp_b = A.alloc("ypre", [1024], BF16)
    ypT, ypT_b = A.alloc("ypT", [1024], BF16)
    P.op("pool", lambda e: e.memset(S32, 0.0), writes=[S32_b])
    caus_b = k.caus_b16
    xs_ = xs
    nT = 0
    nvg = 0
    for st in range(k.NST):
        load_x(k, b, st, src, xs_)
        prologue(k, b, sl, xs_, hT, hT_bufs)
        def stage1_head(h, nT):
            t1, t1_b = T1[nT % 2]
            t2, t2_b = T2[nT % 2]
            T3, T3_b = T3s[nT % 2]
            E1, E1_b = E1s[nT % 2]
            E2, E2_b = E2s[nT % 2]
            qb_, qb_b = k.bank[4 + 2 * (nT % 2)]
            fb_, fb_b = k.bank[5 + 2 * (nT % 2)]
            wq, wq_b = wqf[nT % 2]
            P.dma("sp", k.ld_sem(), wq[:, :, 0, :], winv[:, :, h * 128:(h + 1) * 128], reads=[k.wb_buf["hgrn_w_in"]], writes=[wq_b])
            P.dma("sp", k.ld_sem(), wq[:, :, 1, :], winv[:, :, 1024 + h * 128:1024 + (h + 1) * 128], reads=[k.wb_buf["hgrn_w_in"]],
                  writes=[wq_b])
            for (bk, bk_b, w_) in ((qb_, qb_b, 0), (fb_, fb_b, 1)):
                for kc in range(KC):
                    P.op("pe", lambda e, bk=bk, kc=kc, w_=w_, wq=wq: e.matmul(bk[:, :], wq[:, kc, w_, :], hT[:, kc, :],
                                                                             start=(kc == 0), stop=(kc == KC - 1)),
                         reads=[wq_b, hT_bufs[kc]], writes=[bk_b])
            P.op("act", lambda e, t1=t1: e.activation(out=t1, in_=fb_[:, :], func=AF.Sigmoid), reads=[fb_b], writes=[t1_b])
            P.op("act", lambda e, t2=t2: e.activation(out=t2, in_=fb_[:, :], func=AF.Sigmoid, scale=-1.0), reads=[fb_b], writes=[t2_b])
            P.op("dve", lambda e, t1=t1, h=h: e.tensor_scalar(out=t1, in0=t1, scalar1=lbt[:, 3, h:h + 1], scalar2=lbt[:, 2, h:h + 1],
                                                             op0=ALU.mult, op1=ALU.add), reads=[t1_b, lbt_b], writes=[t1_b])
            P.op("act", lambda e, t1=t1: e.activation(out=t1, in_=t1, func=AF.Ln), reads=[t1_b], writes=[t1_b])
            P.op("dve", lambda e, t1=t1: e.tensor_tensor_scan(out=T3, data0=rmask.rearrange("p c t -> p (c t)"), data1=t1, initial=0.0,
                                                             op0=ALU.mult, op1=ALU.add), reads=[rmask_b, t1_b], writes=[T3_b])
            P.op("dve", lambda e, t2=t2, h=h: e.tensor_scalar(out=t2, in0=t2, scalar1=lbt[:, 3, h:h + 1], scalar2=None, op0=ALU.mult),
                 reads=[t2_b, lbt_b], writes=[t2_b])
            T3c = T3.rearrange("p (c t) -> p c t", t=128)
            esh = es[:, h, :]
            P.op("dve", lambda e, esh=esh: e.tensor_scalar(out=esh[:, 0:4], in0=T3c[:, :, 63], scalar1=-1.0, scalar2=None, op0=ALU.mult),
                 reads=[T3_b], writes=[es_bufs[h]])
            P.op("dve", lambda e, esh=esh: e.tensor_tensor(out=esh[:, 12:16], in0=T3c[:, :, 127], in1=T3c[:, :, 63], op=ALU.subtract),
                 reads=[T3_b], writes=[es_bufs[h]])
            P.op("act", lambda e, esh=esh: e.activation(out=esh[:, 4:8], in_=T3c[:, :, 63], func=AF.Exp), reads=[T3_b], writes=[es_bufs[h]])
            P.op("act", lambda e, esh=esh: e.activation(out=esh[:, 8:12], in_=T3c[:, :, 127], func=AF.Exp), reads=[T3_b], writes=[es_bufs[h]])
            P.op("act", lambda e, esh=esh: e.activation(out=esh[:, 12:16], in_=esh[:, 12:16], func=AF.Exp),
                 reads=[es_bufs[h]], writes=[es_bufs[h]])
            for c in range(4):
                cs = slice(c * 128, (c + 1) * 128)
                P.op("act", lambda e, cs=cs, c=c, esh=esh: e.activation(out=E1[:, cs], in_=T3[:, cs], func=AF.Exp, bias=esh[:, c:c + 1]),
                     reads=[T3_b, es_bufs[h]], writes=[E1_b])
                P.op("act", lambda e, cs=cs, c=c: e.activation(out=E2[:, cs], in_=T3[:, cs], func=AF.Exp, scale=-1.0,
                                                              bias=T3[:, c * 128 + 63:c * 128 + 64]),
                     reads=[T3_b], writes=[E2_b])
            P.op("dve", lambda e, h=h: e.tensor_tensor(out=qpT[:, h, :], in0=qb_[:, :], in1=E1, op=ALU.mult),
                 reads=[qb_b, E1_b], writes=[qpT_bufs[h]])
            P.op("dve", lambda e, h=h, t2=t2: e.tensor_tensor(out=kpT[:, h, :], in0=t2, in1=E2, op=ALU.mult),
                 reads=[t2_b, E2_b], writes=[kpT_bufs[h]])
        for h in range(8):
            stage1_head(h, nT)
            nT += 1
        for t in range(4):
            tok = slice(t * 128, (t + 1) * 128)
            va, va_b = vb[0]
            sga, sg_b = sg[0]
            for c in range(4):
                bk, bk_b = k.bank[2 + nvg % 2]
                nvg += 1
                for kc in range(KC):
                    P.op("pe", lambda e, bk=bk, kc=kc, c=c, tok=tok: e.matmul(
                        bk[:, :], hT[:, kc, tok], win[:, kc, c * 512:(c + 1) * 512], start=(kc == 0), stop=(kc == KC - 1)),
                        reads=[hT_bufs[kc], win_b], writes=[bk_b])
                if c < 2:
                    P.op("act", lambda e, bk=bk, c=c, va=va: e.activation(out=va[:, c * 512:(c + 1) * 512], in_=bk[:, :], func=AF.Copy),
                         reads=[bk_b], writes=[va_b])
                else:
                    P.op("act", lambda e, bk=bk, c=c, sga=sga: e.activation(out=sga[:, (c - 2) * 512:(c - 1) * 512], in_=bk[:, :], func=AF.Silu),
                         reads=[bk_b], writes=[sg_b])
            tb, tb_b = k.bank[0]
            tb16 = tb.bitcast(BF16)
            for h in range(8):
                P.op("pe", lambda e, h=h, tok=tok: e.transpose(out=tb16[:, h * 128:(h + 1) * 128], in_=kpT[:, h, tok], identity=k.idb),
                     reads=[kpT_bufs[h], k.idb_b], writes=[tb_b])
            P.op("act", lambda e: e.activation(out=kp, in_=tb16[:, 0:1024].rearrange("p (h d) -> p h d", h=8), func=AF.Copy),
                 reads=[tb_b], writes=[kp_b])
            P.op("dve", lambda e, t=t: e.tensor_tensor(out=Sb, in0=S32, in1=es[:, :, 4 + t:5 + t].to_broadcast([128, 8, 128]),
                                                      op=ALU.mult), reads=[S32_b] + es_bufs, writes=[Sb_b])
            for hh in range(2):
                sbk, sbk_b = k.bank[1]
                for h4 in range(4):
                    h = hh * 4 + h4
                    P.op("pe", lambda e, h=h, h4=h4, tok=tok: e.matmul(sbk[:, h4 * 128:(h4 + 1) * 128], kpT[:, h, tok], qpT[:, h, tok],
                                                                      start=True, stop=True),
                         reads=[kpT_bufs[h], qpT_bufs[h]], writes=[sbk_b])
                P.op("dve", lambda e, hh=hh: e.tensor_tensor(
                    out=aT[:, hh * 4:(hh + 1) * 4, :], in0=sbk[:, :].rearrange("p (h d) -> p h d", h=4),
                    in1=k.cst[:, C_CAUS:C_CAUS + 128].unsqueeze(1).to_broadcast([128, 4, 128]), op=ALU.mult),
                    reads=[sbk_b, k.cst_b], writes=[aT_b])
            for h in range(8):
                ob, ob_b = k.bank[6 + h // 4]
                oc = slice((h % 4) * 128, (h % 4 + 1) * 128)
                P.op("pe", lambda e, h=h, ob=ob, oc=oc, va=va: e.matmul(ob[:, oc], aT[:, h, :], va[:, h * 128:(h + 1) * 128],
                                                                       start=True, stop=False), reads=[aT_b, va_b], writes=[ob_b])
                P.op("pe", lambda e, h=h, ob=ob, oc=oc, tok=tok: e.matmul(ob[:, oc], qpT[:, h, tok], Sb[:, h, :], start=False, stop=True),
                     reads=[qpT_bufs[h], Sb_b], writes=[ob_b])
            for h in range(8):
                db, db_b = k.bank[4 + h // 4]
                dc = slice((h % 4) * 128, (h % 4 + 1) * 128)
                P.op("pe", lambda e, h=h, db=db, dc=dc, va=va: e.matmul(db[:, dc], kp[:, h, :], va[:, h * 128:(h + 1) * 128],
                                                                       start=True, stop=True), reads=[kp_b, va_b], writes=[db_b])
            for hh in range(2):
                db, db_b = k.bank[4 + hh]
                P.op("dve", lambda e, hh=hh, db=db, t=t: e.tensor_tensor(
                    out=dS8[:, hh * 4:(hh + 1) * 4, :], in0=db[:, :].rearrange("p (h d) -> p h d", h=4),
                    in1=es[:, hh * 4:(hh + 1) * 4, 12 + t:13 + t].to_broadcast([128, 4, 128]), op=ALU.mult),
                    reads=[db_b] + es_bufs, writes=[dS8_b])
            P.op("dve", lambda e, t=t: e.tensor_tensor(out=S32, in0=S32, in1=es[:, :, 8 + t:9 + t].to_broadcast([128, 8, 128]),
                                                      op=ALU.mult), reads=[S32_b] + es_bufs, writes=[S32_b])
            P.op("dve", lambda e: e.tensor_tensor(out=S32, in0=S32, in1=dS8, op=ALU.add), reads=[S32_b, dS8_b], writes=[S32_b])
            for hh in range(2):
                ob, ob_b = k.bank[6 + hh]
                P.op("act", lambda e, ob=ob, hh=hh: e.activation(out=osb[:, hh * 512:(hh + 1) * 512], in_=ob[:, :], func=AF.Copy),
                     reads=[ob_b], writes=[osb_b])
            P.op("dve", lambda e: e.tensor_tensor(out=dS8.rearrange("p h d -> p (h d)"), in0=osb, in1=osb, op=ALU.mult),
                 reads=[osb_b, dS8_b], writes=[dS8_b])
            P.op("dve", lambda e: e.tensor_reduce(out=ss[:, 0:8], in_=dS8, axis=AX.X, op=ALU.add), reads=[dS8_b], writes=[ss_b])
            P.op("dve", lambda e: e.tensor_scalar(out=ss[:, 8:16], in0=ss[:, 0:8], scalar1=1.0 / 128.0, scalar2=1e-6, op0=ALU.mult, op1=ALU.add),
                 reads=[ss_b], writes=[ss_b])
            P.op("pool", lambda e: e.tensor_tensor(out=ss[:, 16:24], in0=ss[:, 8:16], in1=ep.nh[0].to_broadcast([128, 8]), op=ALU.pow),
                 reads=[ss_b, ep.nh[1]], writes=[ss_b])
            P.op("dve", lambda e: e.tensor_tensor(out=osb.rearrange("p (h d) -> p h d", h=8), in0=osb.rearrange("p (h d) -> p h d", h=8),
                                                  in1=ss[:, 16:24].unsqueeze(2).to_broadcast([128, 8, 128]), op=ALU.mult),
                 reads=[osb_b, ss_b], writes=[osb_b])
            P.op("pool", lambda e: e.tensor_tensor(out=osb, in0=osb, in1=nw[0], op=ALU.mult), reads=[osb_b, nw[1]], writes=[osb_b])
            P.op("dve", lambda e, sga=sga: e.tensor_tensor(out=ypre, in0=osb, in1=sga, op=ALU.mult), reads=[osb_b, sg_b], writes=[yp_b])
            out_proj_epi(k, ep, b, st * 4 + t, ypre, yp_b, ypT, ypT_b, wout, wout_b, xs_[t][0], xs_[t][1], dst, 0, (2, 3))


_NC_CACHE = {}


def kernel(**inputs):
    n = 8
    NSEQ = 2
    if "nc" not in _NC_CACHE:
        _NC_CACHE["nc"] = build(NSEQ=NSEQ, S=4096)
    nc = _NC_CACHE["nc"]
    in_maps = []
    shared = None
    for i in range(n):
        sl = slice(i * NSEQ, (i + 1) * NSEQ)
        m = make_in_map(inputs, inputs["x"][sl], inputs["c"][sl], inputs["positions"][sl]) if shared is None else dict(shared)
        if shared is None:
            shared = m
        else:
            m["x"] = np.ascontiguousarray(inputs["x"][sl], dtype=np.float32)
            m["c"] = np.ascontiguousarray(inputs["c"][sl], dtype=np.float32)
            m["positions"] = np.ascontiguousarray(inputs["positions"][sl], dtype=np.int32)
        in_maps.append(m)
    res = run_bass_kernel_spmd(nc, in_maps, core_ids=list(range(n)))
    return np.concatenate([np.asarray(r["out"]) for r in res.results], axis=0).astype(np.float32, copy=False)
```

```python
import contextlib
import math
import numpy as np
import concourse.bass as bass
import concourse.mybir as mybir
from concourse.bass_utils import run_bass_kernel_spmd

F32 = mybir.dt.float32
BF16 = mybir.dt.bfloat16
I32 = mybir.dt.int32
AF = mybir.ActivationFunctionType
ALU = mybir.AluOpType
AX = mybir.AxisListType

D = 1024
KC = 8
FFN_H = 2816
NJ = 22
DEPTH = 4
ALPHA = (2 * DEPTH) ** 0.25
SAME_ENG_WINDOW = 6


class Sem:
    def __init__(self, h, name):
        self.h = h
        self.name = name
        self.count = 0


class Buf:
    __slots__ = ("name", "lw", "rd")

    def __init__(self, name):
        self.name = name
        self.lw = None
        self.rd = {}


class Eng:
    def __init__(self, name, sem, is_pe=False):
        self.name = name
        self.sem = sem
        self.is_pe = is_pe
        self.seen = {}
        self.ops = []


class Prog:
    def __init__(self, nc, stack):
        self.nc = nc
        self.stack = stack
        self.eng = {}
        for name in ("pe", "act", "dve", "pool", "sp"):
            self.eng[name] = Eng(name, self.new_sem("e_" + name), is_pe=(name == "pe"))
        self.snap = {}
        self.n_ops = 0
        self.dma_sems = []

    def new_sem(self, name):
        return Sem(self.stack.enter_context(self.nc.semaphore(name)), name)

    def dma_sem(self, name):
        s = self.new_sem(name)
        self.dma_sems.append(s)
        return s

    def _waits(self, E, reads, writes, is_dma):
        deps = {}

        def add(d, raw):
            if d is None:
                return
            S, v = d
            if S is E.sem and not is_dma:
                if E.is_pe or not raw:
                    return
                if v <= E.sem.count - SAME_ENG_WINDOW:
                    return
            if deps.get(S, 0) < v:
                deps[S] = v

        for r in reads:
            add(r.lw, True)
        for w in writes:
            add(w.lw, False)
            for S, v in w.rd.items():
                add((S, v), False)
        out = []
        for S, v in deps.items():
            if E.seen.get(S, 0) >= v:
                continue
            out.append((S, v))
        return out

    def _apply_waits(self, E, waits):
        for S, v in waits:
            E.ops.append(("wait", S, v))
            sn = self.snap.get((S, v))
            if sn:
                for S2, v2 in sn.items():
                    if E.seen.get(S2, 0) < v2:
                        E.seen[S2] = v2
            if E.seen.get(S, 0) < v:
                E.seen[S] = v

    def op(self, eng, fn, reads=(), writes=()):
        E = self.eng[eng]
        self._apply_waits(E, self._waits(E, reads, writes, False))
        E.sem.count += 1
        done = (E.sem, E.sem.count)
        E.ops.append(("op", fn, E.sem, 1))
        self.snap[done] = dict(E.seen)
        for w in writes:
            w.lw = done
            w.rd = {}
        for r in reads:
            if r.rd.get(E.sem, 0) < E.sem.count:
                r.rd[E.sem] = E.sem.count
        self.n_ops += 1

    def dma(self, eng, sem, out, in_, reads=(), writes=(), **kw):
        E = self.eng[eng]
        waits = self._waits(E, reads, writes, True)
        if sem.count > 0 and E.seen.get(sem, 0) < sem.count and not any(S is sem for S, _ in waits):
            waits.append((sem, sem.count))
        waits = [(S, max(v, sem.count) if S is sem else v) for S, v in waits]
        self._apply_waits(E, waits)
        sem.count += 16
        done = (sem, sem.count)
        E.ops.append(("op", lambda e: e.dma_start(out=out, in_=in_, **kw), sem, 16))
        self.snap[done] = dict(E.seen)
        for w in writes:
            w.lw = done
            w.rd = {}
        for r in reads:
            if r.rd.get(sem, 0) < sem.count:
                r.rd[sem] = sem.count
        self.n_ops += 1

    def barrier(self):
        targets = [(E.sem, E.sem.count) for E in self.eng.values() if E.sem.count > 0]
        targets += [(s, s.count) for s in self.dma_sems if s.count > 0]
        for E in self.eng.values():
            ws = [(S, v) for S, v in targets if S is not E.sem and E.seen.get(S, 0) < v]
            self._apply_waits(E, ws)

    def final_wait(self, eng, sems):
        E = self.eng[eng]
        self._apply_waits(E, [(s, s.count) for s in sems if s.count > 0 and E.seen.get(s, 0) < s.count])

    def emit(self):
        nc = self.nc
        with nc.Block() as block:
            def run(E):
                def body(e):
                    for o in E.ops:
                        if o[0] == "wait":
                            e.wait_ge(o[1].h, o[2])
                        else:
                            o[1](e).then_inc(o[2].h, o[3])
                return body

            block.tensor(run(self.eng["pe"]))
            block.scalar(run(self.eng["act"]))
            block.vector(run(self.eng["dve"]))
            block.gpsimd(run(self.eng["pool"]))
            block.sync(run(self.eng["sp"]))


class Arena:
    def __init__(self, P, name, nbytes):
        self.t = P.stack.enter_context(P.nc.sbuf_tensor(name, [128, nbytes // 4], F32))
        self.nbytes = nbytes
        self.off = 0
        self.name = name

    def reset(self):
        self.off = 0

    def alloc(self, name, shape, dt, nbuf=None):
        esz = 2 if dt == BF16 else 4
        n = int(np.prod(shape))
        nb = (n * esz + 31) // 32 * 32
        res = []
        for i in range(nbuf or 1):
            assert self.off + nb <= self.nbytes, (self.name, name, self.off, nb, self.nbytes)
            v = self.t[:, self.off // 4:(self.off + nb) // 4]
            if dt != F32:
                v = v.bitcast(dt)
            v = v[:, 0:n]
            if len(shape) == 2:
                v = v.rearrange("p (a b) -> p a b", b=shape[1])
            elif len(shape) == 3:
                v = v.rearrange("p (a b c) -> p a b c", b=shape[1], c=shape[2])
            self.off += nb
            res.append((v, Buf(name + str(i))))
        return res if nbuf else res[0]


C_ID = 0
C_CAUS = 128
C_STRICT = 256
C_INV = 384
C_ONES = 392
C_N = 520
TWO_PI = 2.0 * math.pi
CW1 = 6.28125
CW2 = float(np.float32(TWO_PI - CW1))
CW3 = float(TWO_PI - CW1 - CW2)


def make_consts():
    c = np.zeros((128, C_N), np.float32)
    c[:, C_ID:C_ID + 128] = np.eye(128, dtype=np.float32)
    k = np.arange(128)[:, None]
    q = np.arange(128)[None, :]
    c[:, C_CAUS:C_CAUS + 128] = (k <= q)
    c[:, C_STRICT:C_STRICT + 128] = (k > q)
    inv = np.power(np.float32(500000.0), -np.arange(8, dtype=np.float32) * np.float32(2.0) / np.float32(16.0)).astype(np.float32)
    c[:, C_INV:C_INV + 8] = inv[None, :]
    c[:, C_ONES:C_ONES + 128] = 1.0
    return c


W_SPECS = [
    ("mlstm_w_in", 1024, 3080), ("mlstm_w_out", 1024, 1024),
    ("swa_w_in", 1024, 1280), ("swa_w_out", 1024, 1024),
    ("hgrn_w_in", 1024, 4096), ("hgrn_w_out", 1024, 1024),
    ("diff_w_in", 1024, 3072), ("diff_w_out", 1024, 1024),
]


DBG_SKIP = set()


class K:
    pass


def build(NSEQ=2, S=4096, passes=None, arena_kb=188, debug=False):
    NT = S // 128
    NST = S // 512
    if passes is None:
        passes = []
        for l in range(DEPTH):
            passes += [("mix", l), ("ffn", l)]
    nc = bass.Bass("TRN2", target_bir_lowering=False)
    stack = contextlib.ExitStack()
    P = Prog(nc, stack)
    k = K()
    k.P, k.nc, k.NSEQ, k.S, k.NT, k.NST = P, nc, NSEQ, S, NT, NST
    k.debug = debug

    def din(name, shape, dt=F32):
        return nc.dram_tensor(name, list(shape), dt, kind="ExternalInput").ap()

    k.x = din("x", [NSEQ, S, D])
    k.c = din("c", [NSEQ, D])
    k.pos = din("positions", [NSEQ, S], I32)
    k.ada_w = din("ada_w", [DEPTH, 2, D, 3 * D])
    k.ada_b = din("ada_b", [DEPTH, 2, 3 * D])
    k.ln_g = din("ln_g", [DEPTH, 2, D])
    k.ln_b = din("ln_b", [DEPTH, 2, D])
    k.w32 = {}
    for name, r, c_ in W_SPECS:
        k.w32[name] = din(name, [1, r, c_])
    k.mlstm_b_gate = din("mlstm_b_gate", [1, 8])
    k.mlstm_norm = din("mlstm_norm", [1, 1024])
    k.swa_sinks = din("swa_sinks", [1, 16])
    k.hgrn_lb = din("hgrn_lower_bounds", [4, 1024])
    k.hgrn_norm = din("hgrn_norm", [1, 1024])
    k.diff_lambda = din("diff_lambda", [1, 4, 64])
    k.diff_norm = din("diff_norm", [1, 128])
    k.ffn_w_in = din("ffn_w_in", [DEPTH, D, 2 * FFN_H])
    k.ffn_w_out = din("ffn_w_out", [DEPTH, FFN_H, D])
    k.consts = din("consts", [128, C_N])
    k.out = nc.dram_tensor("out", [NSEQ, S, D], F32, kind="ExternalOutput").ap()

    def dscr(name, shape, dt):
        return nc.dram_tensor(name, list(shape), dt, kind="Internal").ap()

    k.act = dscr("act_scr", [NSEQ, S, D], F32)
    k.modrows = dscr("modrows", [8, NSEQ, 3 * D], F32)
    k.qt_scr = dscr("qt_scr", [8, 128, S], BF16)
    k.kt_scr = dscr("kt_scr", [8, 128, S], BF16)
    k.v_scr = dscr("v_scr", [S, 8 * 130], BF16)
    k.qt_bufs = [Buf("qt%d" % i) for i in range(NST)]
    k.kt_bufs = [Buf("kt%d" % i) for i in range(NST)]
    k.v_bufs = [Buf("v%d" % i) for i in range(NT)]
    k.wb = {}
    k.wb_buf = {}
    for name, r, c_ in W_SPECS:
        k.wb[name] = dscr(name + "_bf", [r, c_], BF16)
        k.wb_buf[name] = Buf(name)
    layers_used = sorted(set(l for _, l in passes))
    for l in layers_used:
        k.wb["w1_%d" % l] = dscr("w1bf_%d" % l, [D, 2 * FFN_H], BF16)
        k.wb["w2_%d" % l] = dscr("w2bf_%d" % l, [FFN_H, D], BF16)
        k.wb_buf["w1_%d" % l] = Buf("w1_%d" % l)
        k.wb_buf["w2_%d" % l] = Buf("w2_%d" % l)
    k.act_buf = [[Buf("act%d_%d" % (b, t)) for t in range(NT)] for b in range(NSEQ)]
    k.modrows_buf = [Buf("modrows%d" % i) for i in range(8)]

    pers = Arena(P, "pers", 12 * 1024)
    k.cst, k.cst_b = pers.alloc("cst", [C_N], F32)
    k.idb, k.idb_b = pers.alloc("idb", [128], BF16)
    k.caus_b16, k.caus_b16_b = pers.alloc("causb", [128], BF16)
    k.strict_b16, k.strict_b16_b = pers.alloc("strictb", [128], BF16)
    k.modT, k.modT_b = pers.alloc("modT", [8 * NSEQ * 2 * KC], F32)
    k.rope, k.rope_b = pers.alloc("rope", [NSEQ * 2 * NT * 8], F32)
    k.small, k.small_b = pers.alloc("small", [64], F32)
    k.pers = pers
    k.arena = Arena(P, "arena", arena_kb * 1024)
    k.bank = []
    for i in range(8):
        t = stack.enter_context(nc.psum_tensor("bank%d" % i, [128, 512], F32))
        k.bank.append((t, Buf("bank%d" % i)))
    k.ld_sems = [P.dma_sem("ld%d" % i) for i in range(12)]
    k.st_sems = [P.dma_sem("st%d" % i) for i in range(6)]
    k.ld_i = 0
    k.st_i = 0

    def ld_sem():
        k.ld_i += 1
        return k.ld_sems[k.ld_i % len(k.ld_sems)]

    def st_sem():
        k.st_i += 1
        return k.st_sems[k.st_i % len(k.st_sems)]

    k.ld_sem, k.st_sem = ld_sem, st_sem
    k.dumped = {}
    k.cast_i = {}
    k.cast_n = 0

    def dump(name, ap, buf, dt=F32):
        if not k.debug or name in k.dumped:
            return
        shape = [int(x) for x in ap.shape]
        d = nc.dram_tensor("dbg_" + name, shape, dt, kind="ExternalOutput").ap()
        k.dumped[name] = d
        P.dma("sp", k.ld_sem(), d, ap, reads=[buf])

    k.dump = dump

    P.dma("sp", ld_sem(), k.cst, k.consts, writes=[k.cst_b])
    P.op("dve", lambda e: e.tensor_copy(out=k.idb, in_=k.cst[:, C_ID:C_ID + 128]), reads=[k.cst_b], writes=[k.idb_b])
    P.op("dve", lambda e: e.tensor_copy(out=k.caus_b16, in_=k.cst[:, C_CAUS:C_CAUS + 128]), reads=[k.cst_b], writes=[k.caus_b16_b])
    P.op("dve", lambda e: e.tensor_copy(out=k.strict_b16, in_=k.cst[:, C_STRICT:C_STRICT + 128]), reads=[k.cst_b], writes=[k.strict_b16_b])
    k.ident = k.cst[:, C_ID:C_ID + 128]

    phase_prepass(k, layers_used, passes)
    phase_ada(k, passes)
    if any(kind == "mix" and l in (1, 3) for kind, l in passes):
        phase_rope(k)
    src_is_x = True
    for pi, (kind, l) in enumerate(passes):
        last = pi == len(passes) - 1
        for b in range(NSEQ):
            P.barrier()
            k.arena.reset()
            src = k.x if src_is_x else k.act
            dst = k.out if last else k.act
            if (kind, b) in DBG_SKIP:
                continue
            if kind == "ffn":
                pass_ffn(k, l, b, src, dst)
            else:
                pass_mix(k, l, b, src, dst)
        src_is_x = False
    if debug:
        P.barrier()
        dact = nc.dram_tensor("dbg_act", [NSEQ, S, D], F32, kind="ExternalOutput").ap()
        for b in range(NSEQ):
            P.dma("pool", k.st_sem(), dact[b], k.act[b], reads=[x for x in k.act_buf[b]])
        dmt = nc.dram_tensor("dbg_modT", [128, 8 * NSEQ * 2 * KC], F32, kind="ExternalOutput").ap()
        P.dma("pool", k.st_sem(), dmt, k.modT, reads=[k.modT_b])
        drp = nc.dram_tensor("dbg_rope2", [128, NSEQ * 2 * NT * 8], F32, kind="ExternalOutput").ap()
        P.dma("pool", k.st_sem(), drp, k.rope, reads=[k.rope_b])
        dmod = nc.dram_tensor("dbg_mod", [8, NSEQ, 3 * D], F32, kind="ExternalOutput").ap()
        P.dma("pool", k.st_sem(), dmod, k.modrows, reads=k.modrows_buf)
    P.final_wait("sp", k.st_sems)
    P.emit()
    stack.close()
    return nc


BACKGROUND_CAST = True
CAST_W = 1408

MIXW = {0: ("mlstm_w_in", "mlstm_w_out"), 1: ("swa_w_in", "swa_w_out"),
        2: ("hgrn_w_in", "hgrn_w_out"), 3: ("diff_w_in", "diff_w_out")}


def phase_prepass(k, layers_used, passes):
    P, A = k.P, k.arena
    A.reset()
    s32 = A.alloc("s32", [5632], F32, nbuf=2)
    s16 = A.alloc("s16", [5632], BF16, nbuf=2)
    jobs = []
    k.deferred = {}
    for L in layers_used:
        lj = []
        idxs = [i for i, (kd, ll) in enumerate(passes) if ll == L]
        if ("mix", L) in passes:
            for name in MIXW[L]:
                lj.append((k.w32[name][0], k.wb[name], k.wb_buf[name]))
        if ("ffn", L) in passes:
            lj.append((k.ffn_w_in[L], k.wb["w1_%d" % L], k.wb_buf["w1_%d" % L]))
            lj.append((k.ffn_w_out[L], k.wb["w2_%d" % L], k.wb_buf["w2_%d" % L]))
        host = ("ffn", L - 1)
        if BACKGROUND_CAST and host in passes and passes.index(host) < min(idxs):
            ch = []
            for src, dst, buf in lj:
                R, C = src.shape
                for rb in range(R // 128):
                    for c0 in range(0, C, CAST_W):
                        ch.append((src[rb * 128:(rb + 1) * 128, c0:min(C, c0 + CAST_W)],
                                   dst[rb * 128:(rb + 1) * 128, c0:min(C, c0 + CAST_W)], buf, min(C, c0 + CAST_W) - c0))
            k.deferred[L - 1] = ch
        else:
            jobs += lj
    engs = ["act", "dve", "pool"]
    i = 0
    for src, dst, buf in jobs:
        R, C = src.shape
        for rb in range(R // 128):
            (a32, b32), (a16, b16) = s32[i % 2], s16[i % 2]
            P.dma("sp", k.ld_sem(), a32[:, 0:C], src[rb * 128:(rb + 1) * 128, :], writes=[b32])
            eng = engs[i % 3]
            if eng == "act":
                P.op("act", lambda e, o=a16[:, 0:C], i_=a32[:, 0:C]: e.activation(out=o, in_=i_, func=AF.Copy),
                     reads=[b32], writes=[b16])
            else:
                P.op(eng, lambda e, o=a16[:, 0:C], i_=a32[:, 0:C]: e.tensor_copy(out=o, in_=i_),
                     reads=[b32], writes=[b16])
            P.dma("pool", k.st_sem(), dst[rb * 128:(rb + 1) * 128, :], a16[:, 0:C], reads=[b16], writes=[buf])
            i += 1


def modT5(k):
    return k.modT.rearrange("p (s b w kc) -> p s b w kc", s=8, b=k.NSEQ, w=2, kc=KC)


def phase_ada(k, passes):
    P, A, NSEQ = k.P, k.arena, k.NSEQ
    P.barrier()
    A.reset()
    condT, condT_b = A.alloc("condT", [KC, NSEQ], F32)
    for b in range(NSEQ):
        P.dma("sp", k.ld_sem(), condT[:, :, b], k.c[b].rearrange("(kc p) -> p kc", p=128), writes=[condT_b],
              allow_slow_non_contiguous=True)
    P.op("act", lambda e: e.activation(out=condT, in_=condT, func=AF.Silu), reads=[condT_b], writes=[condT_b])
    slab = A.alloc("aslab", [KC, 512], F32, nbuf=2)
    biasrow = A.alloc("abias", [3072], F32, nbuf=2)
    modrow = A.alloc("modrow", [3072], F32, nbuf=2)
    sls = sorted(set(2 * l + (0 if kind == "mix" else 1) for kind, l in passes))
    m5 = modT5(k)
    cnt = 0
    for idx, sl in enumerate(sls):
        l, s = sl // 2, sl % 2
        br, br_b = biasrow[idx % 2]
        mr, mr_b = modrow[idx % 2]
        for b in range(NSEQ):
            P.dma("sp", k.ld_sem(), br[b:b + 1, :], k.ada_b[l, s:s + 1, :], writes=[br_b])
        wv = k.ada_w[l, s].rearrange("(kc p) e -> p kc e", p=128)
        for n in range(6):
            sa, sa_b = slab[cnt % 2]
            bk, bk_b = k.bank[cnt % 2]
            cnt += 1
            P.dma("sp", k.ld_sem(), sa, wv[:, :, n * 512:(n + 1) * 512], writes=[sa_b])
            for kc in range(KC):
                P.op("pe", lambda e, bk=bk, sa=sa, kc=kc: e.matmul(bk[0:NSEQ, :], condT[:, kc, :], sa[:, kc, :],
                                                                     start=(kc == 0), stop=(kc == KC - 1)),
                     reads=[condT_b, sa_b], writes=[bk_b])
            P.op("dve", lambda e, bk=bk, n=n, mr=mr, br=br: e.tensor_tensor(
                out=mr[0:NSEQ, n * 512:(n + 1) * 512], in0=bk[0:NSEQ, :], in1=br[0:NSEQ, n * 512:(n + 1) * 512],
                op=ALU.add), reads=[bk_b, br_b], writes=[mr_b])
        P.op("dve", lambda e, mr=mr: e.tensor_scalar(out=mr[0:NSEQ, 1024:3072], in0=mr[0:NSEQ, 1024:3072],
                                                    scalar1=1.0, scalar2=None, op0=ALU.add),
             reads=[mr_b], writes=[mr_b])
        P.dma("pool", k.st_sem(), k.modrows[sl], mr[0:NSEQ, :], reads=[mr_b], writes=[k.modrows_buf[sl]])
        for b in range(NSEQ):
            P.dma("sp", k.ld_sem(), m5[:, sl, b, :, :],
                  k.modrows[sl, b, 0:2048].rearrange("(w kc p) -> p w kc", p=128, kc=KC),
                  reads=[k.modrows_buf[sl]], writes=[k.modT_b], allow_slow_non_contiguous=True)


def load_x(k, b, st, src, xs):
    P = k.P
    for t in range(4):
        tt = st * 4 + t
        rd = [k.act_buf[b][tt]] if src is k.act else []
        P.dma("sp", k.ld_sem(), xs[t][0], src[b, tt * 128:(tt + 1) * 128, :], reads=rd, writes=[xs[t][1]])


def prologue(k, b, sl, xs, hT, hT_bufs, tpb=(0, 1)):
    P = k.P
    m5 = modT5(k)
    for kc in range(KC):
        bk, bk_b = k.bank[tpb[kc % 2]]
        for t in range(4):
            P.op("pe", lambda e, bk=bk, t=t, kc=kc: e.transpose(out=bk[:, t * 128:(t + 1) * 128],
                                                                in_=xs[t][0][:, kc * 128:(kc + 1) * 128],
                                                                identity=k.ident),
                 reads=[xs[t][1], k.cst_b], writes=[bk_b])
        P.op("act", lambda e, bk=bk, kc=kc: e.activation(out=hT[:, kc, :], in_=bk[:, :], func=AF.Identity,
                                                         scale=m5[:, sl, b, 1, kc:kc + 1],
                                                         bias=m5[:, sl, b, 0, kc:kc + 1]),
             reads=[bk_b, k.modT_b], writes=[hT_bufs[kc]])


def alloc_epi(k, sl, b, nbuf=2):
    P, A = k.P, k.arena
    ep = K()
    l, s = sl // 2, sl % 2
    ep.g1p = A.alloc("g1p", [1024], F32)
    ep.lng = A.alloc("lng", [1024], F32)
    ep.lnb = A.alloc("lnb", [1024], F32)
    P.dma("sp", k.ld_sem(), ep.g1p[0], k.modrows[sl, b, 2048:3072].partition_broadcast(128),
          reads=[k.modrows_buf[sl]], writes=[ep.g1p[1]])
    P.dma("sp", k.ld_sem(), ep.lng[0], k.ln_g[l, s, :].partition_broadcast(128), writes=[ep.lng[1]])
    P.dma("sp", k.ld_sem(), ep.lnb[0], k.ln_b[l, s, :].partition_broadcast(128), writes=[ep.lnb[1]])
    ep.z = A.alloc("z", [1024], F32, nbuf=nbuf)
    ep.xo = A.alloc("xo", [1024], F32, nbuf=nbuf)
    ep.st = A.alloc("epst", [32], F32, nbuf=4)
    ep.nh = A.alloc("neghalf", [1], F32)
    P.op("pool", lambda e: e.memset(ep.nh[0], -0.5), writes=[ep.nh[1]])
    ep.i = 0
    return ep


def epilogue(k, ep, b, tt, ybanks, x_ap, x_buf, dst):
    P = k.P
    i = ep.i
    ep.i += 1
    z, z_b = ep.z[i % len(ep.z)]
    xn, xn_b = z, z_b
    xo, xo_b = ep.xo[i % len(ep.xo)]
    st, st_b = ep.st[i % 4]
    g1p, g1p_b = ep.g1p
    for h in range(2):
        yb, yb_b = k.bank[ybanks[h]]
        P.op("dve", lambda e, h=h, yb=yb: e.tensor_tensor(out=z[:, h * 512:(h + 1) * 512], in0=yb[:, :],
                                                         in1=g1p[:, h * 512:(h + 1) * 512], op=ALU.mult),
             reads=[yb_b, g1p_b], writes=[z_b])
    P.op("dve", lambda e: e.scalar_tensor_tensor(out=z, in0=x_ap, scalar=ALPHA, in1=z, op0=ALU.mult, op1=ALU.add),
         reads=[x_buf, z_b], writes=[z_b])
    for h in range(2):
        P.op("dve", lambda e, h=h: e.bn_stats(out=st[:, h * 6:(h + 1) * 6], in_=z[:, h * 512:(h + 1) * 512]),
             reads=[z_b], writes=[st_b])
    mv = st[:, 12:14]
    P.op("dve", lambda e: e.bn_aggr(out=mv, in_=st[:, 0:12]), reads=[st_b], writes=[st_b])
    P.op("dve", lambda e: e.tensor_scalar(out=st[:, 14:15], in0=st[:, 13:14], scalar1=1e-5, scalar2=None, op0=ALU.add),
         reads=[st_b], writes=[st_b])
    P.op("pool", lambda e: e.tensor_tensor(out=st[:, 15:16], in0=st[:, 14:15], in1=ep.nh[0], op=ALU.pow),
         reads=[st_b, ep.nh[1]], writes=[st_b])
    P.op("dve", lambda e: e.scalar_tensor_tensor(out=st[:, 16:17], in0=st[:, 12:13], scalar=-1.0, in1=st[:, 15:16],
                                                 op0=ALU.mult, op1=ALU.mult), reads=[st_b], writes=[st_b])
    P.op("act", lambda e: e.activation(out=xn, in_=z, func=AF.Identity, scale=st[:, 15:16], bias=st[:, 16:17]),
         reads=[z_b, st_b], writes=[xn_b])
    P.op("pool", lambda e: e.tensor_tensor(out=xn, in0=xn, in1=ep.lng[0], op=ALU.mult),
         reads=[xn_b, ep.lng[1]], writes=[xn_b])
    P.op("pool", lambda e: e.tensor_tensor(out=xo, in0=xn, in1=ep.lnb[0], op=ALU.add),
         reads=[xn_b, ep.lnb[1]], writes=[xo_b])
    wr = [k.act_buf[b][tt]] if dst is k.act else []
    P.dma("pool", k.st_sem(), dst[b, tt * 128:(tt + 1) * 128, :], xo, reads=[xo_b], writes=wr)


def pass_ffn(k, l, b, src, dst):
    P, A = k.P, k.arena
    sl = 2 * l + 1
    xs = [[A.alloc("x", [1024], F32) for t in range(4)] for i in range(2)]
    hT, _ = A.alloc("hT", [KC, 512], BF16)
    hT_bufs = [Buf("hT%d" % i) for i in range(KC)]
    hid, _ = A.alloc("hid", [NJ, 512], BF16)
    hid_bufs = [Buf("hid%d" % i) for i in range(NJ)]
    w1s = A.alloc("w1s", [KC, 2, 256], BF16, nbuf=2)
    w1s_bufs = [[Buf("w1g"), Buf("w1u")] for i in range(2)]
    w2, w2_b = A.alloc("w2", [NJ, 1024], BF16)
    sg = A.alloc("sg", [512], F32, nbuf=2)
    ep = alloc_epi(k, sl, b)
    w1v = k.wb["w1_%d" % l].rearrange("(kc p) n -> p kc n", p=128)
    w1_buf = k.wb_buf["w1_%d" % l]
    P.dma("sp", k.ld_sem(), w2, k.wb["w2_%d" % l].rearrange("(j p) n -> p j n", p=128),
          reads=[k.wb_buf["w2_%d" % l]], writes=[w2_b])
    dj = k.deferred.get(l) or []
    if dj:
        c32 = A.alloc("c32", [CAST_W], F32, nbuf=2)
        c16 = A.alloc("c16", [CAST_W], BF16, nbuf=2)
        n_slots = k.NSEQ * k.NST
        per = -(-len(dj) // n_slots) if k.cast_i.get(l, 0) == 0 or True else 0

    def cast_tick(n):
        i0 = k.cast_i.get(l, 0)
        for (src_c, dst_c, buf, w_) in dj[i0:i0 + n]:
            i = k.cast_n
            k.cast_n += 1
            (a32, b32), (a16, b16) = c32[i % 2], c16[i % 2]
            P.dma("sp", k.ld_sem(), a32[:, 0:w_], src_c, writes=[b32])
            P.op("pool", lambda e, o=a16[:, 0:w_], i_=a32[:, 0:w_]: e.tensor_copy(out=o, in_=i_), reads=[b32], writes=[b16])
            P.dma("pool", k.st_sem(), dst_c, a16[:, 0:w_], reads=[b16], writes=[buf])
        k.cast_i[l] = min(len(dj), i0 + n)

    load_x(k, b, 0, src, xs[0])
    nslab = 0
    for st in range(k.NST):
        if st + 1 < k.NST:
            load_x(k, b, st + 1, src, xs[(st + 1) % 2])
        xcur = xs[st % 2]
        prologue(k, b, sl, xcur, hT, hT_bufs)
        done_here = 0
        for jj in range(NJ // 2):
            if dj and done_here < per and jj * per // (NJ // 2) >= done_here:
                cast_tick(1)
                done_here += 1
            ws, _ = w1s[nslab % 2]
            wsb = w1s_bufs[nslab % 2]
            nslab += 1
            P.dma("sp", k.ld_sem(), ws[:, :, 0, :], w1v[:, :, jj * 256:(jj + 1) * 256], reads=[w1_buf], writes=[wsb[0]])
            P.dma("sp", k.ld_sem(), ws[:, :, 1, :], w1v[:, :, FFN_H + jj * 256:FFN_H + (jj + 1) * 256],
                  reads=[w1_buf], writes=[wsb[1]])
            for jl in range(2):
                j = 2 * jj + jl
                gb, gb_b = k.bank[2 + j % 2]
                ub, ub_b = k.bank[4 + j % 2]
                for which, (bk, bk_b) in enumerate(((gb, gb_b), (ub, ub_b))):
                    for kc in range(KC):
                        P.op("pe", lambda e, bk=bk, ws=ws, kc=kc, which=which, jl=jl: e.matmul(
                            bk[:, :], ws[:, kc, which, jl * 128:(jl + 1) * 128], hT[:, kc, :],
                            start=(kc == 0), stop=(kc == KC - 1)),
                            reads=[wsb[which], hT_bufs[kc]], writes=[bk_b])
                sga, sga_b = sg[j % 2]
                P.op("act", lambda e, gb=gb, sga=sga: e.activation(out=sga, in_=gb[:, :], func=AF.Silu),
                     reads=[gb_b], writes=[sga_b])
                P.op("dve", lambda e, ub=ub, sga=sga, j=j: e.tensor_tensor(out=hid[:, j, :], in0=ub[:, :], in1=sga,
                                                                        op=ALU.mult),
                     reads=[ub_b, sga_b], writes=[hid_bufs[j]])
        for t in range(4):
            ybk = (6, 7) if t % 2 == 0 else (2, 3)
            for h in range(2):
                yb, yb_b = k.bank[ybk[h]]
                for j in range(NJ):
                    P.op("pe", lambda e, yb=yb, j=j, t=t, h=h: e.matmul(
                        yb[:, :], hid[:, j, t * 128:(t + 1) * 128], w2[:, j, h * 512:(h + 1) * 512],
                        start=(j == 0), stop=(j == NJ - 1)),
                        reads=[hid_bufs[j], w2_b], writes=[yb_b])
            epilogue(k, ep, b, st * 4 + t, ybk, xcur[t][0], xcur[t][1], dst)
        if dj and b == k.NSEQ - 1 and st == k.NST - 1:
            cast_tick(len(dj))


PI_LO = float(np.nextafter(np.float32(np.pi), np.float32(0)))


def phase_rope(k):
    P, A, NSEQ, NT = k.P, k.arena, k.NSEQ, k.NT
    P.barrier()
    A.reset()
    rv = k.rope.rearrange("p (b w n j) -> p b w n j", b=NSEQ, w=2, n=NT, j=8)

    def one(b):
        pi, pi_b = A.alloc("posi", [NT], I32)
        pf, pf_b = A.alloc("posf", [NT], F32)
        ang, ang_b = A.alloc("ang", [NT, 8], F32)
        kf, kf_b = A.alloc("kf", [NT, 8], F32)
        ki, ki_b = A.alloc("ki", [NT, 8], I32)
        r, r_b = A.alloc("r", [NT, 8], F32)
        m, m_b = A.alloc("m", [NT, 8], F32)
        rc, rc_b = A.alloc("rc", [NT, 8], F32)
        P.dma("sp", k.ld_sem(), pi, k.pos[b].rearrange("(n p) -> p n", p=128), writes=[pi_b],
              allow_slow_non_contiguous=True)
        P.op("dve", lambda e: e.tensor_copy(out=pf, in_=pi), reads=[pi_b], writes=[pf_b])
        P.op("dve", lambda e: e.tensor_tensor(out=ang, in0=pf.unsqueeze(2).to_broadcast([128, NT, 8]),
                                              in1=k.cst[:, C_INV:C_INV + 8].unsqueeze(1).to_broadcast([128, NT, 8]),
                                              op=ALU.mult), reads=[pf_b, k.cst_b], writes=[ang_b])
        P.op("dve", lambda e: e.tensor_scalar(out=kf, in0=ang, scalar1=1.0 / TWO_PI, scalar2=None, op0=ALU.mult),
             reads=[ang_b], writes=[kf_b])
        P.op("dve", lambda e: e.tensor_copy(out=ki, in_=kf), reads=[kf_b], writes=[ki_b])
        P.op("dve", lambda e: e.tensor_copy(out=kf, in_=ki), reads=[ki_b], writes=[kf_b])
        P.op("dve", lambda e: e.scalar_tensor_tensor(out=r, in0=kf, scalar=-CW1, in1=ang, op0=ALU.mult, op1=ALU.add),
             reads=[kf_b, ang_b], writes=[r_b])
        for cw in (CW2, CW3):
            P.op("dve", lambda e, cw=cw: e.scalar_tensor_tensor(out=r, in0=kf, scalar=-cw, in1=r, op0=ALU.mult, op1=ALU.add),
                 reads=[kf_b, r_b], writes=[r_b])

        def wrap(t, t_b):
            P.op("dve", lambda e: e.tensor_scalar(out=m, in0=t, scalar1=math.pi, scalar2=None, op0=ALU.is_gt),
                 reads=[t_b], writes=[m_b])
            P.op("dve", lambda e: e.scalar_tensor_tensor(out=t, in0=m, scalar=-TWO_PI, in1=t, op0=ALU.mult, op1=ALU.add),
                 reads=[m_b, t_b], writes=[t_b])
            P.op("dve", lambda e: e.tensor_scalar(out=m, in0=t, scalar1=-math.pi, scalar2=None, op0=ALU.is_lt),
                 reads=[t_b], writes=[m_b])
            P.op("dve", lambda e: e.scalar_tensor_tensor(out=t, in0=m, scalar=TWO_PI, in1=t, op0=ALU.mult, op1=ALU.add),
                 reads=[m_b, t_b], writes=[t_b])
            P.op("dve", lambda e: e.tensor_scalar(out=t, in0=t, scalar1=-PI_LO, scalar2=PI_LO, op0=ALU.max, op1=ALU.min),
                 reads=[t_b], writes=[t_b])

        wrap(r, r_b)
        P.op("dve", lambda e: e.tensor_scalar(out=rc, in0=r, scalar1=math.pi / 2, scalar2=None, op0=ALU.add),
             reads=[r_b], writes=[rc_b])
        wrap(rc, rc_b)
        P.op("act", lambda e, b=b: e.activation(out=rv[:, b, 1, :, :], in_=r, func=AF.Sin), reads=[r_b], writes=[k.rope_b])
        P.op("act", lambda e, b=b: e.activation(out=rv[:, b, 0, :, :], in_=rc, func=AF.Sin), reads=[rc_b], writes=[k.rope_b])

    for b in range(NSEQ):
        one(b)
    k.dump("rope", k.rope, k.rope_b)


def pass_mix(k, l, b, src, dst):
    if l == 0:
        return pass_mlstm(k, b, src, dst)
    if l in (1, 3):
        pass_attn_a(k, l, b, src)
        k.P.barrier()
        k.arena.reset()
        return pass_attn_b(k, l, b, src, dst)
    if l == 2:
        return pass_hgrn(k, b, src, dst)
    raise NotImplementedError


def bcast_row(k, name, row_ap, n, reads=()):
    t = k.arena.alloc(name, [n], F32)
    k.P.dma("sp", k.ld_sem(), t[0], row_ap.partition_broadcast(128), reads=list(reads), writes=[t[1]])
    return t


def out_proj_epi(k, ep, b, tt, ypre, ypre_b, ypT, ypT_b, wout, wout_b, x_ap, x_buf, dst, tpbank, ybanks):
    P = k.P
    bk, bk_b = k.bank[tpbank]
    bk16 = bk.bitcast(BF16)
    for kc in range(KC):
        P.op("pe", lambda e, kc=kc: e.transpose(out=bk16[:, kc * 128:(kc + 1) * 128], in_=ypre[:, kc * 128:(kc + 1) * 128],
                                                identity=k.idb), reads=[ypre_b, k.idb_b], writes=[bk_b])
    P.op("act", lambda e: e.activation(out=ypT, in_=bk16[:, 0:1024], func=AF.Copy), reads=[bk_b], writes=[ypT_b])
    for h in range(2):
        yb, yb_b = k.bank[ybanks[h]]
        for kc in range(KC):
            P.op("pe", lambda e, yb=yb, kc=kc, h=h: e.matmul(yb[:, :], ypT[:, kc * 128:(kc + 1) * 128],
                                                            wout[:, kc, h * 512:(h + 1) * 512],
                                                            start=(kc == 0), stop=(kc == KC - 1)),
                 reads=[ypT_b, wout_b], writes=[yb_b])
    epilogue(k, ep, b, tt, ybanks, x_ap, x_buf, dst)


def pass_mlstm(k, b, src, dst):
    P, A = k.P, k.arena
    sl = 0
    NIN = 3080
    xs = [[A.alloc("x", [1024], F32) for t in range(4)] for i in range(2)]
    hT, _ = A.alloc("hT", [KC, 512], BF16)
    hT_bufs = [Buf("hT%d" % i) for i in range(KC)]
    win, win_b = A.alloc("win", [KC, NIN], BF16)
    wout, wout_b = A.alloc("wout", [KC, 1024], BF16)
    ep = alloc_epi(k, sl, b)
    bg = bcast_row(k, "bgate", k.mlstm_b_gate[0, :], 8)
    nw = bcast_row(k, "normw", k.mlstm_norm[0, :], 1024)
    P.dma("sp", k.ld_sem(), win, k.wb["mlstm_w_in"].rearrange("(kc p) n -> p kc n", p=128),
          reads=[k.wb_buf["mlstm_w_in"]], writes=[win_b])
    P.dma("sp", k.ld_sem(), wout, k.wb["mlstm_w_out"].rearrange("(kc p) n -> p kc n", p=128),
          reads=[k.wb_buf["mlstm_w_out"]], writes=[wout_b])
    gs = A.alloc("gs", [48], F32, nbuf=2)
    qp = A.alloc("qp", [4, 128], BF16, nbuf=2)
    kp = A.alloc("kp", [4, 128], BF16, nbuf=2)
    kpp = A.alloc("kpp", [4, 128], BF16, nbuf=2)
    va = A.alloc("va", [4, 258], BF16, nbuf=2)
    sgo = A.alloc("sgo", [1024], F32, nbuf=2)
    qkT = A.alloc("qkT", [8, 128], BF16, nbuf=2)
    aT = A.alloc("aT", [4, 128], BF16, nbuf=2)
    C32, C32_b = A.alloc("C32", [4, 258], F32)
    Cb = A.alloc("Cb", [4, 258], BF16, nbuf=2)
    ypre = A.alloc("ypre", [1024], BF16, nbuf=1)
    ypT = A.alloc("ypT", [1024], BF16, nbuf=1)
    junk = A.alloc("junk", [256], F32, nbuf=2)
    ytmp = A.alloc("ytmp", [256], F32, nbuf=2)
    hs = A.alloc("hs", [16], F32, nbuf=4)
    P.op("pool", lambda e: e.memset(C32, 0.0), writes=[C32_b])
    for i in range(2):
        P.op("pool", lambda e, i=i: e.memset(Cb[i][0], 0.0), writes=[Cb[i][1]])
        P.op("pool", lambda e, i=i: e.memset(va[i][0][:, :, 256:258], 1.0), writes=[va[i][1]])
    caus32 = k.cst[:, C_CAUS:C_CAUS + 128]
    ones32 = k.cst[:, C_ONES:C_ONES + 128]
    SC = 128.0 ** -0.5
    cnt_h = [0]

    def stage_P(st, t, ci, xcur):
        tok = slice(t * 128, (t + 1) * 128)
        g, g_b = gs[ci % 2]
        qpa, qp_b = qp[ci % 2]
        kpa, kp_b = kp[ci % 2]
        kppa, kpp_b = kpp[ci % 2]
        vaa, va_b = va[ci % 2]
        sgoa, sgo_b = sgo[ci % 2]
        qkTa, qkT_b = qkT[ci % 2]
        aTa, aT_b = aT[ci % 2]
        Cold, Cold_b = Cb[ci % 2]
        Cnew, Cnew_b = Cb[(ci + 1) % 2]
        ypa, yp_b = ypre[0]
        ypTa, ypT_b = ypT[0]
        b4, b4_b = k.bank[4]

        def proj(bank, c0, n):
            bk, bk_b = k.bank[bank]
            for kc in range(KC):
                P.op("pe", lambda e, bk=bk, kc=kc, c0=c0, n=n, tok=tok: e.matmul(bk[:, 0:n], hT[:, kc, tok], win[:, kc, c0:c0 + n],
                                                                      start=(kc == 0), stop=(kc == KC - 1)),
                     reads=[hT_bufs[kc], win_b], writes=[bk_b])
            return bk, bk_b

        proj(4, 3072, 8)
        P.op("dve", lambda e, g=g: e.tensor_tensor(out=g[:, 0:8], in0=b4[:, 0:8], in1=bg[0], op=ALU.add),
             reads=[b4_b, bg[1]], writes=[g_b])
        P.op("act", lambda e, g=g: e.activation(out=g[:, 8:12], in_=g[:, 4:8], func=AF.Exp, scale=-1.0),
             reads=[g_b], writes=[g_b])
        P.op("act", lambda e, g=g: e.activation(out=g[:, 8:12], in_=g[:, 8:12], func=AF.Ln, bias=1.0),
             reads=[g_b], writes=[g_b])
        for c in range(2):
            bk, bk_b = proj(2 + c, 1024 + c * 512, 512)
            P.op("act", lambda e, bk=bk, c=c, vaa=vaa: e.activation(
                out=vaa[:, 2 * c:2 * c + 2, 0:256], in_=bk[:, :].rearrange("p (h d) -> p h d", h=2), func=AF.Copy),
                reads=[bk_b], writes=[va_b])
        for c in range(2):
            bk, bk_b = proj(2 + c, 2048 + c * 512, 512)
            P.op("act", lambda e, bk=bk, c=c, sgoa=sgoa: e.activation(out=sgoa[:, c * 512:(c + 1) * 512], in_=bk[:, :],
                                                                     func=AF.Sigmoid),
                 reads=[bk_b], writes=[sgo_b])
        P.op("pe", lambda e, g=g: e.matmul(b4[:, 8:12], caus32, g[:, 8:12], start=True, stop=True),
             reads=[g_b, k.cst_b], writes=[b4_b])
        P.op("pe", lambda e, g=g: e.matmul(b4[:, 12:16], ones32, g[:, 8:12], start=True, stop=True),
             reads=[g_b, k.cst_b], writes=[b4_b])
        P.op("dve", lambda e, g=g: e.tensor_tensor(out=g[:, 16:20], in0=b4[:, 8:12], in1=g[:, 0:4], op=ALU.add),
             reads=[b4_b, g_b], writes=[g_b])
        P.op("dve", lambda e, g=g: e.tensor_tensor(out=g[:, 20:24], in0=g[:, 16:20], in1=b4[:, 12:16], op=ALU.subtract),
             reads=[b4_b, g_b], writes=[g_b])
        P.op("act", lambda e, g=g: e.activation(out=g[:, 12:16], in_=b4[:, 8:12], func=AF.Exp, scale=-1.0),
             reads=[b4_b, g_b], writes=[g_b])
        P.op("act", lambda e, g=g: e.activation(out=g[:, 24:28], in_=b4[:, 12:16], func=AF.Exp, scale=-1.0),
             reads=[b4_b, g_b], writes=[g_b])
        P.op("act", lambda e, g=g: e.activation(out=g[:, 16:24], in_=g[:, 16:24], func=AF.Exp),
             reads=[g_b], writes=[g_b])
        bk, bk_b = proj(2, 0, 512)
        P.op("dve", lambda e, bk=bk, g=g, qpa=qpa: e.tensor_tensor(
            out=qpa, in0=bk[:, :].rearrange("p (h d) -> p h d", h=4),
            in1=g[:, 12:16].unsqueeze(2).to_broadcast([128, 4, 128]), op=ALU.mult),
            reads=[bk_b, g_b], writes=[qp_b])
        bk, bk_b = proj(3, 512, 512)
        P.op("dve", lambda e, bk=bk, g=g, kpa=kpa: e.scalar_tensor_tensor(
            out=kpa, in0=bk[:, :].rearrange("p (h d) -> p h d", h=4), scalar=SC,
            in1=g[:, 16:20].unsqueeze(2).to_broadcast([128, 4, 128]), op0=ALU.mult, op1=ALU.mult),
            reads=[bk_b, g_b], writes=[kp_b])
        P.op("dve", lambda e, bk=bk, g=g, kppa=kppa: e.scalar_tensor_tensor(
            out=kppa, in0=bk[:, :].rearrange("p (h d) -> p h d", h=4), scalar=SC,
            in1=g[:, 20:24].unsqueeze(2).to_broadcast([128, 4, 128]), op0=ALU.mult, op1=ALU.mult),
            reads=[bk_b, g_b], writes=[kpp_b])
        tb, tb_b = k.bank[0]
        tb16 = tb.bitcast(BF16)
        for h in range(4):
            P.op("pe", lambda e, h=h, qpa=qpa: e.transpose(out=tb16[:, h * 128:(h + 1) * 128], in_=qpa[:, h, :],
                                                          identity=k.idb), reads=[qp_b, k.idb_b], writes=[tb_b])
        for h in range(4):
            P.op("pe", lambda e, h=h, kpa=kpa: e.transpose(out=tb16[:, 512 + h * 128:512 + (h + 1) * 128],
                                                          in_=kpa[:, h, :], identity=k.idb),
                 reads=[kp_b, k.idb_b], writes=[tb_b])
        P.op("act", lambda e, qkTa=qkTa: e.activation(out=qkTa, in_=tb16[:, 0:1024].rearrange("p (h d) -> p h d", h=8),
                                                    func=AF.Copy), reads=[tb_b], writes=[qkT_b])
        sb_, sb_b = k.bank[5]
        for h in range(4):
            P.op("pe", lambda e, h=h, qkTa=qkTa: e.matmul(sb_[:, h * 128:(h + 1) * 128], qkTa[:, 4 + h, :], qkTa[:, h, :],
                                                         start=True, stop=True), reads=[qkT_b], writes=[sb_b])
        P.op("dve", lambda e, aTa=aTa: e.tensor_tensor(
            out=aTa, in0=sb_[:, :].rearrange("p (h d) -> p h d", h=4),
            in1=caus32.unsqueeze(1).to_broadcast([128, 4, 128]), op=ALU.mult),
            reads=[sb_b, k.cst_b], writes=[aT_b])
        return locals()

    def stage_R(L):
        (st, t, xcur, g, g_b, kppa, kpp_b, vaa, va_b, sgoa, sgo_b, qkTa, qkT_b, aTa, aT_b, Cold, Cold_b, Cnew, Cnew_b,
         ypa, yp_b, ypTa, ypT_b) = (L[n] for n in (
            "st", "t", "xcur", "g", "g_b", "kppa", "kpp_b", "vaa", "va_b", "sgoa", "sgo_b", "qkTa", "qkT_b", "aTa", "aT_b",
            "Cold", "Cold_b", "Cnew", "Cnew_b", "ypa", "yp_b", "ypTa", "ypT_b"))
        for hp in range(2):
            cx = []
            for h in (2 * hp, 2 * hp + 1):
                ab, ab_b = k.bank[6 + h % 2]
                P.op("pe", lambda e, h=h, ab=ab: e.matmul(ab[:, 0:257], aTa[:, h, :], vaa[:, h, 0:257], start=True, stop=False),
                     reads=[aT_b, va_b], writes=[ab_b])
                P.op("pe", lambda e, h=h, ab=ab: e.matmul(ab[:, 0:257], qkTa[:, h, :], Cold[:, h, 0:257], start=False, stop=True),
                     reads=[qkT_b, Cold_b], writes=[ab_b])
                s_, s_b = hs[cnt_h[0] % 4]
                jk, jk_b = junk[cnt_h[0] % 2]
                yt, yt_b = ytmp[cnt_h[0] % 2]
                cnt_h[0] += 1
                cx.append((h, ab, ab_b, s_, s_b, jk, jk_b, yt, yt_b))
            steps_ = [
                lambda h, ab, ab_b, s_, s_b, jk, jk_b, yt, yt_b: P.op(
                    "act", lambda e: e.activation(out=s_[:, 0:1], in_=ab[:, 256:257], func=AF.Abs), reads=[ab_b], writes=[s_b]),
                lambda h, ab, ab_b, s_, s_b, jk, jk_b, yt, yt_b: P.op(
                    "act", lambda e: e.activation(out=jk, in_=ab[:, 0:256], func=AF.Square, accum_out=s_[:, 2:3]),
                    reads=[ab_b], writes=[jk_b, s_b]),
                lambda h, ab, ab_b, s_, s_b, jk, jk_b, yt, yt_b: P.op(
                    "dve", lambda e: e.tensor_scalar(out=s_[:, 0:1], in0=s_[:, 0:1], scalar1=1.0, scalar2=None, op0=ALU.max),
                    reads=[s_b], writes=[s_b]),
                lambda h, ab, ab_b, s_, s_b, jk, jk_b, yt, yt_b: P.op(
                    "dve", lambda e: e.reciprocal(out=s_[:, 1:2], in_=s_[:, 0:1]), reads=[s_b], writes=[s_b]),
                lambda h, ab, ab_b, s_, s_b, jk, jk_b, yt, yt_b: P.op(
                    "dve", lambda e: e.scalar_tensor_tensor(out=s_[:, 3:4], in0=s_[:, 2:3], scalar=s_[:, 1:2], in1=s_[:, 1:2],
                                                            op0=ALU.mult, op1=ALU.mult), reads=[s_b], writes=[s_b]),
                lambda h, ab, ab_b, s_, s_b, jk, jk_b, yt, yt_b: P.op(
                    "dve", lambda e: e.tensor_scalar(out=s_[:, 4:5], in0=s_[:, 3:4], scalar1=1.0 / 256.0, scalar2=1e-6,
                                                     op0=ALU.mult, op1=ALU.add), reads=[s_b], writes=[s_b]),
                lambda h, ab, ab_b, s_, s_b, jk, jk_b, yt, yt_b: P.op(
                    "pool", lambda e: e.tensor_tensor(out=s_[:, 5:6], in0=s_[:, 4:5], in1=ep.nh[0], op=ALU.pow),
                    reads=[s_b, ep.nh[1]], writes=[s_b]),
                lambda h, ab, ab_b, s_, s_b, jk, jk_b, yt, yt_b: P.op(
                    "dve", lambda e: e.tensor_tensor(out=s_[:, 6:7], in0=s_[:, 5:6], in1=s_[:, 1:2], op=ALU.mult),
                    reads=[s_b], writes=[s_b]),
                lambda h, ab, ab_b, s_, s_b, jk, jk_b, yt, yt_b: P.op(
                    "dve", lambda e: e.scalar_tensor_tensor(out=yt, in0=ab[:, 0:256], scalar=s_[:, 6:7],
                                                            in1=nw[0][:, h * 256:(h + 1) * 256], op0=ALU.mult, op1=ALU.mult),
                    reads=[ab_b, s_b, nw[1]], writes=[yt_b]),
                lambda h, ab, ab_b, s_, s_b, jk, jk_b, yt, yt_b: P.op(
                    "pool", lambda e: e.tensor_tensor(out=ypa[:, h * 256:(h + 1) * 256], in0=yt,
                                                      in1=sgoa[:, h * 256:(h + 1) * 256], op=ALU.mult),
                    reads=[yt_b, sgo_b], writes=[yp_b]),
            ]
            for fn in steps_:
                for c_ in cx:
                    fn(*c_)
        for h in range(4):
            cb_, cb_b = k.bank[2 + h % 2]
            P.op("pe", lambda e, h=h, cb_=cb_, kppa=kppa, vaa=vaa: e.matmul(cb_[:, 0:257], kppa[:, h, :], vaa[:, h, 0:257],
                                                                           start=True, stop=True),
                 reads=[kpp_b, va_b], writes=[cb_b])
            P.op("dve", lambda e, h=h, cb_=cb_, g=g: e.scalar_tensor_tensor(
                out=C32[:, h, 0:257], in0=C32[:, h, 0:257], scalar=g[:, 24 + h:25 + h], in1=cb_[:, 0:257],
                op0=ALU.mult, op1=ALU.add), reads=[cb_b, g_b, C32_b], writes=[C32_b])
        P.op("act", lambda e, Cnew=Cnew: e.activation(out=Cnew, in_=C32, func=AF.Copy), reads=[C32_b], writes=[Cnew_b])
        out_proj_epi(k, ep, b, st * 4 + t, ypa, yp_b, ypTa, ypT_b, wout, wout_b, xcur[t][0], xcur[t][1], dst, 1, (2, 3))

    seq = [(st, t) for st in range(k.NST) for t in range(4)]
    load_x(k, b, 0, src, xs[0])
    if k.NST > 1:
        load_x(k, b, 1, src, xs[1])
    prologue(k, b, sl, xs[0], hT, hT_bufs)
    ctx = stage_P(0, 0, 0, xs[0])
    for i, (st, t) in enumerate(seq):
        nxt = None
        if i + 1 < len(seq):
            st2, t2 = seq[i + 1]
            if t2 == 0:
                prologue(k, b, sl, xs[st2 % 2], hT, hT_bufs)
            nxt = stage_P(st2, t2, i + 1, xs[st2 % 2])
        stage_R(ctx)
        if t == 3 and st + 2 < k.NST:
            load_x(k, b, st + 2, src, xs[st % 2])
        ctx = nxt


def make_in_map(inp, xb, cb, posb):
    m = {"x": np.ascontiguousarray(xb, dtype=np.float32), "c": np.ascontiguousarray(cb, dtype=np.float32),
         "positions": np.ascontiguousarray(posb, dtype=np.int32), "consts": make_consts()}
    for name in ("ada_w", "ada_b", "ln_g", "ln_b", "mlstm_b_gate", "mlstm_norm", "swa_sinks", "hgrn_lower_bounds",
                 "hgrn_norm", "diff_lambda", "diff_norm", "ffn_w_in", "ffn_w_out"):
        m[name] = np.ascontiguousarray(inp[name], dtype=np.float32)
    for name, r, c_ in W_SPECS:
        m[name] = np.ascontiguousarray(inp[name], dtype=np.float32)
    return m


DBG_STOP = 0


def attn_cfg(l):
    c = K()
    if l == 1:
        c.NIN, c.wi, c.wo = 1280, "swa_w_in", "swa_w_out"
        c.NKP, c.NVH, c.DV = 1, 2, 64
        c.qscale = 0.125
    else:
        c.NIN, c.wi, c.wo = 3072, "diff_w_in", "diff_w_out"
        c.NKP, c.NVH, c.DV = 8, 8, 128
        c.qscale = 0.125
    c.VW = c.NVH * (c.DV + 2)
    return c


def pass_attn_a(k, l, b, src):
    P, A, NSEQ, NT = k.P, k.arena, k.NSEQ, k.NT
    cf = attn_cfg(l)
    sl = 2 * l
    xs = [[A.alloc("x", [1024], F32) for t in range(4)] for i in range(2)]
    hT, _ = A.alloc("hT", [KC, 512], BF16)
    hT_bufs = [Buf("hT%d" % i) for i in range(KC)]
    win, win_b = A.alloc("win", [KC, cf.NIN], BF16)
    P.dma("sp", k.ld_sem(), win, k.wb[cf.wi].rearrange("(kc p) n -> p kc n", p=128), reads=[k.wb_buf[cf.wi]], writes=[win_b])
    qr = A.alloc("qr", [1024], BF16, nbuf=2)
    kr = A.alloc("kr", [128 * cf.NKP], BF16, nbuf=2)
    QTst, QTst_b = A.alloc("QTst", [8, 512], BF16)
    KTst, KTst_b = A.alloc("KTst", [cf.NKP, 512], BF16)
    Vst = A.alloc("Vst", [cf.NVH, cf.DV + 2], BF16, nbuf=2)
    rt = A.alloc("rt", [4, 64], F32, nbuf=2)
    for i in range(2):
        P.op("pool", lambda e, i=i: e.memset(Vst[i][0][:, :, cf.DV:cf.DV + 2], 1.0), writes=[Vst[i][1]])
    rv = k.rope.rearrange("p (b w n j) -> p b w n j", b=NSEQ, w=2, n=NT, j=8)
    qt_v = k.qt_scr.rearrange("n p s -> p n s")
    kt_v = k.kt_scr.rearrange("n p s -> p n s")
    load_x(k, b, 0, src, xs[0])
    ci = 0
    nb = 0
    nr = 0
    for st in range(k.NST):
        if st + 1 < k.NST:
            load_x(k, b, st + 1, src, xs[(st + 1) % 2])
        xcur = xs[st % 2]
        prologue(k, b, sl, xcur, hT, hT_bufs)
        for t in range(4):
            tt = st * 4 + t
            tok = slice(t * 128, (t + 1) * 128)
            qra, qr_b = qr[ci % 2]
            kra, kr_b = kr[ci % 2]
            Va, V_b = Vst[ci % 2]
            ci += 1
            cosb = rv[:, b, 0, tt, :].unsqueeze(1).to_broadcast([128, 8, 8])
            sinb = rv[:, b, 1, tt, :].unsqueeze(1).to_broadcast([128, 8, 8])

            def proj(c0, n):
                nonlocal nb
                bk, bk_b = k.bank[2 + nb % 4]
                nb += 1
                for kc in range(KC):
                    P.op("pe", lambda e, bk=bk, kc=kc, c0=c0, n=n, tok=tok: e.matmul(
                        bk[:, 0:n], hT[:, kc, tok], win[:, kc, c0:c0 + n], start=(kc == 0), stop=(kc == KC - 1)),
                        reads=[hT_bufs[kc], win_b], writes=[bk_b])
                return bk, bk_b

            def rope_evac(src3, src_b, dst3, dst_b, nh, scale):
                nonlocal nr
                r4, r4_b = rt[nr % 2]
                nr += 1
                cb_ = cosb[:, 0:nh, :]
                sb_ = sinb[:, 0:nh, :]
                tv = [r4[:, i, 0:nh * 8].rearrange("p (h j) -> p h j", j=8) for i in range(4)]
                x1, x2 = src3[:, :, 0:8], src3[:, :, 8:16]
                for i, (xa, tb_) in enumerate(((x1, cb_), (x2, sb_), (x2, cb_), (x1, sb_))):
                    P.op("dve", lambda e, i=i, xa=xa, tb_=tb_: e.scalar_tensor_tensor(
                        out=tv[i], in0=xa, scalar=scale, in1=tb_, op0=ALU.mult, op1=ALU.mult),
                        reads=[src_b, k.rope_b], writes=[r4_b])
                P.op("dve", lambda e: e.tensor_tensor(out=dst3[:, :, 0:8], in0=tv[0], in1=tv[1], op=ALU.subtract),
                     reads=[r4_b], writes=[dst_b])
                P.op("dve", lambda e: e.tensor_tensor(out=dst3[:, :, 8:16], in0=tv[2], in1=tv[3], op=ALU.add),
                     reads=[r4_b], writes=[dst_b])
                P.op("act", lambda e: e.activation(out=dst3[:, :, 16:64], in_=src3[:, :, 16:64], func=AF.Copy, scale=scale),
                     reads=[src_b], writes=[dst_b])

            for c in range(2):
                bk, bk_b = proj(c * 512, 512)
                s3 = bk[:, :].rearrange("p (h d) -> p h d", d=64)
                if l == 1:
                    d3 = qra.rearrange("p (m c d) -> p m c d", c=2, d=64)[:, :, c, :]
                else:
                    d3 = qra[:, c * 512:(c + 1) * 512].rearrange("p (h d) -> p h d", d=64)
                rope_evac(s3, bk_b, d3, qr_b, 8, cf.qscale)
            if l == 1:
                bk, bk_b = proj(1024, 256)
                rope_evac(bk[:, 0:128].rearrange("p (h d) -> p h d", d=64), bk_b,
                          kra.rearrange("p (h d) -> p h d", d=64), kr_b, 2, 1.0)
                P.op("act", lambda e, bk=bk, Va=Va: e.activation(out=Va[:, :, 0:64],
                                                                in_=bk[:, 128:256].rearrange("p (h d) -> p h d", d=64),
                                                                func=AF.Copy), reads=[bk_b], writes=[V_b])
            else:
                for c in range(2):
                    bk, bk_b = proj(1024 + c * 512, 512)
                    rope_evac(bk[:, :].rearrange("p (h d) -> p h d", d=64), bk_b,
                              kra[:, c * 512:(c + 1) * 512].rearrange("p (h d) -> p h d", d=64), kr_b, 8, 1.0)
                for c in range(2):
                    bk, bk_b = proj(2048 + c * 512, 512)
                    P.op("act", lambda e, bk=bk, Va=Va, c=c: e.activation(
                        out=Va[:, 4 * c:4 * c + 4, 0:128], in_=bk[:, :].rearrange("p (h d) -> p h d", d=128), func=AF.Copy),
                        reads=[bk_b], writes=[V_b])
            for (sa, sa_b, npair, bank, dstT, dstT_b) in ((qra, qr_b, 8, 0, QTst, QTst_b), (kra, kr_b, cf.NKP, 1, KTst, KTst_b)):
                tb, tb_b = k.bank[bank]
                tb16 = tb.bitcast(BF16)
                for p_ in range(npair):
                    P.op("pe", lambda e, tb16=tb16, p_=p_, sa=sa: e.transpose(out=tb16[:, p_ * 128:(p_ + 1) * 128],
                                                                             in_=sa[:, p_ * 128:(p_ + 1) * 128], identity=k.idb),
                         reads=[sa_b, k.idb_b], writes=[tb_b])
                P.op("act", lambda e, tb16=tb16, npair=npair, dstT=dstT, tok=tok: e.activation(
                    out=dstT[:, :, tok], in_=tb16[:, 0:npair * 128].rearrange("p (n t) -> p n t", t=128), func=AF.Copy),
                    reads=[tb_b], writes=[dstT_b])
            P.dma("pool", k.st_sem(), k.v_scr[tt * 128:(tt + 1) * 128, 0:cf.VW], Va.rearrange("p h d -> p (h d)"),
                  reads=[V_b], writes=[k.v_bufs[tt]])
        P.dma("pool", k.st_sem(), qt_v[:, :, st * 512:(st + 1) * 512], QTst, reads=[QTst_b], writes=[k.qt_bufs[st]])
        P.dma("pool", k.st_sem(), kt_v[:, 0:cf.NKP, st * 512:(st + 1) * 512], KTst, reads=[KTst_b], writes=[k.kt_bufs[st]])


def pass_attn_b(k, l, b, src, dst):
    P, A, NSEQ, NT = k.P, k.arena, k.NSEQ, k.NT
    cf = attn_cfg(l)
    sl = 2 * l
    DV = cf.DV
    wout, wout_b = A.alloc("wout", [KC, 1024], BF16)
    P.dma("sp", k.ld_sem(), wout, k.wb[cf.wo].rearrange("(kc p) n -> p kc n", p=128), reads=[k.wb_buf[cf.wo]], writes=[wout_b])
    nb_ = 1 if l == 3 else 2
    ep = alloc_epi(k, sl, b, nbuf=nb_)
    KTc, _ = A.alloc("KTc", [cf.NKP, k.S], BF16)
    Vc, _ = A.alloc("Vc", [NT, cf.VW], BF16)
    KT_bufs = [Buf("KTc%d" % i) for i in range(NT)]
    V_bufs = [Buf("Vc%d" % i) for i in range(NT)]
    xr = A.alloc("xr", [1024], F32, nbuf=nb_)
    QTb = A.alloc("QTz", [8, 2, 128], BF16, nbuf=nb_)
    for i in range(nb_):
        P.op("pool", lambda e, i=i: e.memset(QTb[i][0], 0.0), writes=[QTb[i][1]])
    PT = A.alloc("PT", [512], BF16, nbuf=4)
    ypre, yp_b = A.alloc("ypre", [1024], BF16)
    ypT, ypT_b = A.alloc("ypT", [1024], BF16)
    sm = A.alloc("sm", [16], F32, nbuf=4)
    ot = A.alloc("ot", [128], F32, nbuf=2)
    jk = A.alloc("jk", [128], F32, nbuf=2)
    cst_t, cst_tb = A.alloc("acst", [160], F32)
    if l == 1:
        P.dma("sp", k.ld_sem(), cst_t[:, 0:16], k.swa_sinks[0, :].partition_broadcast(128), writes=[cst_tb])
        P.op("act", lambda e: e.activation(out=cst_t[:, 0:16], in_=cst_t[:, 0:16], func=AF.Exp), reads=[cst_tb], writes=[cst_tb])
    else:
        lam_init = 0.8 - 0.6 * math.exp(-0.3 * l)
        lv, lv_b = A.alloc("lv", [256], F32)
        P.dma("sp", k.ld_sem(), lv, k.diff_lambda[0].rearrange("a d -> (a d)").partition_broadcast(128), writes=[lv_b])
        P.dma("sp", k.ld_sem(), cst_t[:, 0:128], k.diff_norm[0, :].partition_broadcast(128), writes=[cst_tb])
        P.op("dve", lambda e: e.tensor_scalar(out=cst_t[:, 0:128], in0=cst_t[:, 0:128], scalar1=1.0 - lam_init, scalar2=None,
                                              op0=ALU.mult), reads=[cst_tb], writes=[cst_tb])
        for i in range(2):
            P.op("dve", lambda e, i=i: e.tensor_tensor(out=lv[:, i * 128:i * 128 + 64], in0=lv[:, i * 128:i * 128 + 64],
                                                      in1=lv[:, i * 128 + 64:i * 128 + 128], op=ALU.mult),
                 reads=[lv_b], writes=[lv_b])
            P.op("dve", lambda e, i=i: e.tensor_reduce(out=cst_t[:, 128 + i:129 + i], in_=lv[:, i * 128:i * 128 + 64],
                                                      axis=AX.X, op=ALU.add), reads=[lv_b, cst_tb], writes=[cst_tb])
        P.op("act", lambda e: e.activation(out=cst_t[:, 128:130], in_=cst_t[:, 128:130], func=AF.Exp), reads=[cst_tb], writes=[cst_tb])
        P.op("dve", lambda e: e.tensor_tensor(out=cst_t[:, 130:131], in0=cst_t[:, 129:130], in1=cst_t[:, 128:129], op=ALU.subtract),
             reads=[cst_tb], writes=[cst_tb])
        P.op("dve", lambda e: e.tensor_scalar(out=cst_t[:, 130:131], in0=cst_t[:, 130:131], scalar1=-lam_init, scalar2=None,
                                              op0=ALU.add), reads=[cst_tb], writes=[cst_tb])
    qt_v = k.qt_scr.rearrange("n p s -> p n s")
    kt_v = k.kt_scr.rearrange("n p s -> p n s")
    Vc4 = Vc.rearrange("p n (h d) -> p n h d", d=DV + 2)
    if DBG_STOP == 1:
        return
    steps = []
    for qb in range(NT):
        if l == 1:
            kbs = ([(qb - 1, k.strict_b16, k.strict_b16_b)] if qb > 0 else []) + [(qb, k.caus_b16, k.caus_b16_b)]
        else:
            kbs = [(i, None, None) for i in range(qb)] + [(qb, k.caus_b16, k.caus_b16_b)]
        for grp in range(4):
            for ki, (kb, msk, msk_b) in enumerate(kbs):
                steps.append(dict(qb=qb, grp=grp, ki=ki, kb=kb, msk=msk, msk_b=msk_b, nk=len(kbs)))
    cnt = {"nS": 0, "nsm": 0}

    def maps_of(grp):
        return [(2 * grp + pp, c) for pp in range(2) for c in range(2)]

    def do_S(s_):
        qb, grp, ki, kb = s_["qb"], s_["grp"], s_["ki"], s_["kb"]
        tok = slice(qb * 128, (qb + 1) * 128)
        QTa, QT_b = QTb[qb % nb_]
        if grp == 0 and ki == 0:
            for c in range(2):
                P.dma("sp", k.ld_sem(), QTa[c * 64:(c + 1) * 64, :, c, :], qt_v[c * 64:(c + 1) * 64, :, tok],
                      reads=[k.qt_bufs[qb // 4]], writes=[QT_b])
            P.dma("sp", k.ld_sem(), KTc[:, :, tok], kt_v[:, 0:cf.NKP, tok], reads=[k.kt_bufs[qb // 4]], writes=[KT_bufs[qb]])
            P.dma("sp", k.ld_sem(), Vc[:, qb, :], k.v_scr[tok, 0:cf.VW], reads=[k.v_bufs[qb]], writes=[V_bufs[qb]])
        ktok = slice(kb * 128, (kb + 1) * 128)
        sbk, sbk_b = k.bank[(0, 1, 7, 6)[cnt["nS"] % 4]]
        pt, pt_b = PT[cnt["nS"] % 4]
        cnt["nS"] += 1
        s_["pt"], s_["pt_b"] = pt, pt_b
        if l == 1:
            P.op("pe", lambda e, sbk=sbk, grp=grp, ktok=ktok, QTa=QTa: e.matmul(
                sbk[:, 0:512], KTc[:, 0, ktok], QTa[:, 2 * grp:2 * grp + 2, :, :].rearrange("p a c q -> p (a c q)"),
                start=True, stop=True), reads=[KT_bufs[kb], QT_b], writes=[sbk_b])
        else:
            for pp in range(2):
                p_ = 2 * grp + pp
                P.op("pe", lambda e, sbk=sbk, pp=pp, p_=p_, ktok=ktok, QTa=QTa: e.matmul(
                    sbk[:, pp * 256:(pp + 1) * 256], KTc[:, p_, ktok], QTa[:, p_, :, :].rearrange("p c q -> p (c q)"),
                    start=True, stop=True), reads=[KT_bufs[kb], QT_b], writes=[sbk_b])
        P.op("act", lambda e, sbk=sbk, pt=pt: e.activation(out=pt, in_=sbk[:, :], func=AF.Exp), reads=[sbk_b], writes=[pt_b])
        msk, msk_b = s_["msk"], s_["msk_b"]
        if msk is not None:
            P.op("pool" if l == 1 else "dve", lambda e, pt=pt, msk=msk: e.tensor_tensor(
                out=pt.rearrange("p (m q) -> p m q", q=128), in0=pt.rearrange("p (m q) -> p m q", q=128),
                in1=msk.unsqueeze(1).to_broadcast([128, 4, 128]), op=ALU.mult), reads=[pt_b, msk_b], writes=[pt_b])

    def do_PV(s_):
        qb, grp, ki, kb, nk = s_["qb"], s_["grp"], s_["ki"], s_["kb"], s_["nk"]
        tok = slice(qb * 128, (qb + 1) * 128)
        xra, xr_b = xr[qb % nb_]
        pt, pt_b = s_["pt"], s_["pt_b"]
        maps = maps_of(grp)
        if grp == 0 and ki == 0:
            rd = [k.act_buf[b][qb]] if src is k.act else []
            P.dma("sp", k.ld_sem(), xra, src[b, tok, :], reads=rd, writes=[xr_b])
        for mi, (p_, c) in enumerate(maps):
            vh = c if l == 1 else p_
            ab, ab_b = k.bank[2 + mi]
            P.op("pe", lambda e, ab=ab, mi=mi, pt=pt, kb=kb, vh=vh, ki=ki, nk=nk: e.matmul(
                ab[:, 0:DV + 1], pt[:, mi * 128:(mi + 1) * 128], Vc4[:, kb, vh, 0:DV + 1],
                start=(ki == 0), stop=(ki == nk - 1)), reads=[pt_b, V_bufs[kb]], writes=[ab_b])
        if ki != nk - 1:
            return
        if l == 1:
            for mi, (p_, c) in enumerate(maps):
                m = p_ + 8 * c
                ab, ab_b = k.bank[2 + mi]
                s1, s1_b = sm[cnt["nsm"] % 4]
                cnt["nsm"] += 1
                P.op("dve", lambda e, ab=ab, s1=s1, m=m: e.tensor_tensor(out=s1[:, 0:1], in0=ab[:, DV:DV + 1], in1=cst_t[:, m:m + 1],
                                                                      op=ALU.add), reads=[ab_b, cst_tb], writes=[s1_b])
                P.op("dve", lambda e, s1=s1: e.reciprocal(out=s1[:, 1:2], in_=s1[:, 0:1]), reads=[s1_b], writes=[s1_b])
                P.op("act", lambda e, ab=ab, s1=s1, m=m: e.activation(out=ypre[:, m * 64:(m + 1) * 64], in_=ab[:, 0:DV],
                                                                   func=AF.Copy, scale=s1[:, 1:2]),
                     reads=[ab_b, s1_b], writes=[yp_b])
        else:
            for pp in range(2):
                p_ = 2 * grp + pp
                a1, a1_b = k.bank[2 + 2 * pp]
                a2, a2_b = k.bank[3 + 2 * pp]
                s1, s1_b = sm[cnt["nsm"] % 4]
                o1, o1_b = ot[cnt["nsm"] % 2]
                j_, j_b = jk[cnt["nsm"] % 2]
                cnt["nsm"] += 1
                P.op("dve", lambda e, a1=a1, s1=s1: e.reciprocal(out=s1[:, 0:1], in_=a1[:, DV:DV + 1]), reads=[a1_b], writes=[s1_b])
                P.op("dve", lambda e, a2=a2, s1=s1: e.reciprocal(out=s1[:, 1:2], in_=a2[:, DV:DV + 1]), reads=[a2_b], writes=[s1_b])
                P.op("dve", lambda e, s1=s1: e.tensor_tensor(out=s1[:, 2:3], in0=s1[:, 1:2], in1=cst_t[:, 130:131], op=ALU.mult),
                     reads=[s1_b, cst_tb], writes=[s1_b])
                P.op("dve", lambda e, a1=a1, s1=s1, o1=o1: e.tensor_scalar(out=o1, in0=a1[:, 0:DV], scalar1=s1[:, 0:1], scalar2=None,
                                                                        op0=ALU.mult), reads=[a1_b, s1_b], writes=[o1_b])
                P.op("dve", lambda e, a2=a2, s1=s1, o1=o1: e.scalar_tensor_tensor(out=o1, in0=a2[:, 0:DV], scalar=s1[:, 2:3], in1=o1,
                                                                               op0=ALU.mult, op1=ALU.add),
                     reads=[a2_b, s1_b, o1_b], writes=[o1_b])
                P.op("act", lambda e, o1=o1, s1=s1, j_=j_: e.activation(out=j_, in_=o1, func=AF.Square, accum_out=s1[:, 3:4]),
                     reads=[o1_b], writes=[j_b, s1_b])
                P.op("dve", lambda e, s1=s1: e.tensor_scalar(out=s1[:, 4:5], in0=s1[:, 3:4], scalar1=1.0 / 128.0, scalar2=1e-6,
                                                            op0=ALU.mult, op1=ALU.add), reads=[s1_b], writes=[s1_b])
                P.op("pool", lambda e, s1=s1: e.tensor_tensor(out=s1[:, 5:6], in0=s1[:, 4:5], in1=ep.nh[0], op=ALU.pow),
                     reads=[s1_b, ep.nh[1]], writes=[s1_b])
                P.op("dve", lambda e, s1=s1, o1=o1, p_=p_: e.scalar_tensor_tensor(
                    out=ypre[:, p_ * 128:(p_ + 1) * 128], in0=o1, scalar=s1[:, 5:6], in1=cst_t[:, 0:128],
                    op0=ALU.mult, op1=ALU.mult), reads=[o1_b, s1_b, cst_tb], writes=[yp_b])
        if grp == 3:
            out_proj_epi(k, ep, b, qb, ypre, yp_b, ypT, ypT_b, wout, wout_b, xra, xr_b, dst, 6, (7, 6))

    SKEW = 3
    for i in range(min(SKEW, len(steps))):
        do_S(steps[i])
    for i in range(len(steps)):
        if i + SKEW < len(steps):
            do_S(steps[i + SKEW])
        do_PV(steps[i])


def pass_hgrn(k, b, src, dst):
    P, A, NSEQ, NT = k.P, k.arena, k.NSEQ, k.NT
    l, sl = 2, 4
    NIN = 4096
    xs = [A.alloc("x", [1024], F32) for t in range(4)]
    hT, _ = A.alloc("hT", [KC, 512], BF16)
    hT_bufs = [Buf("hT%d" % i) for i in range(KC)]
    win, win_b = A.alloc("win", [KC, 2048], BF16)
    wqf = A.alloc("wqf", [KC, 2, 128], BF16, nbuf=2)
    wout, wout_b = A.alloc("wout", [KC, 1024], BF16)
    winv = k.wb["hgrn_w_in"].rearrange("(kc p) n -> p kc n", p=128)
    P.dma("sp", k.ld_sem(), win, winv[:, :, 2048:4096], reads=[k.wb_buf["hgrn_w_in"]], writes=[win_b])
    P.dma("sp", k.ld_sem(), wout, k.wb["hgrn_w_out"].rearrange("(kc p) n -> p kc n", p=128), reads=[k.wb_buf["hgrn_w_out"]], writes=[wout_b])
    ep = alloc_epi(k, sl, b, nbuf=1)
    nw = bcast_row(k, "normw", k.hgrn_norm[0, :], 1024)
    lbp, lbp_b = A.alloc("lbp", [4, 8], F32)
    lbt, lbt_b = A.alloc("lbt", [5, 8], F32)
    P.dma("sp", k.ld_sem(), lbp, k.hgrn_lb.rearrange("l (h p) -> p l h", p=128), writes=[lbp_b], allow_slow_non_contiguous=True)
    P.op("act", lambda e: e.activation(out=lbp, in_=lbp, func=AF.Exp), reads=[lbp_b], writes=[lbp_b])
    P.op("dve", lambda e: e.tensor_tensor(out=lbt[:, 1, :], in0=lbp[:, 1, :], in1=lbp[:, 2, :], op=ALU.add), reads=[lbp_b], writes=[lbt_b])
    P.op("dve", lambda e: e.tensor_tensor(out=lbt[:, 4, :], in0=lbp[:, 0, :], in1=lbp[:, 3, :], op=ALU.add), reads=[lbp_b], writes=[lbt_b])
    P.op("dve", lambda e: e.tensor_tensor(out=lbt[:, 0, :], in0=lbt[:, 1, :], in1=lbt[:, 4, :], op=ALU.add), reads=[lbt_b], writes=[lbt_b])
    P.op("dve", lambda e: e.reciprocal(out=lbt[:, 0, :], in_=lbt[:, 0, :]), reads=[lbt_b], writes=[lbt_b])
    P.op("dve", lambda e: e.tensor_tensor(out=lbt[:, 2, :], in0=lbt[:, 1, :], in1=lbt[:, 0, :], op=ALU.mult), reads=[lbt_b], writes=[lbt_b])
    P.op("dve", lambda e: e.tensor_scalar(out=lbt[:, 3, :], in0=lbt[:, 2, :], scalar1=-1.0, scalar2=1.0, op0=ALU.mult, op1=ALU.add),
         reads=[lbt_b], writes=[lbt_b])
    rmask, rmask_b = A.alloc("rmask", [4, 128], F32)
    P.op("pool", lambda e: e.memset(rmask, 1.0), writes=[rmask_b])
    P.op("pool", lambda e: e.memset(rmask[:, :, 0:1], 0.0), reads=[rmask_b], writes=[rmask_b])
    T1 = A.alloc("T1", [512], F32, nbuf=2)
    T2 = A.alloc("T2", [512], F32, nbuf=2)
    T3s = A.alloc("T3", [512], F32, nbuf=2)
    E1s = A.alloc("E1", [512], F32, nbuf=2)
    E2s = A.alloc("E2", [512], F32, nbuf=2)
    qpT, _ = A.alloc("qpT", [8, 512], BF16)
    kpT, _ = A.alloc("kpT", [8, 512], BF16)
    qpT_bufs = [Buf("qpT%d" % i) for i in range(8)]
    kpT_bufs = [Buf("kpT%d" % i) for i in range(8)]
    es, _ = A.alloc("es", [8, 16], F32)
    es_bufs = [Buf("es%d" % i) for i in range(8)]
    vb = A.alloc("vb", [1024], BF16, nbuf=1)
    sg = A.alloc("sg", [1024], F32, nbuf=1)
    kp, kp_b = A.alloc("kp", [8, 128], BF16)
    aT, aT_b = A.alloc("aT", [8, 128], BF16)
    S32, S32_b = A.alloc("S32", [8, 128], F32)
    Sb, Sb_b = A.alloc("Sb", [8, 128], BF16)
    dS8, dS8_b = A.alloc("dS8", [8, 128], F32)
    osb, osb_b = A.alloc("osb", [1024], F32)
    ss, ss_b = A.alloc("hss", [24], F32)
    ypre, yp_b = A.alloc("ypre", [1024], BF16)
    ypT, ypT_b = A.alloc("ypT", [1024], BF16)
    P.op("pool", lambda e: e.memset(S32, 0.0), writes=[S32_b])
    caus_b = k.caus_b16
    xs_ = xs
    nT = 0
    nvg = 0
    for st in range(k.NST):
        load_x(k, b, st, src, xs_)
        prologue(k, b, sl, xs_, hT, hT_bufs)
        def stage1_head(h, nT):
            t1, t1_b = T1[nT % 2]
            t2, t2_b = T2[nT % 2]
            T3, T3_b = T3s[nT % 2]
            E1, E1_b = E1s[nT % 2]
            E2, E2_b = E2s[nT % 2]
            qb_, qb_b = k.bank[4 + 2 * (nT % 2)]
            fb_, fb_b = k.bank[5 + 2 * (nT % 2)]
            wq, wq_b = wqf[nT % 2]
            P.dma("sp", k.ld_sem(), wq[:, :, 0, :], winv[:, :, h * 128:(h + 1) * 128], reads=[k.wb_buf["hgrn_w_in"]], writes=[wq_b])
            P.dma("sp", k.ld_sem(), wq[:, :, 1, :], winv[:, :, 1024 + h * 128:1024 + (h + 1) * 128], reads=[k.wb_buf["hgrn_w_in"]],
                  writes=[wq_b])
            for (bk, bk_b, w_) in ((qb_, qb_b, 0), (fb_, fb_b, 1)):
                for kc in range(KC):
                    P.op("pe", lambda e, bk=bk, kc=kc, w_=w_, wq=wq: e.matmul(bk[:, :], wq[:, kc, w_, :], hT[:, kc, :],
                                                                             start=(kc == 0), stop=(kc == KC - 1)),
                         reads=[wq_b, hT_bufs[kc]], writes=[bk_b])
            P.op("act", lambda e, t1=t1: e.activation(out=t1, in_=fb_[:, :], func=AF.Sigmoid), reads=[fb_b], writes=[t1_b])
            P.op("act", lambda e, t2=t2: e.activation(out=t2, in_=fb_[:, :], func=AF.Sigmoid, scale=-1.0), reads=[fb_b], writes=[t2_b])
            P.op("dve", lambda e, t1=t1, h=h: e.tensor_scalar(out=t1, in0=t1, scalar1=lbt[:, 3, h:h + 1], scalar2=lbt[:, 2, h:h + 1],
                                                             op0=ALU.mult, op1=ALU.add), reads=[t1_b, lbt_b], writes=[t1_b])
            P.op("act", lambda e, t1=t1: e.activation(out=t1, in_=t1, func=AF.Ln), reads=[t1_b], writes=[t1_b])
            P.op("dve", lambda e, t1=t1: e.tensor_tensor_scan(out=T3, data0=rmask.rearrange("p c t -> p (c t)"), data1=t1, initial=0.0,
                                                             op0=ALU.mult, op1=ALU.add), reads=[rmask_b, t1_b], writes=[T3_b])
            P.op("dve", lambda e, t2=t2, h=h: e.tensor_scalar(out=t2, in0=t2, scalar1=lbt[:, 3, h:h + 1], scalar2=None, op0=ALU.mult),
                 reads=[t2_b, lbt_b], writes=[t2_b])
            T3c = T3.rearrange("p (c t) -> p c t", t=128)
            esh = es[:, h, :]
            P.op("dve", lambda e, esh=esh: e.tensor_scalar(out=esh[:, 0:4], in0=T3c[:, :, 63], scalar1=-1.0, scalar2=None, op0=ALU.mult),
                 reads=[T3_b], writes=[es_bufs[h]])
            P.op("dve", lambda e, esh=esh: e.tensor_tensor(out=esh[:, 12:16], in0=T3c[:, :, 127], in1=T3c[:, :, 63], op=ALU.subtract),
                 reads=[T3_b], writes=[es_bufs[h]])
            P.op("act", lambda e, esh=esh: e.activation(out=esh[:, 4:8], in_=T3c[:, :, 63], func=AF.Exp), reads=[T3_b], writes=[es_bufs[h]])
            P.op("act", lambda e, esh=esh: e.activation(out=esh[:, 8:12], in_=T3c[:, :, 127], func=AF.Exp), reads=[T3_b], writes=[es_bufs[h]])
            P.op("act", lambda e, esh=esh: e.activation(out=esh[:, 12:16], in_=esh[:, 12:16], func=AF.Exp),
                 reads=[es_bufs[h]], writes=[es_bufs[h]])
            for c in range(4):
                cs = slice(c * 128, (c + 1) * 128)
                P.op("act", lambda e, cs=cs, c=c, esh=esh: e.activation(out=E1[:, cs], in_=T3[:, cs], func=AF.Exp, bias=esh[:, c:c + 1]),
                     reads=[T3_b, es_bufs[h]], writes=[E1_b])
                P.op("act", lambda e, cs=cs, c=c: e.activation(out=E2[:, cs], in_=T3[:, cs], func=AF.Exp, scale=-1.0,
                                                              bias=T3[:, c * 128 + 63:c * 128 + 64]),
                     reads=[T3_b], writes=[E2_b])
            P.op("dve", lambda e, h=h: e.tensor_tensor(out=qpT[:, h, :], in0=qb_[:, :], in1=E1, op=ALU.mult),
                 reads=[qb_b, E1_b], writes=[qpT_bufs[h]])
            P.op("dve", lambda e, h=h, t2=t2: e.tensor_tensor(out=kpT[:, h, :], in0=t2, in1=E2, op=ALU.mult),
                 reads=[t2_b, E2_b], writes=[kpT_bufs[h]])
        for h in range(8):
            stage1_head(h, nT)
            nT += 1
        for t in range(4):
            tok = slice(t * 128, (t + 1) * 128)
            va, va_b = vb[0]
            sga, sg_b = sg[0]
            for c in range(4):
                bk, bk_b = k.bank[2 + nvg % 2]
                nvg += 1
                for kc in range(KC):
                    P.op("pe", lambda e, bk=bk, kc=kc, c=c, tok=tok: e.matmul(
                        bk[:, :], hT[:, kc, tok], win[:, kc, c * 512:(c + 1) * 512], start=(kc == 0), stop=(kc == KC - 1)),
                        reads=[hT_bufs[kc], win_b], writes=[bk_b])
                if c < 2:
                    P.op("act", lambda e, bk=bk, c=c, va=va: e.activation(out=va[:, c * 512:(c + 1) * 512], in_=bk[:, :], func=AF.Copy),
                         reads=[bk_b], writes=[va_b])
                else:
                    P.op("act", lambda e, bk=bk, c=c, sga=sga: e.activation(out=sga[:, (c - 2) * 512:(c - 1) * 512], in_=bk[:, :], func=AF.Silu),
                         reads=[bk_b], writes=[sg_b])
            tb, tb_b = k.bank[0]
            tb16 = tb.bitcast(BF16)
            for h in range(8):
                P.op("pe", lambda e, h=h, tok=tok: e.transpose(out=tb16[:, h * 128:(h + 1) * 128], in_=kpT[:, h, tok], identity=k.idb),
                     reads=[kpT_bufs[h], k.idb_b], writes=[tb_b])
            P.op("act", lambda e: e.activation(out=kp, in_=tb16[:, 0:1024].rearrange("p (h d) -> p h d", h=8), func=AF.Copy),
                 reads=[tb_b], writes=[kp_b])
            P.op("dve", lambda e, t=t: e.tensor_tensor(out=Sb, in0=S32, in1=es[:, :, 4 + t:5 + t].to_broadcast([128, 8, 128]),
                                                      op=ALU.mult), reads=[S32_b] + es_bufs, writes=[Sb_b])
            for hh in range(2):
                sbk, sbk_b = k.bank[1]
                for h4 in range(4):
                    h = hh * 4 + h4
                    P.op("pe", lambda e, h=h, h4=h4, tok=tok: e.matmul(sbk[:, h4 * 128:(h4 + 1) * 128], kpT[:, h, tok], qpT[:, h, tok],
                                                                      start=True, stop=True),
                         reads=[kpT_bufs[h], qpT_bufs[h]], writes=[sbk_b])
                P.op("dve", lambda e, hh=hh: e.tensor_tensor(
                    out=aT[:, hh * 4:(hh + 1) * 4, :], in0=sbk[:, :].rearrange("p (h d) -> p h d", h=4),
                    in1=k.cst[:, C_CAUS:C_CAUS + 128].unsqueeze(1).to_broadcast([128, 4, 128]), op=ALU.mult),
                    reads=[sbk_b, k.cst_b], writes=[aT_b])
            for h in range(8):
                ob, ob_b = k.bank[6 + h // 4]
                oc = slice((h % 4) * 128, (h % 4 + 1) * 128)
                P.op("pe", lambda e, h=h, ob=ob, oc=oc, va=va: e.matmul(ob[:, oc], aT[:, h, :], va[:, h * 128:(h + 1) * 128],
                                                                       start=True, stop=False), reads=[aT_b, va_b], writes=[ob_b])
                P.op("pe", lambda e, h=h, ob=ob, oc=oc, tok=tok: e.matmul(ob[:, oc], qpT[:, h, tok], Sb[:, h, :], start=False, stop=True),
                     reads=[qpT_bufs[h], Sb_b], writes=[ob_b])
            for h in range(8):
                db, db_b = k.bank[4 + h // 4]
                dc = slice((h % 4) * 128, (h % 4 + 1) * 128)
                P.op("pe", lambda e, h=h, db=db, dc=dc, va=va: e.matmul(db[:, dc], kp[:, h, :], va[:, h * 128:(h + 1) * 128],
                                                                       start=True, stop=True), reads=[kp_b, va_b], writes=[db_b])
            for hh in range(2):
                db, db_b = k.bank[4 + hh]
                P.op("dve", lambda e, hh=hh, db=db, t=t: e.tensor_tensor(
                    out=dS8[:, hh * 4:(hh + 1) * 4, :], in0=db[:, :].rearrange("p (h d) -> p h d", h=4),
                    in1=es[:, hh * 4:(hh + 1) * 4, 12 + t:13 + t].to_broadcast([128, 4, 128]), op=ALU.mult),
                    reads=[db_b] + es_bufs, writes=[dS8_b])
            P.op("dve", lambda e, t=t: e.tensor_tensor(out=S32, in0=S32, in1=es[:, :, 8 + t:9 + t].to_broadcast([128, 8, 128]),
                                                      op=ALU.mult), reads=[S32_b] + es_bufs, writes=[S32_b])
            P.op("dve", lambda e: e.tensor_tensor(out=S32, in0=S32, in1=dS8, op=ALU.add), reads=[S32_b, dS8_b], writes=[S32_b])
            for hh in range(2):
                ob, ob_b = k.bank[6 + hh]
                P.op("act", lambda e, ob=ob, hh=hh: e.activation(out=osb[:, hh * 512:(hh + 1) * 512], in_=ob[:, :], func=AF.Copy),
                     reads=[ob_b], writes=[osb_b])
            P.op("dve", lambda e: e.tensor_tensor(out=dS8.rearrange("p h d -> p (h d)"), in0=osb, in1=osb, op=ALU.mult),
                 reads=[osb_b, dS8_b], writes=[dS8_b])
            P.op("dve", lambda e: e.tensor_reduce(out=ss[:, 0:8], in_=dS8, axis=AX.X, op=ALU.add), reads=[dS8_b], writes=[ss_b])
            P.op("dve", lambda e: e.tensor_scalar(out=ss[:, 8:16], in0=ss[:, 0:8], scalar1=1.0 / 128.0, scalar2=1e-6, op0=ALU.mult, op1=ALU.add),
                 reads=[ss_b], writes=[ss_b])
            P.op("pool", lambda e: e.tensor_tensor(out=ss[:, 16:24], in0=ss[:, 8:16], in1=ep.nh[0].to_broadcast([128, 8]), op=ALU.pow),
                 reads=[ss_b, ep.nh[1]], writes=[ss_b])
            P.op("dve", lambda e: e.tensor_tensor(out=osb.rearrange("p (h d) -> p h d", h=8), in0=osb.rearrange("p (h d) -> p h d", h=8),
                                                  in1=ss[:, 16:24].unsqueeze(2).to_broadcast([128, 8, 128]), op=ALU.mult),
                 reads=[osb_b, ss_b], writes=[osb_b])
            P.op("pool", lambda e: e.tensor_tensor(out=osb, in0=osb, in1=nw[0], op=ALU.mult), reads=[osb_b, nw[1]], writes=[osb_b])
            P.op("dve", lambda e, sga=sga: e.tensor_tensor(out=ypre, in0=osb, in1=sga, op=ALU.mult), reads=[osb_b, sg_b], writes=[yp_b])
            out_proj_epi(k, ep, b, st * 4 + t, ypre, yp_b, ypT, ypT_b, wout, wout_b, xs_[t][0], xs_[t][1], dst, 0, (2, 3))


_NC_CACHE = {}


def kernel(**inputs):
    n = 8
    NSEQ = 2
    if "nc" not in _NC_CACHE:
        _NC_CACHE["nc"] = build(NSEQ=NSEQ, S=4096)
    nc = _NC_CACHE["nc"]
    in_maps = []
    shared = None
    for i in range(n):
        sl = slice(i * NSEQ, (i + 1) * NSEQ)
        m = make_in_map(inputs, inputs["x"][sl], inputs["c"][sl], inputs["positions"][sl]) if shared is None else dict(shared)
        if shared is None:
            shared = m
        else:
            m["x"] = np.ascontiguousarray(inputs["x"][sl], dtype=np.float32)
            m["c"] = np.ascontiguousarray(inputs["c"][sl], dtype=np.float32)
            m["positions"] = np.ascontiguousarray(inputs["positions"][sl], dtype=np.int32)
        in_maps.append(m)
    res = run_bass_kernel_spmd(nc, in_maps, core_ids=list(range(n)))
    return np.concatenate([np.asarray(r["out"]) for r in res.results], axis=0).astype(np.float32, copy=False)
```
